# Optimizing a Trainium2 kernel written in Bass

```python
import math
import jax, jax.numpy as jnp
from jax import lax
import numpy as np

D_MODEL = 2048
BATCH = 4
SEQ = 2048
DEPTH = 2
DEC_BATCH = 128
DEC_SEQ = 4
PAST_LEN = 16384
PAGE_SIZE = 128

D_PLE = 256
D_A = D_MODEL // 2
CONV_A = 3
H_B = 8
DK = 128
DV = 128
D_B = H_B * DV
CONV_B = 4
DELTA_CHUNK = 64
D_C = D_MODEL // 2
CHUNK_C = 128
G_C = 8
C_GROUP_DIM = D_C // G_C
D_FF = ((8 * D_MODEL // 3 + 127) // 128) * 128
CONV_F = 3
N_BRANCH = 3
DEEPNORM_ALPHA = (2 * DEPTH) ** 0.25
DEEPNORM_BETA = (8 * DEPTH) ** -0.25
LN_EPS = 1e-5
RMS_EPS = 1e-6
SPLIT_SIZES = (D_A, D_A, D_A, 3 * D_B, D_B, H_B, H_B, D_C, D_C, N_BRANCH * D_MODEL)
N_IN = sum(SPLIT_SIZES)

kernel_name = 'hybrid_conv_delta_gmlp_deepnorm_step'


def layer_norm(x, g, b):
    xf = x.astype(jnp.float32)
    mu = jnp.mean(xf, -1, keepdims=True)
    xc = xf - mu
    var = jnp.mean(xc * xc, -1, keepdims=True)
    y = xc * lax.rsqrt(var + LN_EPS) * g.astype(jnp.float32) + b.astype(jnp.float32)
    return y.astype(x.dtype)


def rms_norm_f32(x, g):
    xf = x.astype(jnp.float32)
    return xf * lax.rsqrt(jnp.mean(xf * xf, -1, keepdims=True) + RMS_EPS) * g.astype(jnp.float32)


def l2_normalize(t):
    return t * lax.rsqrt(jnp.sum(t * t, -1, keepdims=True) + RMS_EPS)


def split_cols(t, sizes):
    offs = np.cumsum(np.array(sizes))[:-1].tolist()
    return jnp.split(t, offs, axis=-1)


def causal_dwconv(x, hist, w):
    width = w.shape[0]
    L = x.shape[1]
    xf = jnp.concatenate([hist, x], axis=1)
    out = xf[:, 0:L] * w[0]
    for j in range(1, width):
        out = out + xf[:, j:j + L] * w[j]
    return out, xf[:, xf.shape[1] - (width - 1):]


def gated_delta_rule(q, k, v, g, beta, s0):
    Bn, L, H, _ = q.shape
    C = min(DELTA_CHUNK, L)
    pad = (-L) % C
    n_chunks = (L + pad) // C

    def prep(t):
        t = jnp.pad(t, [(0, 0), (0, pad)] + [(0, 0)] * (t.ndim - 2))
        t = t.reshape((Bn, n_chunks, C) + t.shape[2:])
        return jnp.transpose(t, (1, 0, 3, 2) + tuple(range(4, t.ndim)))

    qc, kc, vc, gc, bc = prep(q), prep(k), prep(v), prep(g), prep(beta)
    gcum = jnp.cumsum(gc, axis=-1)
    idx = jnp.arange(C)
    causal = idx[:, None] >= idx[None, :]
    strict = idx[:, None] > idx[None, :]
    decay = jnp.exp(jnp.where(causal, gcum[..., :, None] - gcum[..., None, :], -jnp.inf))
    kb = kc * bc[..., None]
    vb = vc * bc[..., None]
    lmat = jnp.where(strict, jnp.einsum('nbhcd,nbhed->nbhce', kb, kc) * decay, 0.0)
    eye = jnp.eye(C, dtype=jnp.float32)
    tmat = lax.linalg.triangular_solve(eye + lmat, jnp.broadcast_to(eye, lmat.shape),
                                       left_side=True, lower=True, unit_diagonal=True)
    u = jnp.einsum('nbhce,nbhef->nbhcf', tmat, vb)
    w = jnp.einsum('nbhce,nbhed->nbhcd', tmat, kb * jnp.exp(gcum)[..., None])
    qk = jnp.einsum('nbhcd,nbhed->nbhce', qc, kc) * decay
    q_dec = qc * jnp.exp(gcum)[..., None]
    k_dec = kc * jnp.exp(gcum[..., -1:] - gcum)[..., None]
    g_last = jnp.exp(gcum[..., -1])

    def step(s, xs):
        u_n, w_n, qk_n, qd_n, kd_n, gl_n = xs
        v_new = u_n - jnp.einsum('bhcd,bhde->bhce', w_n, s)
        o_n = jnp.einsum('bhcd,bhde->bhce', qd_n, s) + jnp.einsum('bhce,bhef->bhcf', qk_n, v_new)
        s = s * gl_n[..., None, None] + jnp.einsum('bhcd,bhce->bhde', kd_n, v_new)
        return s, o_n

    s_fin, o = lax.scan(step, s0, (u, w, qk, q_dec, k_dec, g_last))
    o = jnp.transpose(o, (1, 0, 3, 2, 4)).reshape(Bn, n_chunks * C, H, DV)[:, :L]
    return o, s_fin


def chunk_spatial_mix(v, w_s, b_s):
    Bn, L, _ = v.shape
    pad = (-L) % CHUNK_C
    vp = jnp.pad(v, ((0, 0), (0, pad), (0, 0))).reshape(Bn, -1, CHUNK_C, G_C, C_GROUP_DIM)
    tri = jnp.tril(jnp.ones((CHUNK_C, CHUNK_C), dtype=bool))
    wm = jnp.where(tri, w_s, 0.0)
    mixed = jnp.einsum('gts,bnsgc->bntgc', wm, vp) + jnp.transpose(b_s)[None, None, :, :, None]
    return mixed.reshape(Bn, -1, D_C)[:, :L]


def layer_step(x, p, hist_a, hist_qkv, s_delta, hist_f,
               w_in, conv_a_w, w_a_out, conv_b_w, a_log, dt_bias, norm_b_g, w_b_out,
               ln_c_g, ln_c_b, w_s, b_s, w_c_out, w_o, ln1_g, ln1_b,
               w_up, conv_f_w, w_down, w_pe, w_pg, ln2_g, ln2_b):
    Bn, L, _ = x.shape
    proj = x @ w_in
    a_h, a_bg, a_cg, qkv, z, beta_raw, dec_raw, c_u, c_v, gates = split_cols(proj, SPLIT_SIZES)

    conv_a, new_hist_a = causal_dwconv(a_cg * a_h, hist_a, conv_a_w)
    out_a = (a_bg * conv_a) @ w_a_out

    qkv_c, new_hist_qkv = causal_dwconv(qkv, hist_qkv, conv_b_w)
    qkv_c = jax.nn.silu(qkv_c).astype(jnp.float32)
    q, k, v = jnp.split(qkv_c, 3, axis=-1)
    q = l2_normalize(q.reshape(Bn, L, H_B, DK)) * (DK ** -0.5)
    k = l2_normalize(k.reshape(Bn, L, H_B, DK))
    v = v.reshape(Bn, L, H_B, DV)
    beta = jax.nn.sigmoid(beta_raw.astype(jnp.float32))
    g = -jnp.exp(a_log.astype(jnp.float32)) * jax.nn.softplus(dec_raw.astype(jnp.float32) + dt_bias.astype(jnp.float32))
    o, new_s = gated_delta_rule(q, k, v, g, beta, s_delta.astype(jnp.float32))
    zf = z.astype(jnp.float32).reshape(Bn, L, H_B, DV)
    o = (rms_norm_f32(o, norm_b_g) * jax.nn.silu(zf)).astype(x.dtype)
    out_b = o.reshape(Bn, L, D_B) @ w_b_out

    u_c = jax.nn.gelu(c_u)
    v_c = layer_norm(jax.nn.gelu(c_v), ln_c_g, ln_c_b)
    out_c = (u_c * chunk_spatial_mix(v_c, w_s, b_s)) @ w_c_out

    gts = jax.nn.sigmoid(gates).reshape(Bn, L, N_BRANCH, D_MODEL)
    merged = gts[:, :, 0] * out_a + gts[:, :, 1] * out_b + gts[:, :, 2] * out_c
    x = layer_norm(DEEPNORM_ALPHA * x + merged @ w_o, ln1_g, ln1_b)

    hg, hu = jnp.split(x @ w_up, 2, axis=-1)
    hg_c, new_hist_f = causal_dwconv(hg, hist_f, conv_f_w)
    ffn = (jax.nn.silu(hg_c) * hu) @ w_down
    ple = jax.nn.sigmoid(x @ w_pg) * (p @ w_pe)
    x = layer_norm(DEEPNORM_ALPHA * x + ffn + ple, ln2_g, ln2_b)
    return x, new_hist_a, new_hist_qkv, new_s.astype(s_delta.dtype), new_hist_f, v_c


def setup_inputs(seed: int = 0) -> dict:
    key = jax.random.key(seed)
    ks = iter(jax.random.split(key, 48))
    f32 = jnp.float32

    def nrm(shape, scale):
        return jax.random.normal(next(ks), shape, f32) * scale

    x_prompt = nrm((BATCH, SEQ, D_MODEL), 1.0)
    x_sample = nrm((DEC_BATCH, DEC_SEQ, D_MODEL), 1.0)
    state_conv_a = nrm((DEPTH, DEC_BATCH, CONV_A - 1, D_A), 1.0)
    state_conv_qkv = nrm((DEPTH, DEC_BATCH, CONV_B - 1, 3 * D_B), 1.0)
    state_delta = nrm((DEPTH, DEC_BATCH, H_B, DK, DV), 0.5)
    state_conv_ffn = nrm((DEPTH, DEC_BATCH, CONV_F - 1, D_FF), 1.0)
    p_prompt = nrm((DEPTH, BATCH, SEQ, D_PLE), 1.0)
    p_sample = nrm((DEPTH, DEC_BATCH, DEC_SEQ, D_PLE), 1.0)
    ln_in_g = 1.0 + nrm((D_MODEL,), 0.02)
    ln_in_b = nrm((D_MODEL,), 0.02)
    w_in = nrm((DEPTH, D_MODEL, N_IN), D_MODEL ** -0.5)
    conv_a_w = nrm((DEPTH, CONV_A, D_A), CONV_A ** -0.5)
    w_a_out = nrm((DEPTH, D_A, D_MODEL), D_A ** -0.5)
    conv_b_w = nrm((DEPTH, CONV_B, 3 * D_B), CONV_B ** -0.5)
    a_log = jnp.log(jax.random.uniform(next(ks), (DEPTH, H_B), f32, 1.0, 16.0))
    dt = jnp.exp(jax.random.uniform(next(ks), (DEPTH, H_B), f32, math.log(1e-3), math.log(1e-1)))
    dt_bias = dt + jnp.log(-jnp.expm1(-dt))
    norm_b_g = 1.0 + nrm((DEPTH, DV), 0.02)
    w_b_out = nrm((DEPTH, D_B, D_MODEL), D_B ** -0.5)
    ln_c_g = 1.0 + nrm((DEPTH, D_C), 0.02)
    ln_c_b = nrm((DEPTH, D_C), 0.02)
    w_s = nrm((DEPTH, G_C, CHUNK_C, CHUNK_C), CHUNK_C ** -0.5)
    b_s = 1.0 + nrm((DEPTH, G_C, CHUNK_C), 0.1)
    w_c_out = nrm((DEPTH, D_C, D_MODEL), D_C ** -0.5)
    w_o = nrm((DEPTH, D_MODEL, D_MODEL), D_MODEL ** -0.5 * DEEPNORM_BETA)
    ln1_g = 1.0 + nrm((DEPTH, D_MODEL), 0.02)
    ln1_b = nrm((DEPTH, D_MODEL), 0.02)
    w_up = nrm((DEPTH, D_MODEL, 2 * D_FF), D_MODEL ** -0.5)
    conv_f_w = nrm((DEPTH, CONV_F, D_FF), CONV_F ** -0.5)
    w_down = nrm((DEPTH, D_FF, D_MODEL), D_FF ** -0.5 * DEEPNORM_BETA)
    w_pe = nrm((DEPTH, D_PLE, D_MODEL), D_PLE ** -0.5 * DEEPNORM_BETA)
    w_pg = nrm((DEPTH, D_MODEL, D_MODEL), D_MODEL ** -0.5)
    ln2_g = 1.0 + nrm((DEPTH, D_MODEL), 0.02)
    ln2_b = nrm((DEPTH, D_MODEL), 0.02)
    return {
        'x_prompt': x_prompt, 'x_sample': x_sample,
        'state_conv_a': state_conv_a, 'state_conv_qkv': state_conv_qkv,
        'state_delta': state_delta, 'state_conv_ffn': state_conv_ffn,
        'p_prompt': p_prompt, 'p_sample': p_sample,
        'ln_in_g': ln_in_g, 'ln_in_b': ln_in_b,
        'w_in': w_in, 'conv_a_w': conv_a_w, 'w_a_out': w_a_out,
        'conv_b_w': conv_b_w, 'a_log': a_log, 'dt_bias': dt_bias, 'norm_b_g': norm_b_g, 'w_b_out': w_b_out,
        'ln_c_g': ln_c_g, 'ln_c_b': ln_c_b, 'w_s': w_s, 'b_s': b_s, 'w_c_out': w_c_out,
        'w_o': w_o, 'ln1_g': ln1_g, 'ln1_b': ln1_b,
        'w_up': w_up, 'conv_f_w': conv_f_w, 'w_down': w_down,
        'w_pe': w_pe, 'w_pg': w_pg, 'ln2_g': ln2_g, 'ln2_b': ln2_b,
    }


def reference(x_prompt, x_sample, state_conv_a, state_conv_qkv, state_delta, state_conv_ffn,
              p_prompt, p_sample, ln_in_g, ln_in_b,
              w_in, conv_a_w, w_a_out, conv_b_w, a_log, dt_bias, norm_b_g, w_b_out,
              ln_c_g, ln_c_b, w_s, b_s, w_c_out, w_o, ln1_g, ln1_b,
              w_up, conv_f_w, w_down, w_pe, w_pg, ln2_g, ln2_b):
    xp = layer_norm(x_prompt, ln_in_g, ln_in_b)
    xs = layer_norm(x_sample, ln_in_g, ln_in_b)
    bp = x_prompt.shape[0]
    dt_p = x_prompt.dtype
    pa, pq, pd, pf = [], [], [], []
    sa, sq, sd, sf, sv = [], [], [], [], []
    for i in range(DEPTH):
        wts = (w_in[i], conv_a_w[i], w_a_out[i], conv_b_w[i], a_log[i], dt_bias[i], norm_b_g[i], w_b_out[i],
               ln_c_g[i], ln_c_b[i], w_s[i], b_s[i], w_c_out[i], w_o[i], ln1_g[i], ln1_b[i],
               w_up[i], conv_f_w[i], w_down[i], w_pe[i], w_pg[i], ln2_g[i], ln2_b[i])
        xp, ha, hq, hs, hf, _ = layer_step(
            xp, p_prompt[i],
            jnp.zeros((bp, CONV_A - 1, D_A), dt_p),
            jnp.zeros((bp, CONV_B - 1, 3 * D_B), dt_p),
            jnp.zeros((bp, H_B, DK, DV), dt_p),
            jnp.zeros((bp, CONV_F - 1, D_FF), dt_p),
            *wts)
        pa.append(ha); pq.append(hq); pd.append(hs); pf.append(hf)
        xs, ha, hq, hs, hf, vrows = layer_step(
            xs, p_sample[i], state_conv_a[i], state_conv_qkv[i], state_delta[i], state_conv_ffn[i], *wts)
        sa.append(ha); sq.append(hq); sd.append(hs); sf.append(hf); sv.append(vrows)
    new_conv_a_p = jnp.stack(pa)
    new_conv_qkv_p = jnp.stack(pq)
    new_delta_p = jnp.stack(pd)
    new_conv_ffn_p = jnp.stack(pf)
    new_conv_a_s = jnp.stack(sa)
    new_conv_qkv_s = jnp.stack(sq)
    new_delta_s = jnp.stack(sd)
    new_conv_ffn_s = jnp.stack(sf)
    new_vchunk_s = jnp.stack(sv)
    return (xp, xs, new_conv_a_p, new_conv_qkv_p, new_delta_p, new_conv_ffn_p,
            new_conv_a_s, new_conv_qkv_s, new_delta_s, new_conv_ffn_s, new_vchunk_s)
```

```python
import numpy as np
from contextlib import ExitStack
import concourse.bass as bass
import concourse.mybir as mybir
from concourse.bass_utils import run_bass_kernel_spmd

F32 = mybir.dt.float32
BF16 = mybir.dt.bfloat16
AF = mybir.ActivationFunctionType
OP = mybir.AluOpType
AX = mybir.AxisListType

D = 2048
KC = 16
NIN = 15376
DFF = 5504
FCH = 43
DPLE = 256
NH = 8
ALPHA = 4 ** 0.25
LN_EPS = 1e-5
RMS_EPS = 1e-6
O_AH, O_ABG, O_ACG, O_Q, O_K, O_V, O_Z, O_BETA, O_DEC, O_CU, O_CV, O_G = (
    0, 1024, 2048, 3072, 4096, 5120, 6144, 7168, 7176, 7184, 8208, 9232)
TB = 2
NPT = 16 // TB
NSLOT = 6
WCOLS = 256
NEG = -30000.0


class _Stop(Exception):
    pass


class Eng:
    def __init__(self, name, h, sem):
        self.name, self.h, self.sem, self.tick = name, h, sem, 0


class Buf:
    def __init__(self, t, name):
        self.t, self.name = t, name
        self.lastw = None
        self.readers = {}
        self.dsem = None
        self.dcount = 0
        self.is_psum = False

    def __getitem__(self, idx):
        return self.t[idx]


class K:
    def __init__(self, nc, es):
        self.nc, self.es = nc, es
        self.plan = False
        self.stopped = False
        self.waited = {}
        self.engs = {}
        self.dbufs = []
        for name, h in (("pe", nc.tensor), ("act", nc.scalar), ("dve", nc.vector),
                        ("pool", nc.gpsimd), ("sp", nc.sync)):
            self.engs[name] = Eng(name, h, es.enter_context(nc.semaphore("s_" + name)))
        self.nalloc = 0

    def sb(self, shape, dt, name=None, stack=None):
        self.nalloc += 1
        name = (name or "t") + "_%d" % self.nalloc
        t = (stack or self.es).enter_context(self.nc.sbuf_tensor(name, list(shape), dt))
        return Buf(t, name)

    def ps(self, name):
        t = self.es.enter_context(self.nc.psum_tensor(name, [128, 512], F32))
        b = Buf(t, name)
        b.is_psum = True
        return b

    def _wait(self, eng, sem, semkey, val):
        k = (eng.name, semkey)
        if self.waited.get(k, 0) < val:
            eng.h.wait_ge(sem, val)
            self.waited[k] = val

    def _deps(self, eng, reads, writes):
        for b in reads:
            if b.lastw is not None:
                self._dep(eng, b, b.lastw)
            if b.is_psum:
                for e, tk in b.readers.items():
                    if e is not eng and e != 'dma':
                        self._dep(eng, b, ('e', e, tk))
        for b in writes:
            if b.lastw is not None:
                self._dep(eng, b, b.lastw)
            for e, tk in b.readers.items():
                if e == 'dma':
                    self._dep(eng, b, ('d', tk))
                else:
                    self._dep(eng, b, ('e', e, tk))

    def _dep(self, eng, b, d):
        if d[0] == 'd':
            self._wait(eng, b.dsem, "d" + b.name, 16 * d[1])
        else:
            src = d[1]
            if src is eng and eng.name in ("pe", "sp"):
                return
            self._wait(eng, src.sem, src.name, d[2])

    def op(self, en, fn, reads, writes):
        if self.plan or self.stopped:
            return
        eng = self.engs[en]
        self._deps(eng, reads, writes)
        ins = fn(eng.h)
        eng.tick += 1
        ins.then_inc(eng.sem, 1)
        for b in reads:
            b.readers[eng] = eng.tick
        for b in writes:
            b.lastw = ('e', eng, eng.tick)
            b.readers = {}

    def dma(self, en, out_ap, in_ap, buf, is_write):
        if self.plan or self.stopped:
            return
        eng = self.engs[en]
        if buf.dsem is None:
            buf.dsem = self.es.enter_context(self.nc.semaphore("d_" + buf.name))
            self.dbufs.append(buf)
        if is_write:
            self._deps(eng, [], [buf])
        else:
            self._deps(eng, [buf], [])
        ins = eng.h.dma_start(out=out_ap, in_=in_ap)
        ins.then_inc(buf.dsem, 16)
        buf.dcount += 1
        if is_write:
            buf.lastw = ('d', buf.dcount)
            buf.readers = {}
        else:
            buf.readers['dma'] = buf.dcount

    def barrier(self):
        if self.plan or self.stopped:
            return
        es = list(self.engs.values())
        for e in es:
            for s in es:
                if s is not e and s.tick > 0:
                    self._wait(e, s.sem, s.name, s.tick)
            for b in self.dbufs:
                if b.dcount > 0:
                    self._wait(e, b.dsem, "d" + b.name, 16 * b.dcount)

    def finish(self):
        sp = self.engs["sp"]
        for e in self.engs.values():
            if e is not sp and e.tick > 0:
                self._wait(sp, e.sem, e.name, e.tick)
        for b in self.dbufs:
            self._wait(sp, b.dsem, "d" + b.name, 16 * b.dcount)


def build_program(dbg=None):
    nc = bass.Bass("TRN2", target_bir_lowering=False)
    es = ExitStack()
    with es:
        _build(nc, es, dbg)
    return nc


def _build(nc, es, dbg=None):
    def din(name, shape):
        return nc.dram_tensor(name, list(shape), F32, kind="ExternalInput").ap()

    def dout(name, shape):
        return nc.dram_tensor(name, list(shape), F32, kind="ExternalOutput").ap()

    xp = din("xp", [2048, D])
    xs = din("xs", [64, D])
    ppT = din("ppT", [2, DPLE, 2048])
    psT = din("psT", [2, DPLE, 64])
    sca = din("sca", [2, 128, 8, 16, 2])
    scq = din("scq", [2, 128, 24, 16, 3])
    scf = din("scf", [2, 128, FCH, 16, 2])
    sdl = din("sdl", [2, 16, NH, 128, 128])
    w_in = din("w_in", [2, D, NIN])
    w_a_out = din("w_a_out", [2, 1024, D])
    w_b_out = din("w_b_out", [2, 1024, D])
    w_c_out = din("w_c_out", [2, 1024, D])
    w_o = din("w_o", [2, D, D])
    w_up = din("w_up", [2, D, 2 * DFF])
    w_down = din("w_down", [2, DFF, D])
    w_pe = din("w_pe", [2, DPLE, D])
    w_pg = din("w_pg", [2, D, D])
    cwa = din("cwa", [2, 128, 8, 3])
    cwq = din("cwq", [2, 128, 24, 4])
    cwf = din("cwf", [2, 128, FCH, 3])
    lnin = din("lnin", [128, 2, D])
    ln1 = din("ln1", [2, 128, 2, D])
    ln2 = din("ln2", [2, 128, 2, D])
    lnc = din("lnc", [2, 128, 2, 1024])
    nbg = din("nbg", [2, 128, 128])
    alog = din("alog", [2, 128, NH])
    dtb = din("dtb", [2, 128, NH])
    wsT_p = din("wsT_p", [2, 128, 8, 128])
    wsT_s = din("wsT_s", [2, 64, 8, 64])
    bsb_p = din("bsb_p", [2, 128, 8, 128])
    bsb_s = din("bsb_s", [2, 128, 8, 64])
    cst_p = din("cst_p", [128, 6, 128])
    cst_s = din("cst_s", [64, 6, 64])
    smk_s = din("smk_s", [64, 16])
    smT_s = din("smT_s", [128, 16, 64])

    y_p = dout("y_p", [2048, D])
    y_s = dout("y_s", [64, D])
    nca_p = dout("nca_p", [2, 128, 8, 2])
    ncq_p = dout("ncq_p", [2, 128, 24, 3])
    ncf_p = dout("ncf_p", [2, 128, FCH, 2])
    nd_p = dout("nd_p", [2, NH, 128, 128])
    nca_s = dout("nca_s", [2, 128, 8, 16, 2])
    ncq_s = dout("ncq_s", [2, 128, 24, 16, 3])
    ncf_s = dout("ncf_s", [2, 128, FCH, 16, 2])
    nd_s = dout("nd_s", [2, 16, NH, 128, 128])
    nv_s = dout("nv_s", [2, 64, 1024])

    es.enter_context(nc.Block())
    k = K(nc, es)
    NMAX = TB * 128

    wslots = [k.sb([128, KC, WCOLS], BF16, "wslot") for _ in range(NSLOT)]
    wbd = k.sb([128, KC, 16], BF16, "wbd")
    x32 = k.sb([128, TB, D], F32, "x32")
    xT = k.sb([128, KC, NMAX], BF16, "xT")
    lnp = Buf(None, "lnp")

    def lnp_alloc(stack):
        k.nalloc += 1
        lnp.t = stack.enter_context(nc.sbuf_tensor("lnp_%d" % k.nalloc, [128, 2, D], F32))
    cp32 = k.sb([128, 6, 128], F32, "cp32")
    cs32 = k.sb([64, 6, 64], F32, "cs32")
    identb = k.sb([128, 128], BF16, "identb")
    onesf = k.sb([128, 128], F32, "onesf")
    smk = k.sb([64, 16], F32, "smk")
    smT = k.sb([128, 16, 64], BF16, "smT")
    S32 = [k.sb([128, NH, 128], F32, "S32") for _ in range(2)]
    hist_a = [k.sb([128, 8, 2], F32, "hista") for _ in range(2)]
    hist_q = [k.sb([128, 24, 3], F32, "histq") for _ in range(2)]
    hist_f = [k.sb([128, FCH, 2], F32, "histf") for _ in range(2)]
    cw_a = k.sb([128, 8, 3], F32, "cwa")
    cw_q = k.sb([128, 24, 4], F32, "cwq")
    cw_f = k.sb([128, FCH, 3], F32, "cwf")
    NCB = 4
    ext = [k.sb([128, 520], F32, "ext") for _ in range(NCB)]
    acc = [k.sb([128, NMAX], F32, "acc") for _ in range(NCB)]
    tmpa = [k.sb([128, NMAX], F32, "tmpa") for _ in range(2)]
    stat = k.sb([128, 4, 6], F32, "stat")
    mv = k.sb([128, 2], F32, "mv")
    rstd = k.sb([128, 1], F32, "rstd")
    nmr = k.sb([128, 1], F32, "nmr")
    pT = k.sb([128, 2, NMAX], BF16, "pT")
    pT32 = k.sb([128, 2, NMAX], F32, "pT32")
    lncp = k.sb([128, 2, 1024], F32, "lncp")
    wm32 = k.sb([128, 8, 128], F32, "wm32")
    bsb = k.sb([128, 8, 128], F32, "bsb")
    ngb = k.sb([128, 128], F32, "ngb")
    alg = k.sb([128, NH], F32, "alg")
    dtbt = k.sb([128, NH], F32, "dtbt")
    xin16 = k.sb([128, D], BF16, "xin16")
    PS = [k.ps("ps%d" % i) for i in range(8)]
    psi = [0]

    def newps():
        p = PS[psi[0] % 8]
        psi[0] += 1
        return p

    NBLK_MAX = 400
    WSB = 200
    wscr = [[nc.dram_tensor("wscr_%d_%d" % (l_, g_), [WSB, 128, KC * WCOLS], BF16, kind="Internal").ap()
             for g_ in range(NBLK_MAX // WSB)] for l_ in range(2)]
    wspecs = []
    wstate = {"issued": 0, "cons": 0, "l": 0, "j": 0}
    wseen = set()

    def wbegin(l):
        wstate["l"] = l
        wstate["j"] = 0

    def _issue(i):
        src, kcn, ncols, l, j = wspecs[i]
        slot = wslots[i % NSLOT]
        sview = slot[:, 0:kcn, 0:ncols]
        cview = wscr[l][j // WSB][j % WSB, :, 0:kcn * ncols].rearrange("p (kc n) -> p kc n", kc=kcn)
        if (l, j) not in wseen:
            wseen.add((l, j))
            k.dma("pool", sview, src.rearrange("(kc p) n -> p kc n", p=128), slot, True)
            k.dma("pool", cview, sview, slot, False)
        else:
            k.dma("sp", sview, cview, slot, True)

    def wnext(src2d):
        kcn = src2d.shape[0] // 128
        ncols = src2d.shape[1]
        if k.stopped:
            return wslots[0], kcn, ncols
        if k.plan:
            wspecs.append((src2d, kcn, ncols, wstate["l"], wstate["j"]))
            wstate["j"] += 1
            assert wstate["j"] <= NBLK_MAX
            return wslots[0], kcn, ncols
        i = wstate["cons"]
        wstate["cons"] += 1
        assert wspecs[i][1] == kcn and wspecs[i][2] == ncols
        while wstate["issued"] < min(len(wspecs), i + NSLOT - 2) or wstate["issued"] <= i:
            _issue(wstate["issued"])
            wstate["issued"] += 1
        return wslots[i % NSLOT], kcn, ncols

    def mmg(out_buf, out_ap, pairs, reads):
        def fn(h):
            n = len(pairs)
            ins = None
            for i, (l, r) in enumerate(pairs):
                ins = h.matmul(out_ap, l, r, start=(i == 0), stop=(i == n - 1))
            return ins
        k.op("pe", fn, reads, [out_buf])

    def act(out, in_, func, reads, writes, scale=None, bias=None, eng="act"):
        kw = {}
        if scale is not None:
            kw["scale"] = scale
        if bias is not None:
            kw["bias"] = bias
        k.op(eng, lambda h: h.activation(out=out, in_=in_, func=func, **kw), reads, writes)

    def tt(out, a, b, op, reads, writes, eng="dve"):
        k.op(eng, lambda h: h.tensor_tensor(out=out, in0=a, in1=b, op=op), reads, writes)

    def ts(out, a, s1, op0, reads, writes, s2=None, op1=None, eng="dve"):
        if op1 is None:
            k.op(eng, lambda h: h.tensor_scalar(out=out, in0=a, scalar1=s1, scalar2=None, op0=op0), reads, writes)
        else:
            k.op(eng, lambda h: h.tensor_scalar(out=out, in0=a, scalar1=s1, scalar2=s2, op0=op0, op1=op1),
                 reads, writes)

    def stt(out, a, s, b, op0, op1, reads, writes):
        k.op("dve", lambda h: h.scalar_tensor_tensor(out=out, in0=a, scalar=s, in1=b, op0=op0, op1=op1),
             reads, writes)

    def cp(out, in_, reads, writes, eng="act"):
        if eng == "act":
            k.op("act", lambda h: h.activation(out=out, in_=in_, func=AF.Copy), reads, writes)
        else:
            k.op(eng, lambda h: h.tensor_copy(out=out, in_=in_), reads, writes)

    def memset(buf, ap, val):
        k.op("dve", lambda h: h.memset(ap, val), [], [buf])

    def load(buf, out_ap, in_ap, eng="sp"):
        k.dma(eng, out_ap, in_ap, buf, True)

    def store(buf, out_ap, in_ap, eng="sp"):
        k.dma(eng, out_ap, in_ap, buf, False)

    def stage(name):
        if dbg is not None and dbg.get("stop") == name:
            k.stopped = True

    dcount = [0]

    def dump(name, buf, ap):
        if dbg is None or k.plan or name not in dbg.get("dumps", ()):
            return
        dcount[0] += 1
        dt = nc.dram_tensor("dbg_%s_%d" % (name, dcount[0]), list(ap.shape), ap.dtype, kind="ExternalOutput").ap()
        store(buf, dt, ap)

    def setup_consts():
        load(cp32, cp32[:], cst_p[:, :, :])
        load(cs32, cs32[:], cst_s[:, :, :])
        load(smk, smk[:], smk_s[:, :])
        load(smT, smT[:], smT_s[:, :, :], eng="pool")
        cp(identb[:], cp32[:, 0, :], [cp32], [identb], eng="dve")
        memset(onesf, onesf[:], 1.0)
        for l in range(2):
            memset(S32[l], S32[l][:], 0.0)
            memset(hist_a[l], hist_a[l][:], 0.0)
            memset(hist_q[l], hist_q[l][:], 0.0)
            memset(hist_f[l], hist_f[l][:], 0.0)

    def layernorm_block(R, b, width=D, src=None, gb=None):
        src = src or x32
        gb = gb or lnp
        xin = src[0:R, b, 0:width]
        nch = width // 512
        for c in range(nch):
            k.op("dve", lambda h, c=c: h.bn_stats(out=stat[0:R, c, :], in_=src[0:R, b, c * 512:(c + 1) * 512]),
                 [src], [stat])
        k.op("dve", lambda h: h.bn_aggr(out=mv[0:R, :], in_=stat[0:R, 0:nch, :]), [stat], [mv])
        ts(rstd[0:R, :], mv[0:R, 1:2], LN_EPS, OP.add, [mv], [rstd])
        act(rstd[0:R, :], rstd[0:R, :], AF.Sqrt, [rstd], [rstd])
        k.op("dve", lambda h: h.reciprocal(out=rstd[0:R, :], in_=rstd[0:R, :]), [rstd], [rstd])
        stt(nmr[0:R, :], mv[0:R, 0:1], -1.0, rstd[0:R, :], OP.mult, OP.mult, [mv, rstd], [nmr])
        act(xin, xin, AF.Identity, [src, rstd, nmr], [src], scale=rstd[0:R, 0:1], bias=nmr[0:R, 0:1])
        tt(xin, xin, gb[0:R, 0, 0:width], OP.mult, [src, gb], [src], eng="pool")
        tt(xin, xin, gb[0:R, 1, 0:width], OP.add, [src, gb], [src])

    def transpose_block(R, b, c0, srcbuf=None, src_of=None):
        for g4 in range(KC // 4):
            p = newps()
            for j in range(4):
                kc = g4 * 4 + j
                l = src_of(kc)
                mmg(p, p[:, j * R:(j + 1) * R], [(l, identb[0:R, 0:R])], [srcbuf, identb])
            cp(xT[:, g4 * 4:(g4 + 1) * 4, c0:c0 + R],
               p[:, 0:4 * R].rearrange("p (a r) -> p a r", a=4), [p], [xT],
               eng=("act" if g4 % 2 == 0 else "dve"))

    def run_tile_layer(l, tl, first_layer, last_layer):
        with ExitStack() as tls:
            _rtl(l, tl, first_layer, last_layer, tls)

    def _rtl(l, tl, first_layer, last_layer, tls):
        kind, R, NB, nseq, Ls, tok0 = tl["kind"], tl["R"], tl["NB"], tl["nseq"], tl["Ls"], tl["tok0"]
        N = R * NB
        sample = kind == "s"
        cst = cs32 if sample else cp32
        identR = cst[0:R, 0, 0:R]
        maskLow = cst[0:R, 1, 0:R]
        maskUp = cst[0:R, 2, 0:R]
        maskC = cst[0:R, 3, 0:R]
        triC = cst[0:R, 4, 0:R]
        totC = cst[0:R, 5, 0:R]
        nlev = {128: 6, 4: 1}[Ls]

        def wst(psb, wb, kcn, c0, rbuf, rhs_of, n0, n1):
            mmg(psb, psb[:, 0:n1 - n0], [(wb[:, kc, c0:c0 + 128], rhs_of(kc)[:, n0:n1]) for kc in range(kcn)],
                [wb, rbuf])

        def xst(psb, pcol, wb, kcn, ncols, lbuf, lhs_of, b):
            mmg(psb, psb[0:R, pcol:pcol + ncols],
                [(lhs_of(kc)[:, b * R:(b + 1) * R], wb[:, kc, 0:ncols]) for kc in range(kcn)], [wb, lbuf])

        xTk = lambda kc: xT[:, kc, :]

        def conv_chunk(psb, W, cw, m, hist_buf, hs_buf, e):
            Wm1 = W - 1
            Lc = Ls if sample else N
            ex = ext[e]
            exv = ex[:, 0:nseq * (Wm1 + Lc)].rearrange("p (q t) -> p q t", q=nseq)
            pv = psb[:, 0:N].rearrange("p (q t) -> p q t", q=nseq)
            cp(exv[:, :, Wm1:], pv, [psb], [ex], eng="act")
            if sample:
                cp(exv[:, :, 0:Wm1], hs_buf[:, m, :, :], [hs_buf], [ex], eng="dve")
            else:
                cp(exv[:, 0, 0:Wm1], hist_buf[:, m, :], [hist_buf], [ex], eng="dve")
            a = acc[e]
            av = a[:, 0:N].rearrange("p (q t) -> p q t", q=nseq)
            act(av, exv[:, :, 0:Lc], AF.Copy, [ex, cw], [a], scale=cw[:, m, 0:1])
            for j in range(1, W):
                stt(av, exv[:, :, j:j + Lc], cw[:, m, j:j + 1], av, OP.mult, OP.add, [ex, cw, a], [a])
            if sample:
                cp(hs_buf[:, m, :, :], exv[:, :, Lc:Lc + Wm1], [ex], [hs_buf], eng="dve")
            else:
                cp(hist_buf[:, m, :], exv[:, 0, Lc:Lc + Wm1], [ex], [hist_buf], eng="dve")
            return a

        wbegin(l)
        hs_a = hs_q = hs_f = None
        if sample:
            hs_a = k.sb([128, 8, 16, 2], F32, "hsa", tls)
            hs_q = k.sb([128, 24, 16, 3], F32, "hsq", tls)
            hs_f = k.sb([128, FCH, 16, 2], F32, "hsf", tls)
        load(cw_a, cw_a[:], cwa[l])
        load(cw_q, cw_q[:], cwq[l])
        load(cw_f, cw_f[:], cwf[l])
        if sample:
            load(hs_a, hs_a[:], sca[l])
            load(hs_q, hs_q[:], scq[l])
            load(hs_f, hs_f[:], scf[l])

        phm = ExitStack()
        NA = N
        with phm:
            mA = k.sb([128, 8, NA], BF16, "mA", phm)
            mB = k.sb([128, 8, NA], BF16, "mB", phm)
            mC = k.sb([128, 8, NA], BF16, "mC", phm)
            mg = k.sb([128, NB, D], BF16, "mg", phm)
            mgt = k.sb([128, NB, WCOLS], F32, "mgt", phm)
            sg = k.sb([128, WCOLS], F32, "sg", phm)
            uT = k.sb([128, 8, NA], BF16, "uT", phm)
            cvall = k.sb([128, NB, 1024], F32, "cvall", phm)
            vcb = k.sb([128, NB, 1024], BF16, "vcb", phm)
            wmT = k.sb([128, 8, 128], BF16, "wmT", phm)
            qkvT = k.sb([128, 3, NA], BF16, "qkvT", phm)
            qkvTb = k.sb([128, 3, NA], BF16, "qkvTb", phm)
            bd = k.sb([128, TB, 16], F32, "bd", phm)
            zs = k.sb([128, NB, 128], BF16, "zs", phm)
            zsb = k.sb([128, NB, 128], BF16, "zsb", phm)
            sc = {n: k.sb([128, TB, NH], F32, n, phm) for n in
                  ("beta", "g", "gcum", "gtot", "egc", "ekd", "nbgc", "t8a", "t8b")}
            eglb = k.sb([128, TB, 16 * NH], F32, "eglb", phm)
            gsel = k.sb([128, 16 * NH], F32, "gsel", phm)
            def _mkT():
                d = dict(
                    sq=k.sb([128, 2, 128], F32, "sq", phm),
                    nT=k.sb([128, 2, 128], BF16, "nT", phm),
                    t1=k.sb([128, 128], F32, "t1", phm), t2=k.sb([128, 128], F32, "t2", phm),
                    dup=k.sb([128, 128], F32, "dup", phm), dlow=k.sb([128, 128], F32, "dlow", phm),
                    Am=[k.sb([128, 128], F32, "Am", phm) for _ in range(2)],
                    At=[k.sb([128, 128], F32, "At", phm) for _ in range(2)],
                    qkT=k.sb([128, 128], BF16, "qkT", phm), Pt32=k.sb([128, 128], F32, "Pt32", phm),
                    Ptb=k.sb([128, 128], BF16, "Ptb", phm), vb=k.sb([128, 128], F32, "vb", phm),
                    kd=k.sb([128, 128], BF16, "kd", phm))
                d["rn"] = d["sq"]
                d["Dg"] = d["t2"]
                return d
            TS2 = [[_mkT() for _b in range(NB)] for _p in range(2)]
            TS = TS2[0]
            t1 = TS[0]["t1"]
            dd = k.sb([128, 128], BF16, "dd", phm)
            vnew = k.sb([128, 128], BF16, "vnew", phm)
            to = k.sb([128, 128], F32, "to", phm)
            oh = k.sb([128, 128], F32, "oh", phm)
            og = k.sb([128, 128], BF16, "og", phm)
            ngz = k.sb([128, 128], F32, "ngz", phm)
            ss = k.sb([128, 1], F32, "ss", phm)
            Sb = k.sb([128, 128], BF16, "Sb", phm)
            if sample:
                S32s = k.sb([128, 16, 128], F32, "S32s", phm)
                Sbs = k.sb([128, 16, 128], BF16, "Sbs", phm)
                knTm = k.sb([128, 16, 64], BF16, "knTm", phm)
                qnTm = k.sb([128, 16, 64], BF16, "qnTm", phm)
                kdm = k.sb([64, 16, 128], BF16, "kdm", phm)
                vc32s = k.sb([64, 1024], F32, "vc32s", phm)

            load(lncp, lncp[:], lnc[l])
            load(ngb, ngb[:], nbg[l])
            load(alg, alg[:], alog[l])
            load(dtbt, dtbt[:], dtb[l])
            if sample:
                load(wm32, wm32[0:64, :, 0:64], wsT_s[l])
                load(bsb, bsb[:, :, 0:64], bsb_s[l])
            else:
                load(wm32, wm32[:], wsT_p[l])
                load(bsb, bsb[:], bsb_p[l])
            tt(wmT[0:R, :, 0:R], wm32[0:R, :, 0:R], maskC.unsqueeze(1).broadcast_to([R, 8, R]), OP.mult,
               [wm32, cst], [wmT])
            act(alg[:], alg[:], AF.Exp, [alg], [alg])
            ts(alg[:], alg[:], -1.0, OP.mult, [alg], [alg])

            for mp in range(4):
                wh, _, _ = wnext(w_in[l, :, O_AH + mp * 256: O_AH + (mp + 1) * 256])
                wg, _, _ = wnext(w_in[l, :, O_ABG + mp * 256: O_ABG + (mp + 1) * 256])
                wc, _, _ = wnext(w_in[l, :, O_ACG + mp * 256: O_ACG + (mp + 1) * 256])
                for mi in range(2):
                    m = mp * 2 + mi
                    e = m % NCB
                    ph, pg, pc = newps(), newps(), newps()
                    wst(ph, wh, KC, mi * 128, xT, xTk, 0, N)
                    wst(pg, wg, KC, mi * 128, xT, xTk, 0, N)
                    wst(pc, wc, KC, mi * 128, xT, xTk, 0, N)
                    cp(tmpa[e % 2][:, 0:N], ph[:, 0:N], [ph], [tmpa[e % 2]], eng="act")
                    tt(pc[:, 0:N], tmpa[e % 2][:, 0:N], pc[:, 0:N], OP.mult, [tmpa[e % 2], pc], [pc])
                    a = conv_chunk(pc, 3, cw_a, m, hist_a[l], hs_a, e)
                    tt(mA[:, m, 0:N], a[:, 0:N], pg[:, 0:N], OP.mult, [a, pg], [mA])
            dump("mA", mA, mA[:])
            stage("S1")
            if sample:
                store(hs_a, nca_s[l], hs_a[:])

            for mp in range(4):
                wu, _, _ = wnext(w_in[l, :, O_CU + mp * 256: O_CU + (mp + 1) * 256])
                for mi in range(2):
                    m = mp * 2 + mi
                    pu = newps()
                    wst(pu, wu, KC, mi * 128, xT, xTk, 0, N)
                    act(uT[:, m, 0:N], pu[:, 0:N], AF.Gelu_apprx_tanh, [pu], [uT])
            stage("S4a")
            for cgp in range(4):
                wv, _, _ = wnext(w_in[l, :, O_CV + cgp * 256: O_CV + (cgp + 1) * 256])
                for b in range(NB):
                    pv = newps()
                    xst(pv, 0, wv, KC, 256, xT, xTk, b)
                    act(cvall[0:R, b, cgp * 256:(cgp + 1) * 256], pv[0:R, 0:256], AF.Gelu_apprx_tanh, [pv], [cvall])
            for b in range(NB):
                for c in range(2):
                    k.op("dve", lambda h, c=c, b=b: h.bn_stats(out=stat[0:R, c, :], in_=cvall[0:R, b, c * 512:(c + 1) * 512]),
                         [cvall], [stat])
                k.op("dve", lambda h: h.bn_aggr(out=mv[0:R, :], in_=stat[0:R, 0:2, :]), [stat], [mv])
                ts(rstd[0:R, :], mv[0:R, 1:2], LN_EPS, OP.add, [mv], [rstd])
                act(rstd[0:R, :], rstd[0:R, :], AF.Sqrt, [rstd], [rstd])
                k.op("dve", lambda h: h.reciprocal(out=rstd[0:R, :], in_=rstd[0:R, :]), [rstd], [rstd])
                ts(cvall[0:R, b, :], cvall[0:R, b, :], mv[0:R, 0:1], OP.subtract, [cvall, mv, rstd], [cvall],
                   s2=rstd[0:R, 0:1], op1=OP.mult)
                tt(cvall[0:R, b, :], cvall[0:R, b, :], lncp[0:R, 0, :], OP.mult, [cvall, lncp], [cvall])
                if sample:
                    tt(vc32s[:, :], cvall[0:R, b, :], lncp[0:R, 1, :], OP.add, [cvall, lncp], [vc32s])
                    store(vc32s, nv_s[l], vc32s[:, :])
                    cp(vcb[0:R, b, :], vc32s[:, :], [vc32s], [vcb], eng="act")
                else:
                    tt(vcb[0:R, b, :], cvall[0:R, b, :], lncp[0:R, 1, :], OP.add, [cvall, lncp], [vcb])
                for gg in range(8):
                    pm = newps()
                    mmg(pm, pm[:, 0:R], [(vcb[0:R, b, gg * 128:(gg + 1) * 128], wmT[0:R, gg, 0:R])], [vcb, wmT])
                    tt(t1[:, 0:R], pm[:, 0:R], bsb[:, gg, 0:R], OP.add, [pm, bsb], [t1])
                    tt(mC[:, gg, b * R:(b + 1) * R], t1[:, 0:R], uT[:, gg, b * R:(b + 1) * R], OP.mult,
                       [t1, uT], [mC])

            dump("mC", mC, mC[:])
            stage("S4")
            dump("uT", uT, uT[:])
            dump("vcb", vcb, vcb[:])
            load_bd = w_in[l, :, O_BETA:O_BETA + 16]
            k.dma("pool", wbd[:, :, :], load_bd.rearrange("(kc p) n -> p kc n", p=128), wbd, True)
            for b in range(NB):
                pb = newps()
                xst(pb, 0, wbd, KC, 16, xT, xTk, b)
                cp(bd[0:R, b, :], pb[0:R, 0:16], [pb], [bd], eng="act")
                act(sc["beta"][0:R, b, :], bd[0:R, b, 0:8], AF.Sigmoid, [bd], [sc["beta"]])
                tt(sc["t8a"][0:R, b, :], bd[0:R, b, 8:16], dtbt[0:R, :], OP.add, [bd, dtbt], [sc["t8a"]])
                act(sc["t8b"][0:R, b, :], sc["t8a"][0:R, b, :], AF.Abs, [sc["t8a"]], [sc["t8b"]])
                act(sc["t8b"][0:R, b, :], sc["t8b"][0:R, b, :], AF.Exp, [sc["t8b"]], [sc["t8b"]], scale=-1.0)
                ts(sc["t8b"][0:R, b, :], sc["t8b"][0:R, b, :], 1.0, OP.add, [sc["t8b"]], [sc["t8b"]])
                act(sc["t8b"][0:R, b, :], sc["t8b"][0:R, b, :], AF.Ln, [sc["t8b"]], [sc["t8b"]])
                ts(sc["t8a"][0:R, b, :], sc["t8a"][0:R, b, :], 0.0, OP.max, [sc["t8a"]], [sc["t8a"]])
                tt(sc["t8a"][0:R, b, :], sc["t8a"][0:R, b, :], sc["t8b"][0:R, b, :], OP.add,
                   [sc["t8a"], sc["t8b"]], [sc["t8a"]])
                tt(sc["g"][0:R, b, :], sc["t8a"][0:R, b, :], alg[0:R, :], OP.mult, [sc["t8a"], alg], [sc["g"]])
                pgc = newps()
                mmg(pgc, pgc[0:R, 0:8], [(triC, sc["g"][0:R, b, :])], [cst, sc["g"]])
                mmg(pgc, pgc[0:R, 8:16], [(totC, sc["g"][0:R, b, :])], [cst, sc["g"]])
                cp(sc["gcum"][0:R, b, :], pgc[0:R, 0:8], [pgc], [sc["gcum"]], eng="act")
                cp(sc["gtot"][0:R, b, :], pgc[0:R, 8:16], [pgc], [sc["gtot"]], eng="act")
                act(sc["egc"][0:R, b, :], sc["gcum"][0:R, b, :], AF.Exp, [sc["gcum"]], [sc["egc"]])
                tt(sc["ekd"][0:R, b, :], sc["gtot"][0:R, b, :], sc["gcum"][0:R, b, :], OP.subtract,
                   [sc["gtot"], sc["gcum"]], [sc["ekd"]])
                act(sc["ekd"][0:R, b, :], sc["ekd"][0:R, b, :], AF.Exp, [sc["ekd"]], [sc["ekd"]])
                tt(sc["nbgc"][0:R, b, :], sc["beta"][0:R, b, :], sc["egc"][0:R, b, :], OP.mult,
                   [sc["beta"], sc["egc"]], [sc["nbgc"]])
                ts(sc["nbgc"][0:R, b, :], sc["nbgc"][0:R, b, :], -1.0, OP.mult, [sc["nbgc"]], [sc["nbgc"]])
                if sample:
                    tt(gsel[0:R, :].rearrange("p (q h) -> p q h", q=16),
                       sc["g"][0:R, b, :].unsqueeze(1).broadcast_to([R, 16, NH]),
                       smk[0:R, :].unsqueeze(2).broadcast_to([R, 16, NH]), OP.mult, [sc["g"], smk], [gsel])
                    nq = 16
                else:
                    cp(gsel[0:R, 0:NH], sc["g"][0:R, b, :], [sc["g"]], [gsel], eng="dve")
                    nq = 1
                pe_ = newps()
                mmg(pe_, pe_[:, 0:nq * NH], [(onesf[0:R, :], gsel[0:R, 0:nq * NH])], [onesf, gsel])
                act(eglb[:, b, 0:nq * NH], pe_[:, 0:nq * NH], AF.Exp, [pe_], [eglb])

            stage("SBs")
            qkvT2 = [qkvT, qkvTb]
            zs2 = [zs, zsb]

            def gen_proj(h, qkvT, zs):
                for j, (off, mch) in enumerate(((O_K, 8 + h), (O_Q, h), (O_V, 16 + h))):
                    wj, _, _ = wnext(w_in[l, :, off + h * 128: off + (h + 1) * 128])
                    pj = newps()
                    wst(pj, wj, KC, 0, xT, xTk, 0, N)
                    a = conv_chunk(pj, 4, cw_q, mch, hist_q[l], hs_q, (3 * h + j) % NCB)
                    act(qkvT[:, j, 0:N], a[:, 0:N], AF.Silu, [a], [qkvT])
                    yield
                wz, _, _ = wnext(w_in[l, :, O_Z + h * 128: O_Z + (h + 1) * 128])
                for b in range(NB):
                    pz = newps()
                    xst(pz, 0, wz, KC, 128, xT, xTk, b)
                    act(zs[0:R, b, :], pz[0:R, 0:128], AF.Silu, [pz], [zs])
                    yield

            def prep_chain(b, T, h, qkvT):
                cs = slice(b * R, (b + 1) * R)
                beta_h = sc["beta"][0:R, b, h:h + 1]
                gcum_h = sc["gcum"][0:R, b, h:h + 1]
                sq, rn, nT, Dg, t1, t2, dup, dlow = (T[n] for n in ("sq", "rn", "nT", "Dg", "t1", "t2", "dup", "dlow"))
                Am, At, qkT, Pt32, Ptb, vb, kd = (T[n] for n in ("Am", "At", "qkT", "Pt32", "Ptb", "vb", "kd"))
                act(sq[:, :, 0:R], qkvT[:, 0:2, cs], AF.Square, [qkvT], [sq])
                ts(Dg[0:R, 0:R], identR, gcum_h, OP.mult, [cst, sc["gcum"]], [Dg], eng="pool")
                yield
                pn = newps()
                mmg(pn, pn[:, 0:2 * R].rearrange("p (a r) -> p a r", a=2),
                    [(onesf[:, :], sq[:, :, 0:R])], [onesf, sq])
                pr = newps()
                mmg(pr, pr[0:R, 0:R], [(onesf[0:R, 0:R], Dg[0:R, 0:R])], [onesf, Dg])
                ts(rn[:, :, 0:R], pn[:, 0:2 * R].rearrange("p (a r) -> p a r", a=2), RMS_EPS, OP.add,
                   [pn], [rn])
                ts(t1[0:R, 0:R], pr[0:R, 0:R], gcum_h, OP.subtract, [pr, sc["gcum"]], [t1])
                yield
                act(rn[:, :, 0:R], rn[:, :, 0:R], AF.Sqrt, [rn], [rn])
                stt(t2[0:R, 0:R], t1[0:R, 0:R], 0.0, maskUp, OP.min, OP.add, [t1, cst], [t2])
                yield
                k.op("dve", lambda hh: hh.reciprocal(out=rn[:, :, 0:R], in_=rn[:, :, 0:R]), [rn], [rn])
                act(dup[0:R, 0:R], t2[0:R, 0:R], AF.Exp, [t2], [dup])
                yield
                ts(rn[:, 1, 0:R], rn[:, 1, 0:R], 128 ** -0.5, OP.mult, [rn], [rn])
                stt(t2[0:R, 0:R], t1[0:R, 0:R], 0.0, maskLow, OP.max, OP.subtract, [t1, cst], [t2])
                yield
                tt(nT[:, :, 0:R], qkvT[:, 0:2, cs], rn[:, :, 0:R], OP.mult, [qkvT, rn], [nT], eng="pool")
                act(dlow[0:R, 0:R], t2[0:R, 0:R], AF.Exp, [t2], [dlow], scale=-1.0)
                yield
                pkv = newps()
                mmg(pkv, pkv[0:R, 0:128], [(nT[:, 0, 0:R], identb[:, :])], [nT, identb])
                mmg(pkv, pkv[0:R, 128:256], [(qkvT[:, 2, cs], identb[:, :])], [qkvT, identb])
                pg_ = newps()
                mmg(pg_, pg_[0:R, 0:2 * R].rearrange("p (a r) -> p a r", a=2),
                    [(nT[:, 0, 0:R], nT[:, :, 0:R])], [nT])
                stt(Am[0][0:R, 0:R], pg_[0:R, 0:R], beta_h, dlow[0:R, 0:R], OP.mult, OP.mult,
                    [pg_, sc["beta"], dlow], [Am[0]])
                ts(kd[0:R, :], pkv[0:R, 0:128], sc["ekd"][0:R, b, h:h + 1], OP.mult, [pkv, sc["ekd"]], [kd])
                tt(qkT[0:R, 0:R], pg_[0:R, R:2 * R], dup[0:R, 0:R], OP.mult, [pg_, dup], [qkT])
                ts(vb[0:R, :], pkv[0:R, 128:256], beta_h, OP.mult, [pkv, sc["beta"]], [vb])
                yield
                pt = newps()
                mmg(pt, pt[0:R, 0:R], [(Am[0][0:R, 0:R], identR)], [Am[0], cst])
                cp(At[0][0:R, 0:R], pt[0:R, 0:R], [pt], [At[0]], eng="act")
                tt(Pt32[0:R, 0:R], identR, pt[0:R, 0:R], OP.subtract, [cst, pt], [Pt32])
                yield
                cur = 0
                for i in range(1, nlev + 2):
                    nx = 1 - cur
                    p1 = p2 = p3 = None
                    if i <= nlev:
                        p1 = newps()
                        mmg(p1, p1[0:R, 0:R], [(At[cur][0:R, 0:R], Am[cur][0:R, 0:R])], [At[cur], Am[cur]])
                        if i < nlev:
                            p2 = newps()
                            mmg(p2, p2[0:R, 0:R], [(Am[cur][0:R, 0:R], At[cur][0:R, 0:R])], [At[cur], Am[cur]])
                    if i >= 2:
                        p3 = newps()
                        mmg(p3, p3[0:R, 0:R], [(Am[cur][0:R, 0:R], Pt32[0:R, 0:R])], [Am[cur], Pt32])
                    if p1 is not None:
                        cp(Am[nx][0:R, 0:R], p1[0:R, 0:R], [p1], [Am[nx]], eng="act")
                    if p2 is not None:
                        cp(At[nx][0:R, 0:R], p2[0:R, 0:R], [p2], [At[nx]], eng="act")
                    if p3 is not None:
                        tt(Pt32[0:R, 0:R], Pt32[0:R, 0:R], p3[0:R, 0:R], OP.add, [Pt32, p3], [Pt32])
                    yield
                    cur = nx
                cp(Ptb[0:R, 0:R], Pt32[0:R, 0:R], [Pt32], [Ptb], eng="act")


            def gen_prep(h, qkvT, TSp):
                gens = [prep_chain(b, TSp[b], h, qkvT) for b in range(NB)]
                while gens:
                    for g_ in list(gens):
                        try:
                            next(g_)
                        except StopIteration:
                            gens.remove(g_)
                    yield

            def gen_chain(*gs):
                for g_ in gs:
                    for _ in g_:
                        yield

            def gen_scan(h, zs, TS):
              if sample:
                load(S32s, S32s[:], sdl[l, :, h].rearrange("q k v -> k q v"))
                cp(Sbs[:], S32s[:], [S32s], [Sbs], eng="act")
              for b in range(NB):
                cs = slice(b * R, (b + 1) * R)
                T = TS[b]
                nT, qkT, Ptb, vb, kd = T["nT"], T["qkT"], T["Ptb"], T["vb"], T["kd"]
                pk_, pq_ = newps(), newps()
                if sample:
                    tt(knTm[:, :, :], nT[:, 0, 0:R].unsqueeze(1).broadcast_to([128, 16, R]), smT[:, :, :],
                       OP.mult, [nT, smT], [knTm])
                    tt(qnTm[:, :, :], nT[:, 1, 0:R].unsqueeze(1).broadcast_to([128, 16, R]), smT[:, :, :],
                       OP.mult, [nT, smT], [qnTm])
                    mmg(pk_, pk_[0:R, 0:128], [(knTm[:, q, :], Sbs[:, q, :]) for q in range(16)], [knTm, Sbs])
                    mmg(pq_, pq_[0:R, 0:128], [(qnTm[:, q, :], Sbs[:, q, :]) for q in range(16)], [qnTm, Sbs])
                else:
                    cp(Sb[:, :], S32[l][:, h, :], [S32[l]], [Sb], eng="act")
                    mmg(pk_, pk_[0:R, 0:128], [(nT[:, 0, 0:R], Sb[:, :])], [nT, Sb])
                    mmg(pq_, pq_[0:R, 0:128], [(nT[:, 1, 0:R], Sb[:, :])], [nT, Sb])
                stt(dd[0:R, :], pk_[0:R, 0:128], sc["nbgc"][0:R, b, h:h + 1], vb[0:R, :], OP.mult, OP.add,
                    [pk_, sc["nbgc"], vb], [dd])
                ts(to[0:R, :], pq_[0:R, 0:128], sc["egc"][0:R, b, h:h + 1], OP.mult, [pq_, sc["egc"]], [to])
                yield
                pv_ = newps()
                mmg(pv_, pv_[0:R, 0:128], [(Ptb[0:R, 0:R], dd[0:R, :])], [Ptb, dd])
                cp(vnew[0:R, :], pv_[0:R, 0:128], [pv_], [vnew], eng="act")
                yield
                po = newps()
                mmg(po, po[0:R, 0:128], [(qkT[0:R, 0:R], vnew[0:R, :])], [qkT, vnew])
                tt(oh[0:R, :], to[0:R, :], po[0:R, 0:128], OP.add, [to, po], [oh])
                if sample:
                    tt(kdm[:, :, :], kd[0:R, :].unsqueeze(1).broadcast_to([R, 16, 128]),
                       smk[0:R, :].unsqueeze(2).broadcast_to([R, 16, 128]), OP.mult, [kd, smk], [kdm])
                    for q in range(16):
                        pss = newps()
                        mmg(pss, pss[:, 0:128], [(kdm[:, q, :], vnew[0:R, :])], [kdm, vnew])
                        stt(S32s[:, q, :], S32s[:, q, :], eglb[:, b, q * NH + h: q * NH + h + 1], pss[:, 0:128],
                            OP.mult, OP.add, [S32s, eglb, pss], [S32s])
                    store(S32s, nd_s[l, :, h].rearrange("q k v -> k q v"), S32s[:])
                else:
                    pss = newps()
                    mmg(pss, pss[:, 0:128], [(kd[0:R, :], vnew[0:R, :])], [kd, vnew])
                    stt(S32[l][:, h, :], S32[l][:, h, :], eglb[:, b, h:h + 1], pss[:, 0:128], OP.mult, OP.add,
                        [S32[l], eglb, pss], [S32[l]])
                yield
                tt(to[0:R, :], oh[0:R, :], oh[0:R, :], OP.mult, [oh], [to], eng="pool")
                k.op("dve", lambda hh: hh.tensor_reduce(out=ss[0:R, :], in_=to[0:R, :], axis=AX.X, op=OP.add),
                     [to], [ss])
                ts(ss[0:R, :], ss[0:R, :], 1.0 / 128, OP.mult, [ss], [ss], s2=RMS_EPS, op1=OP.add)
                yield
                act(ss[0:R, :], ss[0:R, :], AF.Sqrt, [ss], [ss])
                k.op("dve", lambda hh: hh.reciprocal(out=ss[0:R, :], in_=ss[0:R, :]), [ss], [ss])
                tt(ngz[0:R, :], zs[0:R, b, :], ngb[0:R, :], OP.mult, [zs, ngb], [ngz], eng="pool")
                stt(og[0:R, :], oh[0:R, :], ss[0:R, 0:1], ngz[0:R, :], OP.mult, OP.mult, [oh, ss, ngz], [og])
                yield
                pob = newps()
                mmg(pob, pob[:, 0:R], [(og[0:R, :], identb[0:R, 0:R])], [og, identb])
                cp(mB[:, h, cs], pob[:, 0:R], [pob], [mB], eng="act")

            for _ in gen_chain(gen_proj(0, qkvT2[0], zs2[0]), gen_prep(0, qkvT2[0], TS2[0])):
                pass
            for h in range(NH):
                par = h % 2
                gl_ = [gen_scan(h, zs2[par], TS2[par])]
                if h + 1 < NH:
                    gl_.append(gen_chain(gen_proj(h + 1, qkvT2[1 - par], zs2[1 - par]),
                                         gen_prep(h + 1, qkvT2[1 - par], TS2[1 - par])))
                while gl_:
                    for g_ in list(gl_):
                        try:
                            next(g_)
                        except StopIteration:
                            gl_.remove(g_)
            if sample:
                store(hs_q, ncq_s[l], hs_q[:])

            dump("mB", mB, mB[:])
            stage("SB")
            for cg in range(8):
                c0 = cg * 256
                for br, (wout, mbuf) in enumerate(((w_a_out, mA), (w_b_out, mB), (w_c_out, mC))):
                    wo_, _, _ = wnext(wout[l, :, c0:c0 + 256])
                    wg_, _, _ = wnext(w_in[l, :, O_G + br * D + c0: O_G + br * D + c0 + 256])
                    for b in range(NB):
                        po_, pg2 = newps(), newps()
                        xst(po_, 0, wo_, 8, 256, mbuf, lambda kc, mbuf=mbuf: mbuf[:, kc, :], b)
                        xst(pg2, 0, wg_, KC, 256, xT, xTk, b)
                        act(sg[0:R, :], pg2[0:R, 0:256], AF.Sigmoid, [pg2], [sg])
                        if br == 0:
                            tt(mgt[0:R, b, :], sg[0:R, :], po_[0:R, 0:256], OP.mult, [sg, po_], [mgt])
                        else:
                            tt(sg[0:R, :], sg[0:R, :], po_[0:R, 0:256], OP.mult, [sg, po_], [sg])
                            if br == 1:
                                tt(mgt[0:R, b, :], mgt[0:R, b, :], sg[0:R, :], OP.add, [mgt, sg], [mgt])
                            else:
                                tt(mg[0:R, b, c0:c0 + 256], mgt[0:R, b, :], sg[0:R, :], OP.add, [mgt, sg], [mg])
            dump("mg", mg, mg[:])
            stage("S6")
            for b in range(NB):
                transpose_block(R, b, b * R, srcbuf=mg, src_of=lambda kc, b=b: mg[0:R, b, kc * 128:(kc + 1) * 128])
        k.barrier()

        for cg in range(8):
            c0 = cg * 256
            wo_, _, _ = wnext(w_o[l, :, c0:c0 + 256])
            for b in range(NB):
                py = newps()
                xst(py, 0, wo_, KC, 256, xT, xTk, b)
                stt(x32[0:R, b, c0:c0 + 256], x32[0:R, b, c0:c0 + 256], ALPHA, py[0:R, 0:256], OP.mult, OP.add,
                    [x32, py], [x32])
        phf = ExitStack()
        with phf:
            hT = k.sb([128, FCH, N], BF16, "hT", phf)
            sg2 = k.sb([128, WCOLS], F32, "sg2", phf)
            lnp_alloc(phf)
            load(lnp, lnp[:], ln1[l])

            def norm_and_transpose(last):
                for b in range(NB):
                    layernorm_block(R, b)
                    if last:
                        dst = y_s[:, :] if sample else y_p[tok0 + b * 128: tok0 + (b + 1) * 128, :]
                        store(x32, dst, x32[0:R, b, :])
                    else:
                        cp(xin16[0:R, :], x32[0:R, b, :], [x32], [xin16], eng="act")
                        transpose_block(R, b, b * R, srcbuf=xin16,
                                        src_of=lambda kc: xin16[0:R, kc * 128:(kc + 1) * 128])

            norm_and_transpose(False)
            dump("x1", x32, x32[:])
            stage("S7")

            for mp in range(22):
                nm = 2 if mp < 21 else 1
                wgt, _, _ = wnext(w_up[l, :, mp * 256: mp * 256 + nm * 128])
                wut, _, _ = wnext(w_up[l, :, DFF + mp * 256: DFF + mp * 256 + nm * 128])
                for mi in range(nm):
                    m = mp * 2 + mi
                    e = m % NCB
                    pgh, puh = newps(), newps()
                    wst(pgh, wgt, KC, mi * 128, xT, xTk, 0, N)
                    wst(puh, wut, KC, mi * 128, xT, xTk, 0, N)
                    a = conv_chunk(pgh, 3, cw_f, m, hist_f[l], hs_f, e)
                    act(tmpa[e % 2][:, 0:N], a[:, 0:N], AF.Silu, [a], [tmpa[e % 2]])
                    tt(hT[:, m, 0:N], tmpa[e % 2][:, 0:N], puh[:, 0:N], OP.mult, [tmpa[e % 2], puh], [hT])
            if sample:
                store(hs_f, ncf_s[l], hs_f[:])

            stage("S8")
            load(pT32, pT32[:, :, 0:N], (psT[l] if sample else ppT[l, :, tok0:tok0 + N]).rearrange(
                "(kc p) n -> p kc n", p=128))
            cp(pT[:, :, 0:N], pT32[:, :, 0:N], [pT32], [pT], eng="dve")
            load(lnp, lnp[:], ln2[l])
            for cg in range(8):
                c0 = cg * 256
                pf = [newps() for _ in range(NB)]
                kgs = ((0, 16), (16, 16), (32, 11))
                for gi, (k0, kn_) in enumerate(kgs):
                    wd_, _, _ = wnext(w_down[l, k0 * 128:(k0 + kn_) * 128, c0:c0 + 256])
                    for b in range(NB):
                        def fn(h, b=b, wd_=wd_, k0=k0, kn_=kn_, gi=gi):
                            ins = None
                            for kc in range(kn_):
                                ins = h.matmul(pf[b][0:R, 0:256], hT[:, k0 + kc, b * R:(b + 1) * R], wd_[:, kc, 0:256],
                                               start=(gi == 0 and kc == 0), stop=(gi == 2 and kc == kn_ - 1))
                            return ins
                        k.op("pe", fn, [hT, wd_], [pf[b]])
                wpg_, _, _ = wnext(w_pg[l, :, c0:c0 + 256])
                wpe_, _, _ = wnext(w_pe[l, :, c0:c0 + 256])
                for b in range(NB):
                    ppg, ppe = newps(), newps()
                    xst(ppg, 0, wpg_, KC, 256, xT, xTk, b)
                    xst(ppe, 0, wpe_, 2, 256, pT, lambda kc: pT[:, kc, :], b)
                    act(sg2[0:R, :], ppg[0:R, 0:256], AF.Sigmoid, [ppg], [sg2])
                    tt(sg2[0:R, :], sg2[0:R, :], ppe[0:R, 0:256], OP.mult, [sg2, ppe], [sg2])
                    tt(sg2[0:R, :], sg2[0:R, :], pf[b][0:R, 0:256], OP.add, [sg2, pf[b]], [sg2])
                    stt(x32[0:R, b, c0:c0 + 256], x32[0:R, b, c0:c0 + 256], ALPHA, sg2[0:R, :], OP.mult, OP.add,
                        [x32, sg2], [x32])
            dump("hT", hT, hT[:])
            norm_and_transpose(last_layer)
            dump("x2", x32, x32[:])
        k.barrier()

    def first_layer_input(tl):
        kind, R, NB, tok0 = tl["kind"], tl["R"], tl["NB"], tl["tok0"]
        with ExitStack() as st0:
            lnp_alloc(st0)
            load(lnp, lnp[:], lnin[:, :, :])
            for b in range(NB):
                src = xs[:, :] if kind == "s" else xp[tok0 + b * 128: tok0 + (b + 1) * 128, :]
                load(x32, x32[0:R, b, :], src)
                layernorm_block(R, b)
                cp(xin16[0:R, :], x32[0:R, b, :], [x32], [xin16], eng="act")
                transpose_block(R, b, b * R, srcbuf=xin16, src_of=lambda kc: xin16[0:R, kc * 128:(kc + 1) * 128])
            k.barrier()

    tiles = [dict(kind="p", R=128, NB=TB, nseq=1, Ls=128, tok0=t * TB * 128) for t in range(NPT)]
    tiles.append(dict(kind="s", R=64, NB=1, nseq=16, Ls=4, tok0=0))

    def emit_all():
        setup_consts()
        stage("consts")
        tl_list = tiles if dbg is None else [tiles[i] for i in dbg["tiles"]]
        layers = range(2) if dbg is None else dbg["layers"]
        for tl in tl_list:
            first_layer_input(tl)
            dump("xT0", xT, xT[:])
            dump("x0", x32, x32[:])
            stage("in")
            for l in layers:
                run_tile_layer(l, tl, False, l == 1)
        for l in range(2):
            store(hist_a[l], nca_p[l], hist_a[l][:])
            store(hist_q[l], ncq_p[l], hist_q[l][:])
            store(hist_f[l], ncf_p[l], hist_f[l][:])
            store(S32[l], nd_p[l].rearrange("h k v -> k h v"), S32[l][:])

    def emit_guard():
        try:
            emit_all()
        except _Stop:
            pass

    k.plan = True
    emit_guard()
    k.plan = False
    k.stopped = False
    psi[0] = 0
    emit_guard()
    k.stopped = False
    k.finish()


_PROG = {}


def _consts():
    def mk(R, Ls):
        idx = np.arange(R)
        seq = idx // Ls
        same = seq[:, None] == seq[None, :]
        c = np.zeros((R, 6, R), np.float32)
        c[:, 0] = np.eye(R)
        low = same & (idx[:, None] > idx[None, :])
        up = same & (idx[:, None] <= idx[None, :])
        c[:, 1] = np.where(low, 0.0, NEG)
        c[:, 2] = np.where(up, 0.0, NEG)
        c[:, 3] = up.astype(np.float32)
        c[:, 4] = up.astype(np.float32)
        c[:, 5] = same.astype(np.float32)
        return c
    smk = (np.arange(64)[:, None] // 4 == np.arange(16)[None, :]).astype(np.float32)
    smT = np.ascontiguousarray(np.broadcast_to(smk.T[None], (128, 16, 64))).astype(np.float32)
    return mk(128, 128), mk(64, 4), smk, smT


def kernel(x_prompt, x_sample, state_conv_a, state_conv_qkv, state_delta, state_conv_ffn,
           p_prompt, p_sample, ln_in_g, ln_in_b,
           w_in, conv_a_w, w_a_out, conv_b_w, a_log, dt_bias, norm_b_g, w_b_out,
           ln_c_g, ln_c_b, w_s, b_s, w_c_out, w_o, ln1_g, ln1_b,
           w_up, conv_f_w, w_down, w_pe, w_pg, ln2_g, ln2_b):
    f = lambda a: np.ascontiguousarray(np.asarray(a, dtype=np.float32))
    x_prompt, x_sample = f(x_prompt), f(x_sample)
    if "nc" not in _PROG:
        _PROG["nc"] = build_program()
    nc = _PROG["nc"]
    cst_p, cst_s, smk, smT = _consts()

    def rep(a):
        a = f(a)
        return np.ascontiguousarray(np.broadcast_to(a[..., None, :], a.shape[:-1] + (128, a.shape[-1])))

    def fm(a, nch):
        a = f(a)
        return np.ascontiguousarray(a.reshape(2, a.shape[1], nch, 128).transpose(0, 3, 2, 1))

    shared = {
        "w_in": f(w_in), "w_a_out": f(w_a_out), "w_b_out": f(w_b_out), "w_c_out": f(w_c_out), "w_o": f(w_o),
        "w_up": f(w_up), "w_down": f(w_down), "w_pe": f(w_pe), "w_pg": f(w_pg),
        "cwa": fm(conv_a_w, 8), "cwq": fm(conv_b_w, 24), "cwf": fm(conv_f_w, FCH),
        "lnin": np.ascontiguousarray(np.stack([rep(ln_in_g), rep(ln_in_b)], axis=1)),
        "ln1": np.ascontiguousarray(np.stack([rep(ln1_g), rep(ln1_b)], axis=2)),
        "ln2": np.ascontiguousarray(np.stack([rep(ln2_g), rep(ln2_b)], axis=2)),
        "lnc": np.ascontiguousarray(np.stack([rep(ln_c_g), rep(ln_c_b)], axis=2)),
        "nbg": rep(norm_b_g), "alog": rep(a_log), "dtb": rep(dt_bias),
        "cst_p": cst_p, "cst_s": cst_s, "smk_s": smk, "smT_s": smT,
    }
    ws = f(w_s)
    bs = f(b_s)
    shared["wsT_p"] = np.ascontiguousarray(ws.transpose(0, 3, 1, 2))
    w4 = ws[:, :, 0:4, 0:4].transpose(0, 3, 1, 2)
    shared["wsT_s"] = np.ascontiguousarray(np.tile(w4, (1, 16, 1, 16)))
    shared["bsb_p"] = np.ascontiguousarray(np.broadcast_to(bs[:, None], (2, 128, 8, 128)))
    shared["bsb_s"] = np.ascontiguousarray(np.broadcast_to(np.tile(bs[:, :, 0:4], (1, 1, 16))[:, None], (2, 128, 8, 64)))

    sca_f, scq_f, scf_f = f(state_conv_a), f(state_conv_qkv), f(state_conv_ffn)
    sdl_f, pp_f, psm_f = f(state_delta), f(p_prompt), f(p_sample)

    def fms(a, nch):
        return np.ascontiguousarray(a.reshape(2, 16, a.shape[2], nch, 128).transpose(0, 4, 3, 1, 2))

    in_maps = []
    for c in range(8):
        s = c % 4
        q0 = 16 * c
        m = dict(shared)
        m["xp"] = np.ascontiguousarray(x_prompt[s])
        m["xs"] = np.ascontiguousarray(x_sample[q0:q0 + 16].reshape(64, D))
        m["ppT"] = np.ascontiguousarray(pp_f[:, s].transpose(0, 2, 1))
        m["psT"] = np.ascontiguousarray(psm_f[:, q0:q0 + 16].reshape(2, 64, DPLE).transpose(0, 2, 1))
        m["sca"] = fms(sca_f[:, q0:q0 + 16], 8)
        m["scq"] = fms(scq_f[:, q0:q0 + 16], 24)
        m["scf"] = fms(scf_f[:, q0:q0 + 16], FCH)
        m["sdl"] = np.ascontiguousarray(sdl_f[:, q0:q0 + 16])
        in_maps.append(m)
    res = run_bass_kernel_spmd(nc, in_maps, core_ids=list(range(8)))
    R = res.results

    def unfm(a):
        return a.transpose(0, 3, 2, 1).reshape(2, a.shape[3], -1)

    def unfms(a):
        return a.transpose(0, 3, 4, 2, 1).reshape(2, 16, a.shape[4], -1)

    y_prompt = np.stack([R[s]["y_p"] for s in range(4)], 0)
    y_sample = np.concatenate([R[c]["y_s"].reshape(16, 4, D) for c in range(8)], 0)
    nca_p = np.stack([unfm(R[s]["nca_p"]) for s in range(4)], 1)
    ncq_p = np.stack([unfm(R[s]["ncq_p"]) for s in range(4)], 1)
    ncf_p = np.stack([unfm(R[s]["ncf_p"]) for s in range(4)], 1)
    nd_p = np.stack([R[s]["nd_p"] for s in range(4)], 1)
    nca_s = np.concatenate([unfms(R[c]["nca_s"]) for c in range(8)], 1)
    ncq_s = np.concatenate([unfms(R[c]["ncq_s"]) for c in range(8)], 1)
    ncf_s = np.concatenate([unfms(R[c]["ncf_s"]) for c in range(8)], 1)
    nd_s = np.concatenate([R[c]["nd_s"] for c in range(8)], 1)
    nv_s = np.concatenate([R[c]["nv_s"].reshape(2, 16, 4, 1024) for c in range(8)], 1)
    outs = (y_prompt, y_sample, nca_p, ncq_p, nd_p, ncf_p, nca_s, ncq_s, nd_s, ncf_s, nv_s)
    return tuple(np.ascontiguousarray(o, dtype=np.float32) for o in outs)
```

```python
import numpy as np
from contextlib import ExitStack
import concourse.bass as bass
import concourse.mybir as mybir
from concourse.bass_utils import run_bass_kernel_spmd

F32 = mybir.dt.float32
BF16 = mybir.dt.bfloat16
AF = mybir.ActivationFunctionType
OP = mybir.AluOpType
AX = mybir.AxisListType

D = 2048
KC = 16
NIN = 15376
DFF = 5504
FCH = 43
DPLE = 256
NH = 8
ALPHA = 4 ** 0.25
LN_EPS = 1e-5
RMS_EPS = 1e-6
O_AH, O_ABG, O_ACG, O_Q, O_K, O_V, O_Z, O_BETA, O_DEC, O_CU, O_CV, O_G = (
    0, 1024, 2048, 3072, 4096, 5120, 6144, 7168, 7176, 7184, 8208, 9232)
TB = 2
NPT = 16 // TB
NSLOT = 6
WCOLS = 256
NEG = -30000.0


class _Stop(Exception):
    pass


class Eng:
    def __init__(self, name, h, sem):
        self.name, self.h, self.sem, self.tick = name, h, sem, 0


class Buf:
    def __init__(self, t, name):
        self.t, self.name = t, name
        self.lastw = None
        self.readers = {}
        self.dsem = None
        self.dcount = 0
        self.is_psum = False

    def __getitem__(self, idx):
        return self.t[idx]


class K:
    def __init__(self, nc, es):
        self.nc, self.es = nc, es
        self.plan = False
        self.stopped = False
        self.waited = {}
        self.engs = {}
        self.dbufs = []
        for name, h in (("pe", nc.tensor), ("act", nc.scalar), ("dve", nc.vector),
                        ("pool", nc.gpsimd), ("sp", nc.sync)):
            self.engs[name] = Eng(name, h, es.enter_context(nc.semaphore("s_" + name)))
        self.nalloc = 0

    def sb(self, shape, dt, name=None, stack=None):
        self.nalloc += 1
        name = (name or "t") + "_%d" % self.nalloc
        t = (stack or self.es).enter_context(self.nc.sbuf_tensor(name, list(shape), dt))
        return Buf(t, name)

    def ps(self, name):
        t = self.es.enter_context(self.nc.psum_tensor(name, [128, 512], F32))
        b = Buf(t, name)
        b.is_psum = True
        return b

    def _wait(self, eng, sem, semkey, val):
        k = (eng.name, semkey)
        if self.waited.get(k, 0) < val:
            eng.h.wait_ge(sem, val)
            self.waited[k] = val

    def _deps(self, eng, reads, writes):
        for b in reads:
            if b.lastw is not None:
                self._dep(eng, b, b.lastw)
            if b.is_psum:
                for e, tk in b.readers.items():
                    if e is not eng and e != 'dma':
                        self._dep(eng, b, ('e', e, tk))
        for b in writes:
            if b.lastw is not None:
                self._dep(eng, b, b.lastw)
            for e, tk in b.readers.items():
                if e == 'dma':
                    self._dep(eng, b, ('d', tk))
                else:
                    self._dep(eng, b, ('e', e, tk))

    def _dep(self, eng, b, d):
        if d[0] == 'd':
            self._wait(eng, b.dsem, "d" + b.name, 16 * d[1])
        else:
            src = d[1]
            if src is eng and eng.name in ("pe", "sp"):
                return
            self._wait(eng, src.sem, src.name, d[2])

    def op(self, en, fn, reads, writes):
        if self.plan or self.stopped:
            return
        eng = self.engs[en]
        self._deps(eng, reads, writes)
        ins = fn(eng.h)
        eng.tick += 1
        ins.then_inc(eng.sem, 1)
        for b in reads:
            b.readers[eng] = eng.tick
        for b in writes:
            b.lastw = ('e', eng, eng.tick)
            b.readers = {}

    def dma(self, en, out_ap, in_ap, buf, is_write):
        if self.plan or self.stopped:
            return
        eng = self.engs[en]
        if buf.dsem is None:
            buf.dsem = self.es.enter_context(self.nc.semaphore("d_" + buf.name))
            self.dbufs.append(buf)
        if is_write:
            self._deps(eng, [], [buf])
        else:
            self._deps(eng, [buf], [])
        ins = eng.h.dma_start(out=out_ap, in_=in_ap)
        ins.then_inc(buf.dsem, 16)
        buf.dcount += 1
        if is_write:
            buf.lastw = ('d', buf.dcount)
            buf.readers = {}
        else:
            buf.readers['dma'] = buf.dcount

    def barrier(self):
        if self.plan or self.stopped:
            return
        es = list(self.engs.values())
        for e in es:
            for s in es:
                if s is not e and s.tick > 0:
                    self._wait(e, s.sem, s.name, s.tick)
            for b in self.dbufs:
                if b.dcount > 0:
                    self._wait(e, b.dsem, "d" + b.name, 16 * b.dcount)

    def finish(self):
        sp = self.engs["sp"]
        for e in self.engs.values():
            if e is not sp and e.tick > 0:
                self._wait(sp, e.sem, e.name, e.tick)
        for b in self.dbufs:
            self._wait(sp, b.dsem, "d" + b.name, 16 * b.dcount)


def build_program(dbg=None):
    nc = bass.Bass("TRN2", target_bir_lowering=False)
    es = ExitStack()
    with es:
        _build(nc, es, dbg)
    return nc


def _build(nc, es, dbg=None):
    def din(name, shape):
        return nc.dram_tensor(name, list(shape), F32, kind="ExternalInput").ap()

    def dout(name, shape):
        return nc.dram_tensor(name, list(shape), F32, kind="ExternalOutput").ap()

    xp = din("xp", [2048, D])
    xs = din("xs", [64, D])
    ppT = din("ppT", [2, DPLE, 2048])
    psT = din("psT", [2, DPLE, 64])
    sca = din("sca", [2, 128, 8, 16, 2])
    scq = din("scq", [2, 128, 24, 16, 3])
    scf = din("scf", [2, 128, FCH, 16, 2])
    sdl = din("sdl", [2, 16, NH, 128, 128])
    w_in = din("w_in", [2, D, NIN])
    w_a_out = din("w_a_out", [2, 1024, D])
    w_b_out = din("w_b_out", [2, 1024, D])
    w_c_out = din("w_c_out", [2, 1024, D])
    w_o = din("w_o", [2, D, D])
    w_up = din("w_up", [2, D, 2 * DFF])
    w_down = din("w_down", [2, DFF, D])
    w_pe = din("w_pe", [2, DPLE, D])
    w_pg = din("w_pg", [2, D, D])
    cwa = din("cwa", [2, 128, 8, 3])
    cwq = din("cwq", [2, 128, 24, 4])
    cwf = din("cwf", [2, 128, FCH, 3])
    lnin = din("lnin", [128, 2, D])
    ln1 = din("ln1", [2, 128, 2, D])
    ln2 = din("ln2", [2, 128, 2, D])
    lnc = din("lnc", [2, 128, 2, 1024])
    nbg = din("nbg", [2, 128, 128])
    alog = din("alog", [2, 128, NH])
    dtb = din("dtb", [2, 128, NH])
    wsT_p = din("wsT_p", [2, 128, 8, 128])
    wsT_s = din("wsT_s", [2, 64, 8, 64])
    bsb_p = din("bsb_p", [2, 128, 8, 128])
    bsb_s = din("bsb_s", [2, 128, 8, 64])
    cst_p = din("cst_p", [128, 6, 128])
    cst_s = din("cst_s", [64, 6, 64])
    smk_s = din("smk_s", [64, 16])
    smT_s = din("smT_s", [128, 16, 64])

    y_p = dout("y_p", [2048, D])
    y_s = dout("y_s", [64, D])
    nca_p = dout("nca_p", [2, 128, 8, 2])
    ncq_p = dout("ncq_p", [2, 128, 24, 3])
    ncf_p = dout("ncf_p", [2, 128, FCH, 2])
    nd_p = dout("nd_p", [2, NH, 128, 128])
    nca_s = dout("nca_s", [2, 128, 8, 16, 2])
    ncq_s = dout("ncq_s", [2, 128, 24, 16, 3])
    ncf_s = dout("ncf_s", [2, 128, FCH, 16, 2])
    nd_s = dout("nd_s", [2, 16, NH, 128, 128])
    nv_s = dout("nv_s", [2, 64, 1024])

    es.enter_context(nc.Block())
    k = K(nc, es)
    NMAX = TB * 128

    wslots = [k.sb([128, KC, WCOLS], BF16, "wslot") for _ in range(NSLOT)]
    wbd = k.sb([128, KC, 16], BF16, "wbd")
    x32 = k.sb([128, TB, D], F32, "x32")
    xT = k.sb([128, KC, NMAX], BF16, "xT")
    lnp = Buf(None, "lnp")

    def lnp_alloc(stack):
        k.nalloc += 1
        lnp.t = stack.enter_context(nc.sbuf_tensor("lnp_%d" % k.nalloc, [128, 2, D], F32))
    cp32 = k.sb([128, 6, 128], F32, "cp32")
    cs32 = k.sb([64, 6, 64], F32, "cs32")
    identb = k.sb([128, 128], BF16, "identb")
    onesf = k.sb([128, 128], F32, "onesf")
    smk = k.sb([64, 16], F32, "smk")
    smT = k.sb([128, 16, 64], BF16, "smT")
    S32 = [k.sb([128, NH, 128], F32, "S32") for _ in range(2)]
    hist_a = [k.sb([128, 8, 2], F32, "hista") for _ in range(2)]
    hist_q = [k.sb([128, 24, 3], F32, "histq") for _ in range(2)]
    hist_f = [k.sb([128, FCH, 2], F32, "histf") for _ in range(2)]
    cw_a = k.sb([128, 8, 3], F32, "cwa")
    cw_q = k.sb([128, 24, 4], F32, "cwq")
    cw_f = k.sb([128, FCH, 3], F32, "cwf")
    NCB = 4
    ext = [k.sb([128, 520], F32, "ext") for _ in range(NCB)]
    acc = [k.sb([128, NMAX], F32, "acc") for _ in range(NCB)]
    tmpa = [k.sb([128, NMAX], F32, "tmpa") for _ in range(2)]
    stat = k.sb([128, 4, 6], F32, "stat")
    mv = k.sb([128, 2], F32, "mv")
    rstd = k.sb([128, 1], F32, "rstd")
    nmr = k.sb([128, 1], F32, "nmr")
    pT = k.sb([128, 2, NMAX], BF16, "pT")
    pT32 = k.sb([128, 2, NMAX], F32, "pT32")
    lncp = k.sb([128, 2, 1024], F32, "lncp")
    wm32 = k.sb([128, 8, 128], F32, "wm32")
    bsb = k.sb([128, 8, 128], F32, "bsb")
    ngb = k.sb([128, 128], F32, "ngb")
    alg = k.sb([128, NH], F32, "alg")
    dtbt = k.sb([128, NH], F32, "dtbt")
    xin16 = k.sb([128, D], BF16, "xin16")
    PS = [k.ps("ps%d" % i) for i in range(8)]
    psi = [0]

    def newps():
        p = PS[psi[0] % 8]
        psi[0] += 1
        return p

    NBLK_MAX = 400
    WSB = 200
    wscr = [[nc.dram_tensor("wscr_%d_%d" % (l_, g_), [WSB, 128, KC * WCOLS], BF16, kind="Internal").ap()
             for g_ in range(NBLK_MAX // WSB)] for l_ in range(2)]
    wspecs = []
    wstate = {"issued": 0, "cons": 0, "l": 0, "j": 0}
    wseen = set()

    def wbegin(l):
        wstate["l"] = l
        wstate["j"] = 0

    def _issue(i):
        src, kcn, ncols, l, j = wspecs[i]
        slot = wslots[i % NSLOT]
        sview = slot[:, 0:kcn, 0:ncols]
        cview = wscr[l][j // WSB][j % WSB, :, 0:kcn * ncols].rearrange("p (kc n) -> p kc n", kc=kcn)
        if (l, j) not in wseen:
            wseen.add((l, j))
            k.dma("pool", sview, src.rearrange("(kc p) n -> p kc n", p=128), slot, True)
            k.dma("sp", cview, sview, slot, False)
        else:
            k.dma("sp", sview, cview, slot, True)

    def wnext(src2d):
        kcn = src2d.shape[0] // 128
        ncols = src2d.shape[1]
        if k.stopped:
            return wslots[0], kcn, ncols
        if k.plan:
            wspecs.append((src2d, kcn, ncols, wstate["l"], wstate["j"]))
            wstate["j"] += 1
            assert wstate["j"] <= NBLK_MAX
            return wslots[0], kcn, ncols
        i = wstate["cons"]
        wstate["cons"] += 1
        assert wspecs[i][1] == kcn and wspecs[i][2] == ncols
        while wstate["issued"] < min(len(wspecs), i + NSLOT - 2) or wstate["issued"] <= i:
            _issue(wstate["issued"])
            wstate["issued"] += 1
        return wslots[i % NSLOT], kcn, ncols

    def mmg(out_buf, out_ap, pairs, reads):
        def fn(h):
            n = len(pairs)
            ins = None
            for i, (l, r) in enumerate(pairs):
                ins = h.matmul(out_ap, l, r, start=(i == 0), stop=(i == n - 1))
            return ins
        k.op("pe", fn, reads, [out_buf])

    def act(out, in_, func, reads, writes, scale=None, bias=None, eng="act"):
        kw = {}
        if scale is not None:
            kw["scale"] = scale
        if bias is not None:
            kw["bias"] = bias
        k.op(eng, lambda h: h.activation(out=out, in_=in_, func=func, **kw), reads, writes)

    def tt(out, a, b, op, reads, writes, eng="dve"):
        k.op(eng, lambda h: h.tensor_tensor(out=out, in0=a, in1=b, op=op), reads, writes)

    def ts(out, a, s1, op0, reads, writes, s2=None, op1=None, eng="dve"):
        if op1 is None:
            k.op(eng, lambda h: h.tensor_scalar(out=out, in0=a, scalar1=s1, scalar2=None, op0=op0), reads, writes)
        else:
            k.op(eng, lambda h: h.tensor_scalar(out=out, in0=a, scalar1=s1, scalar2=s2, op0=op0, op1=op1),
                 reads, writes)

    def stt(out, a, s, b, op0, op1, reads, writes):
        k.op("dve", lambda h: h.scalar_tensor_tensor(out=out, in0=a, scalar=s, in1=b, op0=op0, op1=op1),
             reads, writes)

    def cp(out, in_, reads, writes, eng="act"):
        if eng == "act":
            k.op("act", lambda h: h.activation(out=out, in_=in_, func=AF.Copy), reads, writes)
        else:
            k.op(eng, lambda h: h.tensor_copy(out=out, in_=in_), reads, writes)

    def memset(buf, ap, val):
        k.op("dve", lambda h: h.memset(ap, val), [], [buf])

    def load(buf, out_ap, in_ap, eng="sp"):
        k.dma(eng, out_ap, in_ap, buf, True)

    def store(buf, out_ap, in_ap, eng="sp"):
        k.dma(eng, out_ap, in_ap, buf, False)

    def stage(name):
        if dbg is not None and dbg.get("stop") == name:
            k.stopped = True

    dcount = [0]

    def dump(name, buf, ap):
        if dbg is None or k.plan or name not in dbg.get("dumps", ()):
            return
        dcount[0] += 1
        dt = nc.dram_tensor("dbg_%s_%d" % (name, dcount[0]), list(ap.shape), ap.dtype, kind="ExternalOutput").ap()
        store(buf, dt, ap)

    def setup_consts():
        load(cp32, cp32[:], cst_p[:, :, :])
        load(cs32, cs32[:], cst_s[:, :, :])
        load(smk, smk[:], smk_s[:, :])
        load(smT, smT[:], smT_s[:, :, :], eng="pool")
        cp(identb[:], cp32[:, 0, :], [cp32], [identb], eng="dve")
        memset(onesf, onesf[:], 1.0)
        for l in range(2):
            memset(S32[l], S32[l][:], 0.0)
            memset(hist_a[l], hist_a[l][:], 0.0)
            memset(hist_q[l], hist_q[l][:], 0.0)
            memset(hist_f[l], hist_f[l][:], 0.0)

    def layernorm_block(R, b, width=D, src=None, gb=None):
        src = src or x32
        gb = gb or lnp
        xin = src[0:R, b, 0:width]
        nch = width // 512
        for c in range(nch):
            k.op("dve", lambda h, c=c: h.bn_stats(out=stat[0:R, c, :], in_=src[0:R, b, c * 512:(c + 1) * 512]),
                 [src], [stat])
        k.op("dve", lambda h: h.bn_aggr(out=mv[0:R, :], in_=stat[0:R, 0:nch, :]), [stat], [mv])
        ts(rstd[0:R, :], mv[0:R, 1:2], LN_EPS, OP.add, [mv], [rstd])
        act(rstd[0:R, :], rstd[0:R, :], AF.Sqrt, [rstd], [rstd])
        k.op("dve", lambda h: h.reciprocal(out=rstd[0:R, :], in_=rstd[0:R, :]), [rstd], [rstd])
        stt(nmr[0:R, :], mv[0:R, 0:1], -1.0, rstd[0:R, :], OP.mult, OP.mult, [mv, rstd], [nmr])
        act(xin, xin, AF.Identity, [src, rstd, nmr], [src], scale=rstd[0:R, 0:1], bias=nmr[0:R, 0:1])
        tt(xin, xin, gb[0:R, 0, 0:width], OP.mult, [src, gb], [src])
        tt(xin, xin, gb[0:R, 1, 0:width], OP.add, [src, gb], [src])

    def transpose_block(R, b, c0, srcbuf=None, src_of=None):
        for g4 in range(KC // 4):
            p = newps()
            for j in range(4):
                kc = g4 * 4 + j
                l = src_of(kc)
                mmg(p, p[:, j * R:(j + 1) * R], [(l, identb[0:R, 0:R])], [srcbuf, identb])
            cp(xT[:, g4 * 4:(g4 + 1) * 4, c0:c0 + R],
               p[:, 0:4 * R].rearrange("p (a r) -> p a r", a=4), [p], [xT],
               eng=("act" if g4 % 2 == 0 else "dve"))

    def run_tile_layer(l, tl, first_layer, last_layer):
        with ExitStack() as tls:
            _rtl(l, tl, first_layer, last_layer, tls)

    def _rtl(l, tl, first_layer, last_layer, tls):
        kind, R, NB, nseq, Ls, tok0 = tl["kind"], tl["R"], tl["NB"], tl["nseq"], tl["Ls"], tl["tok0"]
        N = R * NB
        sample = kind == "s"
        cst = cs32 if sample else cp32
        identR = cst[0:R, 0, 0:R]
        maskLow = cst[0:R, 1, 0:R]
        maskUp = cst[0:R, 2, 0:R]
        maskC = cst[0:R, 3, 0:R]
        triC = cst[0:R, 4, 0:R]
        totC = cst[0:R, 5, 0:R]
        nlev = {128: 6, 4: 1}[Ls]

        def wst(psb, wb, kcn, c0, rbuf, rhs_of, n0, n1):
            mmg(psb, psb[:, 0:n1 - n0], [(wb[:, kc, c0:c0 + 128], rhs_of(kc)[:, n0:n1]) for kc in range(kcn)],
                [wb, rbuf])

        def xst(psb, pcol, wb, kcn, ncols, lbuf, lhs_of, b):
            mmg(psb, psb[0:R, pcol:pcol + ncols],
                [(lhs_of(kc)[:, b * R:(b + 1) * R], wb[:, kc, 0:ncols]) for kc in range(kcn)], [wb, lbuf])

        xTk = lambda kc: xT[:, kc, :]

        def conv_chunk(psb, W, cw, m, hist_buf, hs_buf, e):
            Wm1 = W - 1
            Lc = Ls if sample else N
            ex = ext[e]
            exv = ex[:, 0:nseq * (Wm1 + Lc)].rearrange("p (q t) -> p q t", q=nseq)
            pv = psb[:, 0:N].rearrange("p (q t) -> p q t", q=nseq)
            cp(exv[:, :, Wm1:], pv, [psb], [ex], eng="act")
            if sample:
                cp(exv[:, :, 0:Wm1], hs_buf[:, m, :, :], [hs_buf], [ex], eng="dve")
            else:
                cp(exv[:, 0, 0:Wm1], hist_buf[:, m, :], [hist_buf], [ex], eng="dve")
            a = acc[e]
            av = a[:, 0:N].rearrange("p (q t) -> p q t", q=nseq)
            act(av, exv[:, :, 0:Lc], AF.Copy, [ex, cw], [a], scale=cw[:, m, 0:1])
            for j in range(1, W):
                stt(av, exv[:, :, j:j + Lc], cw[:, m, j:j + 1], av, OP.mult, OP.add, [ex, cw, a], [a])
            if sample:
                cp(hs_buf[:, m, :, :], exv[:, :, Lc:Lc + Wm1], [ex], [hs_buf], eng="dve")
            else:
                cp(hist_buf[:, m, :], exv[:, 0, Lc:Lc + Wm1], [ex], [hist_buf], eng="dve")
            return a

        wbegin(l)
        hs_a = hs_q = hs_f = None
        if sample:
            hs_a = k.sb([128, 8, 16, 2], F32, "hsa", tls)
            hs_q = k.sb([128, 24, 16, 3], F32, "hsq", tls)
            hs_f = k.sb([128, FCH, 16, 2], F32, "hsf", tls)
        load(cw_a, cw_a[:], cwa[l])
        load(cw_q, cw_q[:], cwq[l])
        load(cw_f, cw_f[:], cwf[l])
        if sample:
            load(hs_a, hs_a[:], sca[l])
            load(hs_q, hs_q[:], scq[l])
            load(hs_f, hs_f[:], scf[l])

        phm = ExitStack()
        NA = N
        with phm:
            mA = k.sb([128, 8, NA], BF16, "mA", phm)
            mB = k.sb([128, 8, NA], BF16, "mB", phm)
            mC = k.sb([128, 8, NA], BF16, "mC", phm)
            mg = k.sb([128, NB, D], BF16, "mg", phm)
            mgt = k.sb([128, NB, WCOLS], F32, "mgt", phm)
            sg = k.sb([128, WCOLS], F32, "sg", phm)
            uT = k.sb([128, 8, NA], BF16, "uT", phm)
            cvall = k.sb([128, NB, 1024], F32, "cvall", phm)
            vcb = k.sb([128, NB, 1024], BF16, "vcb", phm)
            wmT = k.sb([128, 8, 128], BF16, "wmT", phm)
            qkvT = k.sb([128, 3, NA], BF16, "qkvT", phm)
            qkvTb = k.sb([128, 3, NA], BF16, "qkvTb", phm)
            bd = k.sb([128, TB, 16], F32, "bd", phm)
            zs = k.sb([128, NB, 128], BF16, "zs", phm)
            zsb = k.sb([128, NB, 128], BF16, "zsb", phm)
            sc = {n: k.sb([128, TB, NH], F32, n, phm) for n in
                  ("beta", "g", "gcum", "gtot", "egc", "ekd", "nbgc", "t8a", "t8b")}
            eglb = k.sb([128, TB, 16 * NH], F32, "eglb", phm)
            gsel = k.sb([128, 16 * NH], F32, "gsel", phm)
            def _mkT():
                d = dict(
                    sq=k.sb([128, 2, 128], F32, "sq", phm),
                    nT=k.sb([128, 2, 128], BF16, "nT", phm),
                    t1=k.sb([128, 128], F32, "t1", phm), t2=k.sb([128, 128], F32, "t2", phm),
                    dup=k.sb([128, 128], F32, "dup", phm), dlow=k.sb([128, 128], F32, "dlow", phm),
                    Am=[k.sb([128, 128], F32, "Am", phm) for _ in range(2)],
                    At=[k.sb([128, 128], F32, "At", phm) for _ in range(2)],
                    qkT=k.sb([128, 128], BF16, "qkT", phm), Pt32=k.sb([128, 128], F32, "Pt32", phm),
                    Ptb=k.sb([128, 128], BF16, "Ptb", phm), vb=k.sb([128, 128], F32, "vb", phm),
                    kd=k.sb([128, 128], BF16, "kd", phm))
                d["rn"] = d["sq"]
                d["Dg"] = d["t2"]
                return d
            TS2 = [[_mkT() for _b in range(NB)] for _p in range(2)]
            TS = TS2[0]
            t1 = TS[0]["t1"]
            dd = k.sb([128, 128], BF16, "dd", phm)
            vnew = k.sb([128, 128], BF16, "vnew", phm)
            to = k.sb([128, 128], F32, "to", phm)
            oh = k.sb([128, 128], F32, "oh", phm)
            og = k.sb([128, 128], BF16, "og", phm)
            ngz = k.sb([128, 128], F32, "ngz", phm)
            ss = k.sb([128, 1], F32, "ss", phm)
            Sb = k.sb([128, 128], BF16, "Sb", phm)
            if sample:
                S32s = k.sb([128, 16, 128], F32, "S32s", phm)
                Sbs = k.sb([128, 16, 128], BF16, "Sbs", phm)
                knTm = k.sb([128, 16, 64], BF16, "knTm", phm)
                qnTm = k.sb([128, 16, 64], BF16, "qnTm", phm)
                kdm = k.sb([64, 16, 128], BF16, "kdm", phm)
                vc32s = k.sb([64, 1024], F32, "vc32s", phm)

            load(lncp, lncp[:], lnc[l])
            load(ngb, ngb[:], nbg[l])
            load(alg, alg[:], alog[l])
            load(dtbt, dtbt[:], dtb[l])
            if sample:
                load(wm32, wm32[0:64, :, 0:64], wsT_s[l])
                load(bsb, bsb[:, :, 0:64], bsb_s[l])
            else:
                load(wm32, wm32[:], wsT_p[l])
                load(bsb, bsb[:], bsb_p[l])
            tt(wmT[0:R, :, 0:R], wm32[0:R, :, 0:R], maskC.unsqueeze(1).broadcast_to([R, 8, R]), OP.mult,
               [wm32, cst], [wmT])
            act(alg[:], alg[:], AF.Exp, [alg], [alg])
            ts(alg[:], alg[:], -1.0, OP.mult, [alg], [alg])

            for mp in range(4):
                wh, _, _ = wnext(w_in[l, :, O_AH + mp * 256: O_AH + (mp + 1) * 256])
                wg, _, _ = wnext(w_in[l, :, O_ABG + mp * 256: O_ABG + (mp + 1) * 256])
                wc, _, _ = wnext(w_in[l, :, O_ACG + mp * 256: O_ACG + (mp + 1) * 256])
                for mi in range(2):
                    m = mp * 2 + mi
                    e = m % NCB
                    ph, pg, pc = newps(), newps(), newps()
                    wst(ph, wh, KC, mi * 128, xT, xTk, 0, N)
                    wst(pg, wg, KC, mi * 128, xT, xTk, 0, N)
                    wst(pc, wc, KC, mi * 128, xT, xTk, 0, N)
                    cp(tmpa[e % 2][:, 0:N], ph[:, 0:N], [ph], [tmpa[e % 2]], eng="act")
                    tt(pc[:, 0:N], tmpa[e % 2][:, 0:N], pc[:, 0:N], OP.mult, [tmpa[e % 2], pc], [pc])
                    a = conv_chunk(pc, 3, cw_a, m, hist_a[l], hs_a, e)
                    tt(mA[:, m, 0:N], a[:, 0:N], pg[:, 0:N], OP.mult, [a, pg], [mA])
            dump("mA", mA, mA[:])
            stage("S1")
            if sample:
                store(hs_a, nca_s[l], hs_a[:])

            for mp in range(4):
                wu, _, _ = wnext(w_in[l, :, O_CU + mp * 256: O_CU + (mp + 1) * 256])
                for mi in range(2):
                    m = mp * 2 + mi
                    pu = newps()
                    wst(pu, wu, KC, mi * 128, xT, xTk, 0, N)
                    act(uT[:, m, 0:N], pu[:, 0:N], AF.Gelu_apprx_tanh, [pu], [uT])
            stage("S4a")
            for cgp in range(4):
                wv, _, _ = wnext(w_in[l, :, O_CV + cgp * 256: O_CV + (cgp + 1) * 256])
                for b in range(NB):
                    pv = newps()
                    xst(pv, 0, wv, KC, 256, xT, xTk, b)
                    act(cvall[0:R, b, cgp * 256:(cgp + 1) * 256], pv[0:R, 0:256], AF.Gelu_apprx_tanh, [pv], [cvall])
            for b in range(NB):
                for c in range(2):
                    k.op("dve", lambda h, c=c, b=b: h.bn_stats(out=stat[0:R, c, :], in_=cvall[0:R, b, c * 512:(c + 1) * 512]),
                         [cvall], [stat])
                k.op("dve", lambda h: h.bn_aggr(out=mv[0:R, :], in_=stat[0:R, 0:2, :]), [stat], [mv])
                ts(rstd[0:R, :], mv[0:R, 1:2], LN_EPS, OP.add, [mv], [rstd])
                act(rstd[0:R, :], rstd[0:R, :], AF.Sqrt, [rstd], [rstd])
                k.op("dve", lambda h: h.reciprocal(out=rstd[0:R, :], in_=rstd[0:R, :]), [rstd], [rstd])
                ts(cvall[0:R, b, :], cvall[0:R, b, :], mv[0:R, 0:1], OP.subtract, [cvall, mv, rstd], [cvall],
                   s2=rstd[0:R, 0:1], op1=OP.mult)
                tt(cvall[0:R, b, :], cvall[0:R, b, :], lncp[0:R, 0, :], OP.mult, [cvall, lncp], [cvall])
                if sample:
                    tt(vc32s[:, :], cvall[0:R, b, :], lncp[0:R, 1, :], OP.add, [cvall, lncp], [vc32s])
                    store(vc32s, nv_s[l], vc32s[:, :])
                    cp(vcb[0:R, b, :], vc32s[:, :], [vc32s], [vcb], eng="act")
                else:
                    tt(vcb[0:R, b, :], cvall[0:R, b, :], lncp[0:R, 1, :], OP.add, [cvall, lncp], [vcb])
                for gg in range(8):
                    pm = newps()
                    mmg(pm, pm[:, 0:R], [(vcb[0:R, b, gg * 128:(gg + 1) * 128], wmT[0:R, gg, 0:R])], [vcb, wmT])
                    tt(t1[:, 0:R], pm[:, 0:R], bsb[:, gg, 0:R], OP.add, [pm, bsb], [t1])
                    tt(mC[:, gg, b * R:(b + 1) * R], t1[:, 0:R], uT[:, gg, b * R:(b + 1) * R], OP.mult,
                       [t1, uT], [mC])

            dump("mC", mC, mC[:])
            stage("S4")
            dump("uT", uT, uT[:])
            dump("vcb", vcb, vcb[:])
            load_bd = w_in[l, :, O_BETA:O_BETA + 16]
            k.dma("pool", wbd[:, :, :], load_bd.rearrange("(kc p) n -> p kc n", p=128), wbd, True)
            for b in range(NB):
                pb = newps()
                xst(pb, 0, wbd, KC, 16, xT, xTk, b)
                cp(bd[0:R, b, :], pb[0:R, 0:16], [pb], [bd], eng="act")
                act(sc["beta"][0:R, b, :], bd[0:R, b, 0:8], AF.Sigmoid, [bd], [sc["beta"]])
                tt(sc["t8a"][0:R, b, :], bd[0:R, b, 8:16], dtbt[0:R, :], OP.add, [bd, dtbt], [sc["t8a"]])
                act(sc["t8b"][0:R, b, :], sc["t8a"][0:R, b, :], AF.Abs, [sc["t8a"]], [sc["t8b"]])
                act(sc["t8b"][0:R, b, :], sc["t8b"][0:R, b, :], AF.Exp, [sc["t8b"]], [sc["t8b"]], scale=-1.0)
                ts(sc["t8b"][0:R, b, :], sc["t8b"][0:R, b, :], 1.0, OP.add, [sc["t8b"]], [sc["t8b"]])
                act(sc["t8b"][0:R, b, :], sc["t8b"][0:R, b, :], AF.Ln, [sc["t8b"]], [sc["t8b"]])
                ts(sc["t8a"][0:R, b, :], sc["t8a"][0:R, b, :], 0.0, OP.max, [sc["t8a"]], [sc["t8a"]])
                tt(sc["t8a"][0:R, b, :], sc["t8a"][0:R, b, :], sc["t8b"][0:R, b, :], OP.add,
                   [sc["t8a"], sc["t8b"]], [sc["t8a"]])
                tt(sc["g"][0:R, b, :], sc["t8a"][0:R, b, :], alg[0:R, :], OP.mult, [sc["t8a"], alg], [sc["g"]])
                pgc = newps()
                mmg(pgc, pgc[0:R, 0:8], [(triC, sc["g"][0:R, b, :])], [cst, sc["g"]])
                mmg(pgc, pgc[0:R, 8:16], [(totC, sc["g"][0:R, b, :])], [cst, sc["g"]])
                cp(sc["gcum"][0:R, b, :], pgc[0:R, 0:8], [pgc], [sc["gcum"]], eng="act")
                cp(sc["gtot"][0:R, b, :], pgc[0:R, 8:16], [pgc], [sc["gtot"]], eng="act")
                act(sc["egc"][0:R, b, :], sc["gcum"][0:R, b, :], AF.Exp, [sc["gcum"]], [sc["egc"]])
                tt(sc["ekd"][0:R, b, :], sc["gtot"][0:R, b, :], sc["gcum"][0:R, b, :], OP.subtract,
                   [sc["gtot"], sc["gcum"]], [sc["ekd"]])
                act(sc["ekd"][0:R, b, :], sc["ekd"][0:R, b, :], AF.Exp, [sc["ekd"]], [sc["ekd"]])
                tt(sc["nbgc"][0:R, b, :], sc["beta"][0:R, b, :], sc["egc"][0:R, b, :], OP.mult,
                   [sc["beta"], sc["egc"]], [sc["nbgc"]])
                ts(sc["nbgc"][0:R, b, :], sc["nbgc"][0:R, b, :], -1.0, OP.mult, [sc["nbgc"]], [sc["nbgc"]])
                if sample:
                    tt(gsel[0:R, :].rearrange("p (q h) -> p q h", q=16),
                       sc["g"][0:R, b, :].unsqueeze(1).broadcast_to([R, 16, NH]),
                       smk[0:R, :].unsqueeze(2).broadcast_to([R, 16, NH]), OP.mult, [sc["g"], smk], [gsel])
                    nq = 16
                else:
                    cp(gsel[0:R, 0:NH], sc["g"][0:R, b, :], [sc["g"]], [gsel], eng="dve")
                    nq = 1
                pe_ = newps()
                mmg(pe_, pe_[:, 0:nq * NH], [(onesf[0:R, :], gsel[0:R, 0:nq * NH])], [onesf, gsel])
                act(eglb[:, b, 0:nq * NH], pe_[:, 0:nq * NH], AF.Exp, [pe_], [eglb])

            stage("SBs")
            qkvT2 = [qkvT, qkvTb]
            zs2 = [zs, zsb]

            def gen_proj(h, qkvT, zs):
                for j, (off, mch) in enumerate(((O_K, 8 + h), (O_Q, h), (O_V, 16 + h))):
                    wj, _, _ = wnext(w_in[l, :, off + h * 128: off + (h + 1) * 128])
                    pj = newps()
                    wst(pj, wj, KC, 0, xT, xTk, 0, N)
                    a = conv_chunk(pj, 4, cw_q, mch, hist_q[l], hs_q, (3 * h + j) % NCB)
                    act(qkvT[:, j, 0:N], a[:, 0:N], AF.Silu, [a], [qkvT])
                    yield
                wz, _, _ = wnext(w_in[l, :, O_Z + h * 128: O_Z + (h + 1) * 128])
                for b in range(NB):
                    pz = newps()
                    xst(pz, 0, wz, KC, 128, xT, xTk, b)
                    act(zs[0:R, b, :], pz[0:R, 0:128], AF.Silu, [pz], [zs])
                    yield

            def prep_chain(b, T, h, qkvT):
                cs = slice(b * R, (b + 1) * R)
                beta_h = sc["beta"][0:R, b, h:h + 1]
                gcum_h = sc["gcum"][0:R, b, h:h + 1]
                sq, rn, nT, Dg, t1, t2, dup, dlow = (T[n] for n in ("sq", "rn", "nT", "Dg", "t1", "t2", "dup", "dlow"))
                Am, At, qkT, Pt32, Ptb, vb, kd = (T[n] for n in ("Am", "At", "qkT", "Pt32", "Ptb", "vb", "kd"))
                act(sq[:, :, 0:R], qkvT[:, 0:2, cs], AF.Square, [qkvT], [sq])
                ts(Dg[0:R, 0:R], identR, gcum_h, OP.mult, [cst, sc["gcum"]], [Dg])
                yield
                pn = newps()
                mmg(pn, pn[:, 0:2 * R].rearrange("p (a r) -> p a r", a=2),
                    [(onesf[:, :], sq[:, :, 0:R])], [onesf, sq])
                pr = newps()
                mmg(pr, pr[0:R, 0:R], [(onesf[0:R, 0:R], Dg[0:R, 0:R])], [onesf, Dg])
                ts(rn[:, :, 0:R], pn[:, 0:2 * R].rearrange("p (a r) -> p a r", a=2), RMS_EPS, OP.add,
                   [pn], [rn])
                ts(t1[0:R, 0:R], pr[0:R, 0:R], gcum_h, OP.subtract, [pr, sc["gcum"]], [t1])
                yield
                act(rn[:, :, 0:R], rn[:, :, 0:R], AF.Sqrt, [rn], [rn])
                stt(t2[0:R, 0:R], t1[0:R, 0:R], 0.0, maskUp, OP.min, OP.add, [t1, cst], [t2])
                yield
                k.op("dve", lambda hh: hh.reciprocal(out=rn[:, :, 0:R], in_=rn[:, :, 0:R]), [rn], [rn])
                act(dup[0:R, 0:R], t2[0:R, 0:R], AF.Exp, [t2], [dup])
                yield
                ts(rn[:, 1, 0:R], rn[:, 1, 0:R], 128 ** -0.5, OP.mult, [rn], [rn])
                stt(t2[0:R, 0:R], t1[0:R, 0:R], 0.0, maskLow, OP.max, OP.subtract, [t1, cst], [t2])
                yield
                tt(nT[:, :, 0:R], qkvT[:, 0:2, cs], rn[:, :, 0:R], OP.mult, [qkvT, rn], [nT])
                act(dlow[0:R, 0:R], t2[0:R, 0:R], AF.Exp, [t2], [dlow], scale=-1.0)
                yield
                pkv = newps()
                mmg(pkv, pkv[0:R, 0:128], [(nT[:, 0, 0:R], identb[:, :])], [nT, identb])
                mmg(pkv, pkv[0:R, 128:256], [(qkvT[:, 2, cs], identb[:, :])], [qkvT, identb])
                pg_ = newps()
                mmg(pg_, pg_[0:R, 0:2 * R].rearrange("p (a r) -> p a r", a=2),
                    [(nT[:, 0, 0:R], nT[:, :, 0:R])], [nT])
                stt(Am[0][0:R, 0:R], pg_[0:R, 0:R], beta_h, dlow[0:R, 0:R], OP.mult, OP.mult,
                    [pg_, sc["beta"], dlow], [Am[0]])
                ts(kd[0:R, :], pkv[0:R, 0:128], sc["ekd"][0:R, b, h:h + 1], OP.mult, [pkv, sc["ekd"]], [kd])
                tt(qkT[0:R, 0:R], pg_[0:R, R:2 * R], dup[0:R, 0:R], OP.mult, [pg_, dup], [qkT])
                ts(vb[0:R, :], pkv[0:R, 128:256], beta_h, OP.mult, [pkv, sc["beta"]], [vb])
                yield
                pt = newps()
                mmg(pt, pt[0:R, 0:R], [(Am[0][0:R, 0:R], identR)], [Am[0], cst])
                cp(At[0][0:R, 0:R], pt[0:R, 0:R], [pt], [At[0]], eng="act")
                tt(Pt32[0:R, 0:R], identR, pt[0:R, 0:R], OP.subtract, [cst, pt], [Pt32])
                yield
                cur = 0
                for i in range(1, nlev + 2):
                    nx = 1 - cur
                    p1 = p2 = p3 = None
                    if i <= nlev:
                        p1 = newps()
                        mmg(p1, p1[0:R, 0:R], [(At[cur][0:R, 0:R], Am[cur][0:R, 0:R])], [At[cur], Am[cur]])
                        if i < nlev:
                            p2 = newps()
                            mmg(p2, p2[0:R, 0:R], [(Am[cur][0:R, 0:R], At[cur][0:R, 0:R])], [At[cur], Am[cur]])
                    if i >= 2:
                        p3 = newps()
                        mmg(p3, p3[0:R, 0:R], [(Am[cur][0:R, 0:R], Pt32[0:R, 0:R])], [Am[cur], Pt32])
                    if p1 is not None:
                        cp(Am[nx][0:R, 0:R], p1[0:R, 0:R], [p1], [Am[nx]], eng="act")
                    if p2 is not None:
                        cp(At[nx][0:R, 0:R], p2[0:R, 0:R], [p2], [At[nx]], eng="act")
                    if p3 is not None:
                        tt(Pt32[0:R, 0:R], Pt32[0:R, 0:R], p3[0:R, 0:R], OP.add, [Pt32, p3], [Pt32])
                    yield
                    cur = nx
                cp(Ptb[0:R, 0:R], Pt32[0:R, 0:R], [Pt32], [Ptb], eng="act")


            def gen_prep(h, qkvT, TSp):
                gens = [prep_chain(b, TSp[b], h, qkvT) for b in range(NB)]
                while gens:
                    for g_ in list(gens):
                        try:
                            next(g_)
                        except StopIteration:
                            gens.remove(g_)
                    yield

            def gen_chain(*gs):
                for g_ in gs:
                    for _ in g_:
                        yield

            def gen_scan(h, zs, TS):
              if sample:
                load(S32s, S32s[:], sdl[l, :, h].rearrange("q k v -> k q v"))
                cp(Sbs[:], S32s[:], [S32s], [Sbs], eng="act")
              for b in range(NB):
                cs = slice(b * R, (b + 1) * R)
                T = TS[b]
                nT, qkT, Ptb, vb, kd = T["nT"], T["qkT"], T["Ptb"], T["vb"], T["kd"]
                pk_, pq_ = newps(), newps()
                if sample:
                    tt(knTm[:, :, :], nT[:, 0, 0:R].unsqueeze(1).broadcast_to([128, 16, R]), smT[:, :, :],
                       OP.mult, [nT, smT], [knTm])
                    tt(qnTm[:, :, :], nT[:, 1, 0:R].unsqueeze(1).broadcast_to([128, 16, R]), smT[:, :, :],
                       OP.mult, [nT, smT], [qnTm])
                    mmg(pk_, pk_[0:R, 0:128], [(knTm[:, q, :], Sbs[:, q, :]) for q in range(16)], [knTm, Sbs])
                    mmg(pq_, pq_[0:R, 0:128], [(qnTm[:, q, :], Sbs[:, q, :]) for q in range(16)], [qnTm, Sbs])
                else:
                    cp(Sb[:, :], S32[l][:, h, :], [S32[l]], [Sb], eng="act")
                    mmg(pk_, pk_[0:R, 0:128], [(nT[:, 0, 0:R], Sb[:, :])], [nT, Sb])
                    mmg(pq_, pq_[0:R, 0:128], [(nT[:, 1, 0:R], Sb[:, :])], [nT, Sb])
                stt(dd[0:R, :], pk_[0:R, 0:128], sc["nbgc"][0:R, b, h:h + 1], vb[0:R, :], OP.mult, OP.add,
                    [pk_, sc["nbgc"], vb], [dd])
                ts(to[0:R, :], pq_[0:R, 0:128], sc["egc"][0:R, b, h:h + 1], OP.mult, [pq_, sc["egc"]], [to])
                yield
                pv_ = newps()
                mmg(pv_, pv_[0:R, 0:128], [(Ptb[0:R, 0:R], dd[0:R, :])], [Ptb, dd])
                cp(vnew[0:R, :], pv_[0:R, 0:128], [pv_], [vnew], eng="act")
                yield
                po = newps()
                mmg(po, po[0:R, 0:128], [(qkT[0:R, 0:R], vnew[0:R, :])], [qkT, vnew])
                tt(oh[0:R, :], to[0:R, :], po[0:R, 0:128], OP.add, [to, po], [oh])
                if sample:
                    tt(kdm[:, :, :], kd[0:R, :].unsqueeze(1).broadcast_to([R, 16, 128]),
                       smk[0:R, :].unsqueeze(2).broadcast_to([R, 16, 128]), OP.mult, [kd, smk], [kdm])
                    for q in range(16):
                        pss = newps()
                        mmg(pss, pss[:, 0:128], [(kdm[:, q, :], vnew[0:R, :])], [kdm, vnew])
                        stt(S32s[:, q, :], S32s[:, q, :], eglb[:, b, q * NH + h: q * NH + h + 1], pss[:, 0:128],
                            OP.mult, OP.add, [S32s, eglb, pss], [S32s])
                    store(S32s, nd_s[l, :, h].rearrange("q k v -> k q v"), S32s[:])
                else:
                    pss = newps()
                    mmg(pss, pss[:, 0:128], [(kd[0:R, :], vnew[0:R, :])], [kd, vnew])
                    stt(S32[l][:, h, :], S32[l][:, h, :], eglb[:, b, h:h + 1], pss[:, 0:128], OP.mult, OP.add,
                        [S32[l], eglb, pss], [S32[l]])
                yield
                tt(to[0:R, :], oh[0:R, :], oh[0:R, :], OP.mult, [oh], [to])
                k.op("dve", lambda hh: hh.tensor_reduce(out=ss[0:R, :], in_=to[0:R, :], axis=AX.X, op=OP.add),
                     [to], [ss])
                ts(ss[0:R, :], ss[0:R, :], 1.0 / 128, OP.mult, [ss], [ss], s2=RMS_EPS, op1=OP.add)
                yield
                act(ss[0:R, :], ss[0:R, :], AF.Sqrt, [ss], [ss])
                k.op("dve", lambda hh: hh.reciprocal(out=ss[0:R, :], in_=ss[0:R, :]), [ss], [ss])
                tt(ngz[0:R, :], zs[0:R, b, :], ngb[0:R, :], OP.mult, [zs, ngb], [ngz])
                stt(og[0:R, :], oh[0:R, :], ss[0:R, 0:1], ngz[0:R, :], OP.mult, OP.mult, [oh, ss, ngz], [og])
                yield
                pob = newps()
                mmg(pob, pob[:, 0:R], [(og[0:R, :], identb[0:R, 0:R])], [og, identb])
                cp(mB[:, h, cs], pob[:, 0:R], [pob], [mB], eng="act")

            for _ in gen_chain(gen_proj(0, qkvT2[0], zs2[0]), gen_prep(0, qkvT2[0], TS2[0])):
                pass
            for h in range(NH):
                par = h % 2
                gl_ = [gen_scan(h, zs2[par], TS2[par])]
                if h + 1 < NH:
                    gl_.append(gen_chain(gen_proj(h + 1, qkvT2[1 - par], zs2[1 - par]),
                                         gen_prep(h + 1, qkvT2[1 - par], TS2[1 - par])))
                while gl_:
                    for g_ in list(gl_):
                        try:
                            next(g_)
                        except StopIteration:
                            gl_.remove(g_)
            if sample:
                store(hs_q, ncq_s[l], hs_q[:])

            dump("mB", mB, mB[:])
            stage("SB")
            for cg in range(8):
                c0 = cg * 256
                for br, (wout, mbuf) in enumerate(((w_a_out, mA), (w_b_out, mB), (w_c_out, mC))):
                    wo_, _, _ = wnext(wout[l, :, c0:c0 + 256])
                    wg_, _, _ = wnext(w_in[l, :, O_G + br * D + c0: O_G + br * D + c0 + 256])
                    for b in range(NB):
                        po_, pg2 = newps(), newps()
                        xst(po_, 0, wo_, 8, 256, mbuf, lambda kc, mbuf=mbuf: mbuf[:, kc, :], b)
                        xst(pg2, 0, wg_, KC, 256, xT, xTk, b)
                        act(sg[0:R, :], pg2[0:R, 0:256], AF.Sigmoid, [pg2], [sg])
                        if br == 0:
                            tt(mgt[0:R, b, :], sg[0:R, :], po_[0:R, 0:256], OP.mult, [sg, po_], [mgt])
                        else:
                            tt(sg[0:R, :], sg[0:R, :], po_[0:R, 0:256], OP.mult, [sg, po_], [sg])
                            if br == 1:
                                tt(mgt[0:R, b, :], mgt[0:R, b, :], sg[0:R, :], OP.add, [mgt, sg], [mgt])
                            else:
                                tt(mg[0:R, b, c0:c0 + 256], mgt[0:R, b, :], sg[0:R, :], OP.add, [mgt, sg], [mg])
            dump("mg", mg, mg[:])
            stage("S6")
            for b in range(NB):
                transpose_block(R, b, b * R, srcbuf=mg, src_of=lambda kc, b=b: mg[0:R, b, kc * 128:(kc + 1) * 128])
        k.barrier()

        for cg in range(8):
            c0 = cg * 256
            wo_, _, _ = wnext(w_o[l, :, c0:c0 + 256])
            for b in range(NB):
                py = newps()
                xst(py, 0, wo_, KC, 256, xT, xTk, b)
                stt(x32[0:R, b, c0:c0 + 256], x32[0:R, b, c0:c0 + 256], ALPHA, py[0:R, 0:256], OP.mult, OP.add,
                    [x32, py], [x32])
        phf = ExitStack()
        with phf:
            hT = k.sb([128, FCH, N], BF16, "hT", phf)
            sg2 = k.sb([128, WCOLS], F32, "sg2", phf)
            lnp_alloc(phf)
            load(lnp, lnp[:], ln1[l])

            def norm_and_transpose(last):
                for b in range(NB):
                    layernorm_block(R, b)
                    if last:
                        dst = y_s[:, :] if sample else y_p[tok0 + b * 128: tok0 + (b + 1) * 128, :]
                        store(x32, dst, x32[0:R, b, :])
                    else:
                        cp(xin16[0:R, :], x32[0:R, b, :], [x32], [xin16], eng="act")
                        transpose_block(R, b, b * R, srcbuf=xin16,
                                        src_of=lambda kc: xin16[0:R, kc * 128:(kc + 1) * 128])

            norm_and_transpose(False)
            dump("x1", x32, x32[:])
            stage("S7")

            for mp in range(22):
                nm = 2 if mp < 21 else 1
                wgt, _, _ = wnext(w_up[l, :, mp * 256: mp * 256 + nm * 128])
                wut, _, _ = wnext(w_up[l, :, DFF + mp * 256: DFF + mp * 256 + nm * 128])
                for mi in range(nm):
                    m = mp * 2 + mi
                    e = m % NCB
                    pgh, puh = newps(), newps()
                    wst(pgh, wgt, KC, mi * 128, xT, xTk, 0, N)
                    wst(puh, wut, KC, mi * 128, xT, xTk, 0, N)
                    a = conv_chunk(pgh, 3, cw_f, m, hist_f[l], hs_f, e)
                    act(tmpa[e % 2][:, 0:N], a[:, 0:N], AF.Silu, [a], [tmpa[e % 2]])
                    tt(hT[:, m, 0:N], tmpa[e % 2][:, 0:N], puh[:, 0:N], OP.mult, [tmpa[e % 2], puh], [hT])
            if sample:
                store(hs_f, ncf_s[l], hs_f[:])

            stage("S8")
            load(pT32, pT32[:, :, 0:N], (psT[l] if sample else ppT[l, :, tok0:tok0 + N]).rearrange(
                "(kc p) n -> p kc n", p=128))
            cp(pT[:, :, 0:N], pT32[:, :, 0:N], [pT32], [pT], eng="dve")
            load(lnp, lnp[:], ln2[l])
            for cg in range(8):
                c0 = cg * 256
                pf = [newps() for _ in range(NB)]
                kgs = ((0, 16), (16, 16), (32, 11))
                for gi, (k0, kn_) in enumerate(kgs):
                    wd_, _, _ = wnext(w_down[l, k0 * 128:(k0 + kn_) * 128, c0:c0 + 256])
                    for b in range(NB):
                        def fn(h, b=b, wd_=wd_, k0=k0, kn_=kn_, gi=gi):
                            ins = None
                            for kc in range(kn_):
                                ins = h.matmul(pf[b][0:R, 0:256], hT[:, k0 + kc, b * R:(b + 1) * R], wd_[:, kc, 0:256],
                                               start=(gi == 0 and kc == 0), stop=(gi == 2 and kc == kn_ - 1))
                            return ins
                        k.op("pe", fn, [hT, wd_], [pf[b]])
                wpg_, _, _ = wnext(w_pg[l, :, c0:c0 + 256])
                wpe_, _, _ = wnext(w_pe[l, :, c0:c0 + 256])
                for b in range(NB):
                    ppg, ppe = newps(), newps()
                    xst(ppg, 0, wpg_, KC, 256, xT, xTk, b)
                    xst(ppe, 0, wpe_, 2, 256, pT, lambda kc: pT[:, kc, :], b)
                    act(sg2[0:R, :], ppg[0:R, 0:256], AF.Sigmoid, [ppg], [sg2])
                    tt(sg2[0:R, :], sg2[0:R, :], ppe[0:R, 0:256], OP.mult, [sg2, ppe], [sg2])
                    tt(sg2[0:R, :], sg2[0:R, :], pf[b][0:R, 0:256], OP.add, [sg2, pf[b]], [sg2])
                    stt(x32[0:R, b, c0:c0 + 256], x32[0:R, b, c0:c0 + 256], ALPHA, sg2[0:R, :], OP.mult, OP.add,
                        [x32, sg2], [x32])
            dump("hT", hT, hT[:])
            norm_and_transpose(last_layer)
            dump("x2", x32, x32[:])
        k.barrier()

    def first_layer_input(tl):
        kind, R, NB, tok0 = tl["kind"], tl["R"], tl["NB"], tl["tok0"]
        with ExitStack() as st0:
            lnp_alloc(st0)
            load(lnp, lnp[:], lnin[:, :, :])
            for b in range(NB):
                src = xs[:, :] if kind == "s" else xp[tok0 + b * 128: tok0 + (b + 1) * 128, :]
                load(x32, x32[0:R, b, :], src)
                layernorm_block(R, b)
                cp(xin16[0:R, :], x32[0:R, b, :], [x32], [xin16], eng="act")
                transpose_block(R, b, b * R, srcbuf=xin16, src_of=lambda kc: xin16[0:R, kc * 128:(kc + 1) * 128])
            k.barrier()

    tiles = [dict(kind="p", R=128, NB=TB, nseq=1, Ls=128, tok0=t * TB * 128) for t in range(NPT)]
    tiles.append(dict(kind="s", R=64, NB=1, nseq=16, Ls=4, tok0=0))

    def emit_all():
        setup_consts()
        stage("consts")
        tl_list = tiles if dbg is None else [tiles[i] for i in dbg["tiles"]]
        layers = range(2) if dbg is None else dbg["layers"]
        for tl in tl_list:
            first_layer_input(tl)
            dump("xT0", xT, xT[:])
            dump("x0", x32, x32[:])
            stage("in")
            for l in layers:
                run_tile_layer(l, tl, False, l == 1)
        for l in range(2):
            store(hist_a[l], nca_p[l], hist_a[l][:])
            store(hist_q[l], ncq_p[l], hist_q[l][:])
            store(hist_f[l], ncf_p[l], hist_f[l][:])
            store(S32[l], nd_p[l].rearrange("h k v -> k h v"), S32[l][:])

    def emit_guard():
        try:
            emit_all()
        except _Stop:
            pass

    k.plan = True
    emit_guard()
    k.plan = False
    k.stopped = False
    psi[0] = 0
    emit_guard()
    k.stopped = False
    k.finish()


_PROG = {}


def _consts():
    def mk(R, Ls):
        idx = np.arange(R)
        seq = idx // Ls
        same = seq[:, None] == seq[None, :]
        c = np.zeros((R, 6, R), np.float32)
        c[:, 0] = np.eye(R)
        low = same & (idx[:, None] > idx[None, :])
        up = same & (idx[:, None] <= idx[None, :])
        c[:, 1] = np.where(low, 0.0, NEG)
        c[:, 2] = np.where(up, 0.0, NEG)
        c[:, 3] = up.astype(np.float32)
        c[:, 4] = up.astype(np.float32)
        c[:, 5] = same.astype(np.float32)
        return c
    smk = (np.arange(64)[:, None] // 4 == np.arange(16)[None, :]).astype(np.float32)
    smT = np.ascontiguousarray(np.broadcast_to(smk.T[None], (128, 16, 64))).astype(np.float32)
    return mk(128, 128), mk(64, 4), smk, smT


def kernel(x_prompt, x_sample, state_conv_a, state_conv_qkv, state_delta, state_conv_ffn,
           p_prompt, p_sample, ln_in_g, ln_in_b,
           w_in, conv_a_w, w_a_out, conv_b_w, a_log, dt_bias, norm_b_g, w_b_out,
           ln_c_g, ln_c_b, w_s, b_s, w_c_out, w_o, ln1_g, ln1_b,
           w_up, conv_f_w, w_down, w_pe, w_pg, ln2_g, ln2_b):
    f = lambda a: np.ascontiguousarray(np.asarray(a, dtype=np.float32))
    x_prompt, x_sample = f(x_prompt), f(x_sample)
    if "nc" not in _PROG:
        _PROG["nc"] = build_program()
    nc = _PROG["nc"]
    cst_p, cst_s, smk, smT = _consts()

    def rep(a):
        a = f(a)
        return np.ascontiguousarray(np.broadcast_to(a[..., None, :], a.shape[:-1] + (128, a.shape[-1])))

    def fm(a, nch):
        a = f(a)
        return np.ascontiguousarray(a.reshape(2, a.shape[1], nch, 128).transpose(0, 3, 2, 1))

    shared = {
        "w_in": f(w_in), "w_a_out": f(w_a_out), "w_b_out": f(w_b_out), "w_c_out": f(w_c_out), "w_o": f(w_o),
        "w_up": f(w_up), "w_down": f(w_down), "w_pe": f(w_pe), "w_pg": f(w_pg),
        "cwa": fm(conv_a_w, 8), "cwq": fm(conv_b_w, 24), "cwf": fm(conv_f_w, FCH),
        "lnin": np.ascontiguousarray(np.stack([rep(ln_in_g), rep(ln_in_b)], axis=1)),
        "ln1": np.ascontiguousarray(np.stack([rep(ln1_g), rep(ln1_b)], axis=2)),
        "ln2": np.ascontiguousarray(np.stack([rep(ln2_g), rep(ln2_b)], axis=2)),
        "lnc": np.ascontiguousarray(np.stack([rep(ln_c_g), rep(ln_c_b)], axis=2)),
        "nbg": rep(norm_b_g), "alog": rep(a_log), "dtb": rep(dt_bias),
        "cst_p": cst_p, "cst_s": cst_s, "smk_s": smk, "smT_s": smT,
    }
    ws = f(w_s)
    bs = f(b_s)
    shared["wsT_p"] = np.ascontiguousarray(ws.transpose(0, 3, 1, 2))
    w4 = ws[:, :, 0:4, 0:4].transpose(0, 3, 1, 2)
    shared["wsT_s"] = np.ascontiguousarray(np.tile(w4, (1, 16, 1, 16)))
    shared["bsb_p"] = np.ascontiguousarray(np.broadcast_to(bs[:, None], (2, 128, 8, 128)))
    shared["bsb_s"] = np.ascontiguousarray(np.broadcast_to(np.tile(bs[:, :, 0:4], (1, 1, 16))[:, None], (2, 128, 8, 64)))

    sca_f, scq_f, scf_f = f(state_conv_a), f(state_conv_qkv), f(state_conv_ffn)
    sdl_f, pp_f, psm_f = f(state_delta), f(p_prompt), f(p_sample)

    def fms(a, nch):
        return np.ascontiguousarray(a.reshape(2, 16, a.shape[2], nch, 128).transpose(0, 4, 3, 1, 2))

    in_maps = []
    for c in range(8):
        s = c % 4
        q0 = 16 * c
        m = dict(shared)
        m["xp"] = np.ascontiguousarray(x_prompt[s])
        m["xs"] = np.ascontiguousarray(x_sample[q0:q0 + 16].reshape(64, D))
        m["ppT"] = np.ascontiguousarray(pp_f[:, s].transpose(0, 2, 1))
        m["psT"] = np.ascontiguousarray(psm_f[:, q0:q0 + 16].reshape(2, 64, DPLE).transpose(0, 2, 1))
        m["sca"] = fms(sca_f[:, q0:q0 + 16], 8)
        m["scq"] = fms(scq_f[:, q0:q0 + 16], 24)
        m["scf"] = fms(scf_f[:, q0:q0 + 16], FCH)
        m["sdl"] = np.ascontiguousarray(sdl_f[:, q0:q0 + 16])
        in_maps.append(m)
    res = run_bass_kernel_spmd(nc, in_maps, core_ids=list(range(8)))
    R = res.results

    def unfm(a):
        return a.transpose(0, 3, 2, 1).reshape(2, a.shape[3], -1)

    def unfms(a):
        return a.transpose(0, 3, 4, 2, 1).reshape(2, 16, a.shape[4], -1)

    y_prompt = np.stack([R[s]["y_p"] for s in range(4)], 0)
    y_sample = np.concatenate([R[c]["y_s"].reshape(16, 4, D) for c in range(8)], 0)
    nca_p = np.stack([unfm(R[s]["nca_p"]) for s in range(4)], 1)
    ncq_p = np.stack([unfm(R[s]["ncq_p"]) for s in range(4)], 1)
    ncf_p = np.stack([unfm(R[s]["ncf_p"]) for s in range(4)], 1)
    nd_p = np.stack([R[s]["nd_p"] for s in range(4)], 1)
    nca_s = np.concatenate([unfms(R[c]["nca_s"]) for c in range(8)], 1)
    ncq_s = np.concatenate([unfms(R[c]["ncq_s"]) for c in range(8)], 1)
    ncf_s = np.concatenate([unfms(R[c]["ncf_s"]) for c in range(8)], 1)
    nd_s = np.concatenate([R[c]["nd_s"] for c in range(8)], 1)
    nv_s = np.concatenate([R[c]["nv_s"].reshape(2, 16, 4, 1024) for c in range(8)], 1)
    outs = (y_prompt, y_sample, nca_p, ncq_p, nd_p, ncf_p, nca_s, ncq_s, nd_s, ncf_s, nv_s)
    return tuple(np.ascontiguousarray(o, dtype=np.float32) for o in outs)
```

```python
import numpy as np
from contextlib import ExitStack
import concourse.bass as bass
import concourse.mybir as mybir
from concourse.bass_utils import run_bass_kernel_spmd

F32 = mybir.dt.float32
BF16 = mybir.dt.bfloat16
AF = mybir.ActivationFunctionType
OP = mybir.AluOpType
AX = mybir.AxisListType

D = 2048
KC = 16
NIN = 15376
DFF = 5504
FCH = 43
DPLE = 256
NH = 8
ALPHA = 4 ** 0.25
LN_EPS = 1e-5
RMS_EPS = 1e-6
O_AH, O_ABG, O_ACG, O_Q, O_K, O_V, O_Z, O_BETA, O_DEC, O_CU, O_CV, O_G = (
    0, 1024, 2048, 3072, 4096, 5120, 6144, 7168, 7176, 7184, 8208, 9232)
TB = 2
NPT = 16 // TB
NSLOT = 6
WCOLS = 256
NEG = -30000.0


class _Stop(Exception):
    pass


class Eng:
    def __init__(self, name, h, sem):
        self.name, self.h, self.sem, self.tick = name, h, sem, 0


class Buf:
    def __init__(self, t, name):
        self.t, self.name = t, name
        self.lastw = None
        self.readers = {}
        self.dsem = None
        self.dcount = 0
        self.is_psum = False

    def __getitem__(self, idx):
        return self.t[idx]


class K:
    def __init__(self, nc, es):
        self.nc, self.es = nc, es
        self.plan = False
        self.stopped = False
        self.waited = {}
        self.engs = {}
        self.dbufs = []
        for name, h in (("pe", nc.tensor), ("act", nc.scalar), ("dve", nc.vector),
                        ("pool", nc.gpsimd), ("sp", nc.sync)):
            self.engs[name] = Eng(name, h, es.enter_context(nc.semaphore("s_" + name)))
        self.nalloc = 0

    def sb(self, shape, dt, name=None, stack=None):
        self.nalloc += 1
        name = (name or "t") + "_%d" % self.nalloc
        t = (stack or self.es).enter_context(self.nc.sbuf_tensor(name, list(shape), dt))
        return Buf(t, name)

    def ps(self, name):
        t = self.es.enter_context(self.nc.psum_tensor(name, [128, 512], F32))
        b = Buf(t, name)
        b.is_psum = True
        return b

    def _wait(self, eng, sem, semkey, val):
        k = (eng.name, semkey)
        if self.waited.get(k, 0) < val:
            eng.h.wait_ge(sem, val)
            self.waited[k] = val

    def _deps(self, eng, reads, writes):
        for b in reads:
            if b.lastw is not None:
                self._dep(eng, b, b.lastw)
            if b.is_psum:
                for e, tk in b.readers.items():
                    if e is not eng and e != 'dma':
                        self._dep(eng, b, ('e', e, tk))
        for b in writes:
            if b.lastw is not None:
                self._dep(eng, b, b.lastw)
            for e, tk in b.readers.items():
                if e == 'dma':
                    self._dep(eng, b, ('d', tk))
                else:
                    self._dep(eng, b, ('e', e, tk))

    def _dep(self, eng, b, d):
        if d[0] == 'd':
            self._wait(eng, b.dsem, "d" + b.name, 16 * d[1])
        else:
            src = d[1]
            if src is eng and eng.name in ("pe", "sp"):
                return
            self._wait(eng, src.sem, src.name, d[2])

    def op(self, en, fn, reads, writes):
        if self.plan or self.stopped:
            return
        eng = self.engs[en]
        self._deps(eng, reads, writes)
        ins = fn(eng.h)
        eng.tick += 1
        ins.then_inc(eng.sem, 1)
        for b in reads:
            b.readers[eng] = eng.tick
        for b in writes:
            b.lastw = ('e', eng, eng.tick)
            b.readers = {}

    def dma(self, en, out_ap, in_ap, buf, is_write):
        if self.plan or self.stopped:
            return
        eng = self.engs[en]
        if buf.dsem is None:
            buf.dsem = self.es.enter_context(self.nc.semaphore("d_" + buf.name))
            self.dbufs.append(buf)
        if is_write:
            self._deps(eng, [], [buf])
        else:
            self._deps(eng, [buf], [])
        ins = eng.h.dma_start(out=out_ap, in_=in_ap)
        ins.then_inc(buf.dsem, 16)
        buf.dcount += 1
        if is_write:
            buf.lastw = ('d', buf.dcount)
            buf.readers = {}
        else:
            buf.readers['dma'] = buf.dcount

    def barrier(self):
        if self.plan or self.stopped:
            return
        es = list(self.engs.values())
        for e in es:
            for s in es:
                if s is not e and s.tick > 0:
                    self._wait(e, s.sem, s.name, s.tick)
            for b in self.dbufs:
                if b.dcount > 0:
                    self._wait(e, b.dsem, "d" + b.name, 16 * b.dcount)

    def finish(self):
        sp = self.engs["sp"]
        for e in self.engs.values():
            if e is not sp and e.tick > 0:
                self._wait(sp, e.sem, e.name, e.tick)
        for b in self.dbufs:
            self._wait(sp, b.dsem, "d" + b.name, 16 * b.dcount)


def build_program(dbg=None):
    nc = bass.Bass("TRN2", target_bir_lowering=False)
    es = ExitStack()
    with es:
        _build(nc, es, dbg)
    return nc


def _build(nc, es, dbg=None):
    def din(name, shape):
        return nc.dram_tensor(name, list(shape), F32, kind="ExternalInput").ap()

    def dout(name, shape):
        return nc.dram_tensor(name, list(shape), F32, kind="ExternalOutput").ap()

    xp = din("xp", [2048, D])
    xs = din("xs", [64, D])
    ppT = din("ppT", [2, DPLE, 2048])
    psT = din("psT", [2, DPLE, 64])
    sca = din("sca", [2, 128, 8, 16, 2])
    scq = din("scq", [2, 128, 24, 16, 3])
    scf = din("scf", [2, 128, FCH, 16, 2])
    sdl = din("sdl", [2, 16, NH, 128, 128])
    w_in = din("w_in", [2, D, NIN])
    w_a_out = din("w_a_out", [2, 1024, D])
    w_b_out = din("w_b_out", [2, 1024, D])
    w_c_out = din("w_c_out", [2, 1024, D])
    w_o = din("w_o", [2, D, D])
    w_up = din("w_up", [2, D, 2 * DFF])
    w_down = din("w_down", [2, DFF, D])
    w_pe = din("w_pe", [2, DPLE, D])
    w_pg = din("w_pg", [2, D, D])
    cwa = din("cwa", [2, 128, 8, 3])
    cwq = din("cwq", [2, 128, 24, 4])
    cwf = din("cwf", [2, 128, FCH, 3])
    lnin = din("lnin", [128, 2, D])
    ln1 = din("ln1", [2, 128, 2, D])
    ln2 = din("ln2", [2, 128, 2, D])
    lnc = din("lnc", [2, 128, 2, 1024])
    nbg = din("nbg", [2, 128, 128])
    alog = din("alog", [2, 128, NH])
    dtb = din("dtb", [2, 128, NH])
    wsT_p = din("wsT_p", [2, 128, 8, 128])
    wsT_s = din("wsT_s", [2, 64, 8, 64])
    bsb_p = din("bsb_p", [2, 128, 8, 128])
    bsb_s = din("bsb_s", [2, 128, 8, 64])
    cst_p = din("cst_p", [128, 6, 128])
    cst_s = din("cst_s", [64, 6, 64])
    smk_s = din("smk_s", [64, 16])
    smT_s = din("smT_s", [128, 16, 64])

    y_p = dout("y_p", [2048, D])
    y_s = dout("y_s", [64, D])
    nca_p = dout("nca_p", [2, 128, 8, 2])
    ncq_p = dout("ncq_p", [2, 128, 24, 3])
    ncf_p = dout("ncf_p", [2, 128, FCH, 2])
    nd_p = dout("nd_p", [2, NH, 128, 128])
    nca_s = dout("nca_s", [2, 128, 8, 16, 2])
    ncq_s = dout("ncq_s", [2, 128, 24, 16, 3])
    ncf_s = dout("ncf_s", [2, 128, FCH, 16, 2])
    nd_s = dout("nd_s", [2, 16, NH, 128, 128])
    nv_s = dout("nv_s", [2, 64, 1024])

    es.enter_context(nc.Block())
    k = K(nc, es)
    NMAX = TB * 128

    wslots = [k.sb([128, KC, WCOLS], BF16, "wslot") for _ in range(NSLOT)]
    wbd = k.sb([128, KC, 16], BF16, "wbd")
    x32 = k.sb([128, TB, D], F32, "x32")
    xT = k.sb([128, KC, NMAX], BF16, "xT")
    lnp = Buf(None, "lnp")

    def lnp_alloc(stack):
        k.nalloc += 1
        lnp.t = stack.enter_context(nc.sbuf_tensor("lnp_%d" % k.nalloc, [128, 2, D], F32))
    cp32 = k.sb([128, 6, 128], F32, "cp32")
    cs32 = k.sb([64, 6, 64], F32, "cs32")
    identb = k.sb([128, 128], BF16, "identb")
    onesf = k.sb([128, 128], F32, "onesf")
    smk = k.sb([64, 16], F32, "smk")
    smT = k.sb([128, 16, 64], BF16, "smT")
    S32 = [k.sb([128, NH, 128], F32, "S32") for _ in range(2)]
    hist_a = [k.sb([128, 8, 2], F32, "hista") for _ in range(2)]
    hist_q = [k.sb([128, 24, 3], F32, "histq") for _ in range(2)]
    hist_f = [k.sb([128, FCH, 2], F32, "histf") for _ in range(2)]
    cw_a = k.sb([128, 8, 3], F32, "cwa")
    cw_q = k.sb([128, 24, 4], F32, "cwq")
    cw_f = k.sb([128, FCH, 3], F32, "cwf")
    NCB = 4
    ext = [k.sb([128, 520], F32, "ext") for _ in range(NCB)]
    acc = [k.sb([128, NMAX], F32, "acc") for _ in range(NCB)]
    tmpa = [k.sb([128, NMAX], F32, "tmpa") for _ in range(2)]
    stat = k.sb([128, 4, 6], F32, "stat")
    mv = k.sb([128, 2], F32, "mv")
    rstd = k.sb([128, 1], F32, "rstd")
    nmr = k.sb([128, 1], F32, "nmr")
    pT = k.sb([128, 2, NMAX], BF16, "pT")
    pT32 = k.sb([128, 2, NMAX], F32, "pT32")
    lncp = k.sb([128, 2, 1024], F32, "lncp")
    wm32 = k.sb([128, 8, 128], F32, "wm32")
    bsb = k.sb([128, 8, 128], F32, "bsb")
    ngb = k.sb([128, 128], F32, "ngb")
    alg = k.sb([128, NH], F32, "alg")
    dtbt = k.sb([128, NH], F32, "dtbt")
    xin16 = k.sb([128, D], BF16, "xin16")
    PS = [k.ps("ps%d" % i) for i in range(8)]
    psi = [0]

    def newps():
        p = PS[psi[0] % 8]
        psi[0] += 1
        return p

    NBLK_MAX = 400
    WSB = 200
    wscr = [[nc.dram_tensor("wscr_%d_%d" % (l_, g_), [WSB, 128, KC * WCOLS], BF16, kind="Internal").ap()
             for g_ in range(NBLK_MAX // WSB)] for l_ in range(2)]
    wspecs = []
    wstate = {"issued": 0, "cons": 0, "l": 0, "j": 0}
    wseen = set()

    def wbegin(l):
        wstate["l"] = l
        wstate["j"] = 0

    def _issue(i):
        src, kcn, ncols, l, j = wspecs[i]
        slot = wslots[i % NSLOT]
        sview = slot[:, 0:kcn, 0:ncols]
        cview = wscr[l][j // WSB][j % WSB, :, 0:kcn * ncols].rearrange("p (kc n) -> p kc n", kc=kcn)
        if (l, j) not in wseen:
            wseen.add((l, j))
            k.dma("pool", sview, src.rearrange("(kc p) n -> p kc n", p=128), slot, True)
            k.dma("sp", cview, sview, slot, False)
        else:
            k.dma("sp", sview, cview, slot, True)

    def wnext(src2d, held=0):
        kcn = src2d.shape[0] // 128
        ncols = src2d.shape[1]
        if k.stopped:
            return wslots[0], kcn, ncols
        if k.plan:
            wspecs.append((src2d, kcn, ncols, wstate["l"], wstate["j"]))
            wstate["j"] += 1
            assert wstate["j"] <= NBLK_MAX
            return wslots[0], kcn, ncols
        i = wstate["cons"]
        wstate["cons"] += 1
        assert wspecs[i][1] == kcn and wspecs[i][2] == ncols
        while wstate["issued"] < min(len(wspecs), i + NSLOT - held) or wstate["issued"] <= i:
            _issue(wstate["issued"])
            wstate["issued"] += 1
        return wslots[i % NSLOT], kcn, ncols

    def mmg(out_buf, out_ap, pairs, reads):
        def fn(h):
            n = len(pairs)
            ins = None
            for i, (l, r) in enumerate(pairs):
                ins = h.matmul(out_ap, l, r, start=(i == 0), stop=(i == n - 1))
            return ins
        k.op("pe", fn, reads, [out_buf])

    def act(out, in_, func, reads, writes, scale=None, bias=None, eng="act"):
        kw = {}
        if scale is not None:
            kw["scale"] = scale
        if bias is not None:
            kw["bias"] = bias
        k.op(eng, lambda h: h.activation(out=out, in_=in_, func=func, **kw), reads, writes)

    def tt(out, a, b, op, reads, writes, eng="dve"):
        k.op(eng, lambda h: h.tensor_tensor(out=out, in0=a, in1=b, op=op), reads, writes)

    def ts(out, a, s1, op0, reads, writes, s2=None, op1=None, eng="dve"):
        if op1 is None:
            k.op(eng, lambda h: h.tensor_scalar(out=out, in0=a, scalar1=s1, scalar2=None, op0=op0), reads, writes)
        else:
            k.op(eng, lambda h: h.tensor_scalar(out=out, in0=a, scalar1=s1, scalar2=s2, op0=op0, op1=op1),
                 reads, writes)

    def stt(out, a, s, b, op0, op1, reads, writes):
        k.op("dve", lambda h: h.scalar_tensor_tensor(out=out, in0=a, scalar=s, in1=b, op0=op0, op1=op1),
             reads, writes)

    def cp(out, in_, reads, writes, eng="act"):
        if eng == "act":
            k.op("act", lambda h: h.activation(out=out, in_=in_, func=AF.Copy), reads, writes)
        else:
            k.op(eng, lambda h: h.tensor_copy(out=out, in_=in_), reads, writes)

    def memset(buf, ap, val):
        k.op("dve", lambda h: h.memset(ap, val), [], [buf])

    def load(buf, out_ap, in_ap, eng="sp"):
        k.dma(eng, out_ap, in_ap, buf, True)

    def store(buf, out_ap, in_ap, eng="sp"):
        k.dma(eng, out_ap, in_ap, buf, False)

    def stage(name):
        if dbg is not None and dbg.get("stop") == name:
            k.stopped = True

    dcount = [0]

    def dump(name, buf, ap):
        if dbg is None or k.plan or name not in dbg.get("dumps", ()):
            return
        dcount[0] += 1
        dt = nc.dram_tensor("dbg_%s_%d" % (name, dcount[0]), list(ap.shape), ap.dtype, kind="ExternalOutput").ap()
        store(buf, dt, ap)

    def setup_consts():
        load(cp32, cp32[:], cst_p[:, :, :])
        load(cs32, cs32[:], cst_s[:, :, :])
        load(smk, smk[:], smk_s[:, :])
        load(smT, smT[:], smT_s[:, :, :], eng="pool")
        cp(identb[:], cp32[:, 0, :], [cp32], [identb], eng="dve")
        memset(onesf, onesf[:], 1.0)
        for l in range(2):
            memset(S32[l], S32[l][:], 0.0)
            memset(hist_a[l], hist_a[l][:], 0.0)
            memset(hist_q[l], hist_q[l][:], 0.0)
            memset(hist_f[l], hist_f[l][:], 0.0)

    def layernorm_block(R, b, width=D, src=None, gb=None):
        src = src or x32
        gb = gb or lnp
        xin = src[0:R, b, 0:width]
        nch = width // 512
        for c in range(nch):
            k.op("dve", lambda h, c=c: h.bn_stats(out=stat[0:R, c, :], in_=src[0:R, b, c * 512:(c + 1) * 512]),
                 [src], [stat])
        k.op("dve", lambda h: h.bn_aggr(out=mv[0:R, :], in_=stat[0:R, 0:nch, :]), [stat], [mv])
        ts(rstd[0:R, :], mv[0:R, 1:2], LN_EPS, OP.add, [mv], [rstd])
        act(rstd[0:R, :], rstd[0:R, :], AF.Sqrt, [rstd], [rstd])
        k.op("dve", lambda h: h.reciprocal(out=rstd[0:R, :], in_=rstd[0:R, :]), [rstd], [rstd])
        stt(nmr[0:R, :], mv[0:R, 0:1], -1.0, rstd[0:R, :], OP.mult, OP.mult, [mv, rstd], [nmr])
        act(xin, xin, AF.Identity, [src, rstd, nmr], [src], scale=rstd[0:R, 0:1], bias=nmr[0:R, 0:1])
        tt(xin, xin, gb[0:R, 0, 0:width], OP.mult, [src, gb], [src])
        tt(xin, xin, gb[0:R, 1, 0:width], OP.add, [src, gb], [src])

    def transpose_block(R, b, c0, srcbuf=None, src_of=None):
        for g4 in range(KC // 4):
            p = newps()
            for j in range(4):
                kc = g4 * 4 + j
                l = src_of(kc)
                mmg(p, p[:, j * R:(j + 1) * R], [(l, identb[0:R, 0:R])], [srcbuf, identb])
            cp(xT[:, g4 * 4:(g4 + 1) * 4, c0:c0 + R],
               p[:, 0:4 * R].rearrange("p (a r) -> p a r", a=4), [p], [xT],
               eng=("act" if g4 % 2 == 0 else "dve"))

    def run_tile_layer(l, tl, first_layer, last_layer):
        with ExitStack() as tls:
            _rtl(l, tl, first_layer, last_layer, tls)

    def _rtl(l, tl, first_layer, last_layer, tls):
        kind, R, NB, nseq, Ls, tok0 = tl["kind"], tl["R"], tl["NB"], tl["nseq"], tl["Ls"], tl["tok0"]
        N = R * NB
        sample = kind == "s"
        cst = cs32 if sample else cp32
        identR = cst[0:R, 0, 0:R]
        maskLow = cst[0:R, 1, 0:R]
        maskUp = cst[0:R, 2, 0:R]
        maskC = cst[0:R, 3, 0:R]
        triC = cst[0:R, 4, 0:R]
        totC = cst[0:R, 5, 0:R]
        nlev = {128: 6, 4: 1}[Ls]

        def wst(psb, wb, kcn, c0, rbuf, rhs_of, n0, n1):
            mmg(psb, psb[:, 0:n1 - n0], [(wb[:, kc, c0:c0 + 128], rhs_of(kc)[:, n0:n1]) for kc in range(kcn)],
                [wb, rbuf])

        def xst(psb, pcol, wb, kcn, ncols, lbuf, lhs_of, b):
            mmg(psb, psb[0:R, pcol:pcol + ncols],
                [(lhs_of(kc)[:, b * R:(b + 1) * R], wb[:, kc, 0:ncols]) for kc in range(kcn)], [wb, lbuf])

        xTk = lambda kc: xT[:, kc, :]

        def conv_chunk(psb, W, cw, m, hist_buf, hs_buf, e):
            Wm1 = W - 1
            Lc = Ls if sample else N
            ex = ext[e]
            exv = ex[:, 0:nseq * (Wm1 + Lc)].rearrange("p (q t) -> p q t", q=nseq)
            pv = psb[:, 0:N].rearrange("p (q t) -> p q t", q=nseq)
            cp(exv[:, :, Wm1:], pv, [psb], [ex], eng="act")
            if sample:
                cp(exv[:, :, 0:Wm1], hs_buf[:, m, :, :], [hs_buf], [ex], eng="dve")
            else:
                cp(exv[:, 0, 0:Wm1], hist_buf[:, m, :], [hist_buf], [ex], eng="dve")
            a = acc[e]
            av = a[:, 0:N].rearrange("p (q t) -> p q t", q=nseq)
            act(av, exv[:, :, 0:Lc], AF.Copy, [ex, cw], [a], scale=cw[:, m, 0:1])
            for j in range(1, W):
                stt(av, exv[:, :, j:j + Lc], cw[:, m, j:j + 1], av, OP.mult, OP.add, [ex, cw, a], [a])
            if sample:
                cp(hs_buf[:, m, :, :], exv[:, :, Lc:Lc + Wm1], [ex], [hs_buf], eng="dve")
            else:
                cp(hist_buf[:, m, :], exv[:, 0, Lc:Lc + Wm1], [ex], [hist_buf], eng="dve")
            return a

        wbegin(l)
        hs_a = hs_q = hs_f = None
        if sample:
            hs_a = k.sb([128, 8, 16, 2], F32, "hsa", tls)
            hs_q = k.sb([128, 24, 16, 3], F32, "hsq", tls)
            hs_f = k.sb([128, FCH, 16, 2], F32, "hsf", tls)
        load(cw_a, cw_a[:], cwa[l])
        load(cw_q, cw_q[:], cwq[l])
        load(cw_f, cw_f[:], cwf[l])
        if sample:
            load(hs_a, hs_a[:], sca[l])
            load(hs_q, hs_q[:], scq[l])
            load(hs_f, hs_f[:], scf[l])

        phm = ExitStack()
        NA = N
        with phm:
            mA = k.sb([128, 8, NA], BF16, "mA", phm)
            mB = k.sb([128, 8, NA], BF16, "mB", phm)
            mC = k.sb([128, 8, NA], BF16, "mC", phm)
            mg = k.sb([128, NB, D], BF16, "mg", phm)
            mgt = k.sb([128, NB, WCOLS], F32, "mgt", phm)
            sg = k.sb([128, WCOLS], F32, "sg", phm)
            uT = k.sb([128, 8, NA], BF16, "uT", phm)
            cvall = k.sb([128, NB, 1024], F32, "cvall", phm)
            vcb = k.sb([128, NB, 1024], BF16, "vcb", phm)
            wmT = k.sb([128, 8, 128], BF16, "wmT", phm)
            qkvT = k.sb([128, 3, NA], BF16, "qkvT", phm)
            qkvTb = k.sb([128, 3, NA], BF16, "qkvTb", phm)
            bd = k.sb([128, TB, 16], F32, "bd", phm)
            zs = k.sb([128, NB, 128], BF16, "zs", phm)
            zsb = k.sb([128, NB, 128], BF16, "zsb", phm)
            sc = {n: k.sb([128, TB, NH], F32, n, phm) for n in
                  ("beta", "g", "gcum", "gtot", "egc", "ekd", "nbgc", "t8a", "t8b")}
            eglb = k.sb([128, TB, 16 * NH], F32, "eglb", phm)
            gsel = k.sb([128, 16 * NH], F32, "gsel", phm)
            def _mkT():
                d = dict(
                    sq=k.sb([128, 2, 128], F32, "sq", phm),
                    nT=k.sb([128, 2, 128], BF16, "nT", phm),
                    t1=k.sb([128, 128], F32, "t1", phm), t2=k.sb([128, 128], F32, "t2", phm),
                    dup=k.sb([128, 128], F32, "dup", phm), dlow=k.sb([128, 128], F32, "dlow", phm),
                    Am=[k.sb([128, 128], F32, "Am", phm) for _ in range(2)],
                    At=[k.sb([128, 128], F32, "At", phm) for _ in range(2)],
                    qkT=k.sb([128, 128], BF16, "qkT", phm), Pt32=k.sb([128, 128], F32, "Pt32", phm),
                    Ptb=k.sb([128, 128], BF16, "Ptb", phm), vb=k.sb([128, 128], F32, "vb", phm),
                    kd=k.sb([128, 128], BF16, "kd", phm))
                d["rn"] = d["sq"]
                d["Dg"] = d["t2"]
                return d
            TS2 = [[_mkT() for _b in range(NB)] for _p in range(2)]
            TS = TS2[0]
            t1 = TS[0]["t1"]
            dd = k.sb([128, 128], BF16, "dd", phm)
            vnew = k.sb([128, 128], BF16, "vnew", phm)
            to = k.sb([128, 128], F32, "to", phm)
            oh = k.sb([128, 128], F32, "oh", phm)
            og = k.sb([128, 128], BF16, "og", phm)
            ngz = k.sb([128, 128], F32, "ngz", phm)
            ss = k.sb([128, 1], F32, "ss", phm)
            Sb = k.sb([128, 128], BF16, "Sb", phm)
            if sample:
                S32s = k.sb([128, 16, 128], F32, "S32s", phm)
                Sbs = k.sb([128, 16, 128], BF16, "Sbs", phm)
                knTm = k.sb([128, 16, 64], BF16, "knTm", phm)
                qnTm = k.sb([128, 16, 64], BF16, "qnTm", phm)
                kdm = k.sb([64, 16, 128], BF16, "kdm", phm)
                vc32s = k.sb([64, 1024], F32, "vc32s", phm)

            load(lncp, lncp[:], lnc[l])
            load(ngb, ngb[:], nbg[l])
            load(alg, alg[:], alog[l])
            load(dtbt, dtbt[:], dtb[l])
            if sample:
                load(wm32, wm32[0:64, :, 0:64], wsT_s[l])
                load(bsb, bsb[:, :, 0:64], bsb_s[l])
            else:
                load(wm32, wm32[:], wsT_p[l])
                load(bsb, bsb[:], bsb_p[l])
            tt(wmT[0:R, :, 0:R], wm32[0:R, :, 0:R], maskC.unsqueeze(1).broadcast_to([R, 8, R]), OP.mult,
               [wm32, cst], [wmT])
            act(alg[:], alg[:], AF.Exp, [alg], [alg])
            ts(alg[:], alg[:], -1.0, OP.mult, [alg], [alg])

            for mp in range(4):
                wh, _, _ = wnext(w_in[l, :, O_AH + mp * 256: O_AH + (mp + 1) * 256])
                wg, _, _ = wnext(w_in[l, :, O_ABG + mp * 256: O_ABG + (mp + 1) * 256], held=1)
                wc, _, _ = wnext(w_in[l, :, O_ACG + mp * 256: O_ACG + (mp + 1) * 256], held=2)
                for mi in range(2):
                    m = mp * 2 + mi
                    e = m % NCB
                    ph, pg, pc = newps(), newps(), newps()
                    wst(ph, wh, KC, mi * 128, xT, xTk, 0, N)
                    wst(pg, wg, KC, mi * 128, xT, xTk, 0, N)
                    wst(pc, wc, KC, mi * 128, xT, xTk, 0, N)
                    cp(tmpa[e % 2][:, 0:N], ph[:, 0:N], [ph], [tmpa[e % 2]], eng="act")
                    tt(pc[:, 0:N], tmpa[e % 2][:, 0:N], pc[:, 0:N], OP.mult, [tmpa[e % 2], pc], [pc])
                    a = conv_chunk(pc, 3, cw_a, m, hist_a[l], hs_a, e)
                    tt(mA[:, m, 0:N], a[:, 0:N], pg[:, 0:N], OP.mult, [a, pg], [mA])
            dump("mA", mA, mA[:])
            stage("S1")
            if sample:
                store(hs_a, nca_s[l], hs_a[:])

            for mp in range(4):
                wu, _, _ = wnext(w_in[l, :, O_CU + mp * 256: O_CU + (mp + 1) * 256])
                for mi in range(2):
                    m = mp * 2 + mi
                    pu = newps()
                    wst(pu, wu, KC, mi * 128, xT, xTk, 0, N)
                    act(uT[:, m, 0:N], pu[:, 0:N], AF.Gelu_apprx_tanh, [pu], [uT])
            stage("S4a")
            for cgp in range(4):
                wv, _, _ = wnext(w_in[l, :, O_CV + cgp * 256: O_CV + (cgp + 1) * 256])
                for b in range(NB):
                    pv = newps()
                    xst(pv, 0, wv, KC, 256, xT, xTk, b)
                    act(cvall[0:R, b, cgp * 256:(cgp + 1) * 256], pv[0:R, 0:256], AF.Gelu_apprx_tanh, [pv], [cvall])
            for b in range(NB):
                for c in range(2):
                    k.op("dve", lambda h, c=c, b=b: h.bn_stats(out=stat[0:R, c, :], in_=cvall[0:R, b, c * 512:(c + 1) * 512]),
                         [cvall], [stat])
                k.op("dve", lambda h: h.bn_aggr(out=mv[0:R, :], in_=stat[0:R, 0:2, :]), [stat], [mv])
                ts(rstd[0:R, :], mv[0:R, 1:2], LN_EPS, OP.add, [mv], [rstd])
                act(rstd[0:R, :], rstd[0:R, :], AF.Sqrt, [rstd], [rstd])
                k.op("dve", lambda h: h.reciprocal(out=rstd[0:R, :], in_=rstd[0:R, :]), [rstd], [rstd])
                ts(cvall[0:R, b, :], cvall[0:R, b, :], mv[0:R, 0:1], OP.subtract, [cvall, mv, rstd], [cvall],
                   s2=rstd[0:R, 0:1], op1=OP.mult)
                tt(cvall[0:R, b, :], cvall[0:R, b, :], lncp[0:R, 0, :], OP.mult, [cvall, lncp], [cvall])
                if sample:
                    tt(vc32s[:, :], cvall[0:R, b, :], lncp[0:R, 1, :], OP.add, [cvall, lncp], [vc32s])
                    store(vc32s, nv_s[l], vc32s[:, :])
                    cp(vcb[0:R, b, :], vc32s[:, :], [vc32s], [vcb], eng="act")
                else:
                    tt(vcb[0:R, b, :], cvall[0:R, b, :], lncp[0:R, 1, :], OP.add, [cvall, lncp], [vcb])
                for gg in range(8):
                    pm = newps()
                    mmg(pm, pm[:, 0:R], [(vcb[0:R, b, gg * 128:(gg + 1) * 128], wmT[0:R, gg, 0:R])], [vcb, wmT])
                    tt(t1[:, 0:R], pm[:, 0:R], bsb[:, gg, 0:R], OP.add, [pm, bsb], [t1])
                    tt(mC[:, gg, b * R:(b + 1) * R], t1[:, 0:R], uT[:, gg, b * R:(b + 1) * R], OP.mult,
                       [t1, uT], [mC])

            dump("mC", mC, mC[:])
            stage("S4")
            dump("uT", uT, uT[:])
            dump("vcb", vcb, vcb[:])
            load_bd = w_in[l, :, O_BETA:O_BETA + 16]
            k.dma("pool", wbd[:, :, :], load_bd.rearrange("(kc p) n -> p kc n", p=128), wbd, True)
            for b in range(NB):
                pb = newps()
                xst(pb, 0, wbd, KC, 16, xT, xTk, b)
                cp(bd[0:R, b, :], pb[0:R, 0:16], [pb], [bd], eng="act")
                act(sc["beta"][0:R, b, :], bd[0:R, b, 0:8], AF.Sigmoid, [bd], [sc["beta"]])
                tt(sc["t8a"][0:R, b, :], bd[0:R, b, 8:16], dtbt[0:R, :], OP.add, [bd, dtbt], [sc["t8a"]])
                act(sc["t8b"][0:R, b, :], sc["t8a"][0:R, b, :], AF.Abs, [sc["t8a"]], [sc["t8b"]])
                act(sc["t8b"][0:R, b, :], sc["t8b"][0:R, b, :], AF.Exp, [sc["t8b"]], [sc["t8b"]], scale=-1.0)
                ts(sc["t8b"][0:R, b, :], sc["t8b"][0:R, b, :], 1.0, OP.add, [sc["t8b"]], [sc["t8b"]])
                act(sc["t8b"][0:R, b, :], sc["t8b"][0:R, b, :], AF.Ln, [sc["t8b"]], [sc["t8b"]])
                ts(sc["t8a"][0:R, b, :], sc["t8a"][0:R, b, :], 0.0, OP.max, [sc["t8a"]], [sc["t8a"]])
                tt(sc["t8a"][0:R, b, :], sc["t8a"][0:R, b, :], sc["t8b"][0:R, b, :], OP.add,
                   [sc["t8a"], sc["t8b"]], [sc["t8a"]])
                tt(sc["g"][0:R, b, :], sc["t8a"][0:R, b, :], alg[0:R, :], OP.mult, [sc["t8a"], alg], [sc["g"]])
                pgc = newps()
                mmg(pgc, pgc[0:R, 0:8], [(triC, sc["g"][0:R, b, :])], [cst, sc["g"]])
                mmg(pgc, pgc[0:R, 8:16], [(totC, sc["g"][0:R, b, :])], [cst, sc["g"]])
                cp(sc["gcum"][0:R, b, :], pgc[0:R, 0:8], [pgc], [sc["gcum"]], eng="act")
                cp(sc["gtot"][0:R, b, :], pgc[0:R, 8:16], [pgc], [sc["gtot"]], eng="act")
                act(sc["egc"][0:R, b, :], sc["gcum"][0:R, b, :], AF.Exp, [sc["gcum"]], [sc["egc"]])
                tt(sc["ekd"][0:R, b, :], sc["gtot"][0:R, b, :], sc["gcum"][0:R, b, :], OP.subtract,
                   [sc["gtot"], sc["gcum"]], [sc["ekd"]])
                act(sc["ekd"][0:R, b, :], sc["ekd"][0:R, b, :], AF.Exp, [sc["ekd"]], [sc["ekd"]])
                tt(sc["nbgc"][0:R, b, :], sc["beta"][0:R, b, :], sc["egc"][0:R, b, :], OP.mult,
                   [sc["beta"], sc["egc"]], [sc["nbgc"]])
                ts(sc["nbgc"][0:R, b, :], sc["nbgc"][0:R, b, :], -1.0, OP.mult, [sc["nbgc"]], [sc["nbgc"]])
                if sample:
                    tt(gsel[0:R, :].rearrange("p (q h) -> p q h", q=16),
                       sc["g"][0:R, b, :].unsqueeze(1).broadcast_to([R, 16, NH]),
                       smk[0:R, :].unsqueeze(2).broadcast_to([R, 16, NH]), OP.mult, [sc["g"], smk], [gsel])
                    nq = 16
                else:
                    cp(gsel[0:R, 0:NH], sc["g"][0:R, b, :], [sc["g"]], [gsel], eng="dve")
                    nq = 1
                pe_ = newps()
                mmg(pe_, pe_[:, 0:nq * NH], [(onesf[0:R, :], gsel[0:R, 0:nq * NH])], [onesf, gsel])
                act(eglb[:, b, 0:nq * NH], pe_[:, 0:nq * NH], AF.Exp, [pe_], [eglb])

            stage("SBs")
            qkvT2 = [qkvT, qkvTb]
            zs2 = [zs, zsb]

            def gen_proj(h, qkvT, zs):
                for j, (off, mch) in enumerate(((O_K, 8 + h), (O_Q, h), (O_V, 16 + h))):
                    wj, _, _ = wnext(w_in[l, :, off + h * 128: off + (h + 1) * 128])
                    pj = newps()
                    wst(pj, wj, KC, 0, xT, xTk, 0, N)
                    a = conv_chunk(pj, 4, cw_q, mch, hist_q[l], hs_q, (3 * h + j) % NCB)
                    act(qkvT[:, j, 0:N], a[:, 0:N], AF.Silu, [a], [qkvT])
                    yield
                wz, _, _ = wnext(w_in[l, :, O_Z + h * 128: O_Z + (h + 1) * 128])
                for b in range(NB):
                    pz = newps()
                    xst(pz, 0, wz, KC, 128, xT, xTk, b)
                    act(zs[0:R, b, :], pz[0:R, 0:128], AF.Silu, [pz], [zs])
                    yield

            def prep_chain(b, T, h, qkvT):
                cs = slice(b * R, (b + 1) * R)
                beta_h = sc["beta"][0:R, b, h:h + 1]
                gcum_h = sc["gcum"][0:R, b, h:h + 1]
                sq, rn, nT, Dg, t1, t2, dup, dlow = (T[n] for n in ("sq", "rn", "nT", "Dg", "t1", "t2", "dup", "dlow"))
                Am, At, qkT, Pt32, Ptb, vb, kd = (T[n] for n in ("Am", "At", "qkT", "Pt32", "Ptb", "vb", "kd"))
                act(sq[:, :, 0:R], qkvT[:, 0:2, cs], AF.Square, [qkvT], [sq])
                ts(Dg[0:R, 0:R], identR, gcum_h, OP.mult, [cst, sc["gcum"]], [Dg])
                yield
                pn = newps()
                mmg(pn, pn[:, 0:2 * R].rearrange("p (a r) -> p a r", a=2),
                    [(onesf[:, :], sq[:, :, 0:R])], [onesf, sq])
                pr = newps()
                mmg(pr, pr[0:R, 0:R], [(onesf[0:R, 0:R], Dg[0:R, 0:R])], [onesf, Dg])
                ts(rn[:, :, 0:R], pn[:, 0:2 * R].rearrange("p (a r) -> p a r", a=2), RMS_EPS, OP.add,
                   [pn], [rn])
                ts(t1[0:R, 0:R], pr[0:R, 0:R], gcum_h, OP.subtract, [pr, sc["gcum"]], [t1])
                yield
                act(rn[:, :, 0:R], rn[:, :, 0:R], AF.Sqrt, [rn], [rn])
                stt(t2[0:R, 0:R], t1[0:R, 0:R], 0.0, maskUp, OP.min, OP.add, [t1, cst], [t2])
                yield
                k.op("dve", lambda hh: hh.reciprocal(out=rn[:, :, 0:R], in_=rn[:, :, 0:R]), [rn], [rn])
                act(dup[0:R, 0:R], t2[0:R, 0:R], AF.Exp, [t2], [dup])
                yield
                ts(rn[:, 1, 0:R], rn[:, 1, 0:R], 128 ** -0.5, OP.mult, [rn], [rn])
                stt(t2[0:R, 0:R], t1[0:R, 0:R], 0.0, maskLow, OP.max, OP.subtract, [t1, cst], [t2])
                yield
                tt(nT[:, :, 0:R], qkvT[:, 0:2, cs], rn[:, :, 0:R], OP.mult, [qkvT, rn], [nT])
                act(dlow[0:R, 0:R], t2[0:R, 0:R], AF.Exp, [t2], [dlow], scale=-1.0)
                yield
                pkv = newps()
                mmg(pkv, pkv[0:R, 0:128], [(nT[:, 0, 0:R], identb[:, :])], [nT, identb])
                mmg(pkv, pkv[0:R, 128:256], [(qkvT[:, 2, cs], identb[:, :])], [qkvT, identb])
                pg_ = newps()
                mmg(pg_, pg_[0:R, 0:2 * R].rearrange("p (a r) -> p a r", a=2),
                    [(nT[:, 0, 0:R], nT[:, :, 0:R])], [nT])
                stt(Am[0][0:R, 0:R], pg_[0:R, 0:R], beta_h, dlow[0:R, 0:R], OP.mult, OP.mult,
                    [pg_, sc["beta"], dlow], [Am[0]])
                ts(kd[0:R, :], pkv[0:R, 0:128], sc["ekd"][0:R, b, h:h + 1], OP.mult, [pkv, sc["ekd"]], [kd])
                tt(qkT[0:R, 0:R], pg_[0:R, R:2 * R], dup[0:R, 0:R], OP.mult, [pg_, dup], [qkT])
                ts(vb[0:R, :], pkv[0:R, 128:256], beta_h, OP.mult, [pkv, sc["beta"]], [vb])
                yield
                pt = newps()
                mmg(pt, pt[0:R, 0:R], [(Am[0][0:R, 0:R], identR)], [Am[0], cst])
                cp(At[0][0:R, 0:R], pt[0:R, 0:R], [pt], [At[0]], eng="act")
                tt(Pt32[0:R, 0:R], identR, pt[0:R, 0:R], OP.subtract, [cst, pt], [Pt32])
                yield
                cur = 0
                for i in range(1, nlev + 2):
                    nx = 1 - cur
                    p1 = p2 = p3 = None
                    if i <= nlev:
                        p1 = newps()
                        mmg(p1, p1[0:R, 0:R], [(At[cur][0:R, 0:R], Am[cur][0:R, 0:R])], [At[cur], Am[cur]])
                        if i < nlev:
                            p2 = newps()
                            mmg(p2, p2[0:R, 0:R], [(Am[cur][0:R, 0:R], At[cur][0:R, 0:R])], [At[cur], Am[cur]])
                    if i >= 2:
                        p3 = newps()
                        mmg(p3, p3[0:R, 0:R], [(Am[cur][0:R, 0:R], Pt32[0:R, 0:R])], [Am[cur], Pt32])
                    if p1 is not None:
                        cp(Am[nx][0:R, 0:R], p1[0:R, 0:R], [p1], [Am[nx]], eng="act")
                    if p2 is not None:
                        cp(At[nx][0:R, 0:R], p2[0:R, 0:R], [p2], [At[nx]], eng="act")
                    if p3 is not None:
                        tt(Pt32[0:R, 0:R], Pt32[0:R, 0:R], p3[0:R, 0:R], OP.add, [Pt32, p3], [Pt32])
                    yield
                    cur = nx
                cp(Ptb[0:R, 0:R], Pt32[0:R, 0:R], [Pt32], [Ptb], eng="act")


            def gen_prep(h, qkvT, TSp):
                gens = [prep_chain(b, TSp[b], h, qkvT) for b in range(NB)]
                while gens:
                    for g_ in list(gens):
                        try:
                            next(g_)
                        except StopIteration:
                            gens.remove(g_)
                    yield

            def gen_chain(*gs):
                for g_ in gs:
                    for _ in g_:
                        yield

            def gen_scan(h, zs, TS):
              if sample:
                load(S32s, S32s[:], sdl[l, :, h].rearrange("q k v -> k q v"))
                cp(Sbs[:], S32s[:], [S32s], [Sbs], eng="act")
              for b in range(NB):
                cs = slice(b * R, (b + 1) * R)
                T = TS[b]
                nT, qkT, Ptb, vb, kd = T["nT"], T["qkT"], T["Ptb"], T["vb"], T["kd"]
                pk_, pq_ = newps(), newps()
                if sample:
                    tt(knTm[:, :, :], nT[:, 0, 0:R].unsqueeze(1).broadcast_to([128, 16, R]), smT[:, :, :],
                       OP.mult, [nT, smT], [knTm])
                    tt(qnTm[:, :, :], nT[:, 1, 0:R].unsqueeze(1).broadcast_to([128, 16, R]), smT[:, :, :],
                       OP.mult, [nT, smT], [qnTm])
                    mmg(pk_, pk_[0:R, 0:128], [(knTm[:, q, :], Sbs[:, q, :]) for q in range(16)], [knTm, Sbs])
                    mmg(pq_, pq_[0:R, 0:128], [(qnTm[:, q, :], Sbs[:, q, :]) for q in range(16)], [qnTm, Sbs])
                else:
                    cp(Sb[:, :], S32[l][:, h, :], [S32[l]], [Sb], eng="act")
                    mmg(pk_, pk_[0:R, 0:128], [(nT[:, 0, 0:R], Sb[:, :])], [nT, Sb])
                    mmg(pq_, pq_[0:R, 0:128], [(nT[:, 1, 0:R], Sb[:, :])], [nT, Sb])
                stt(dd[0:R, :], pk_[0:R, 0:128], sc["nbgc"][0:R, b, h:h + 1], vb[0:R, :], OP.mult, OP.add,
                    [pk_, sc["nbgc"], vb], [dd])
                ts(to[0:R, :], pq_[0:R, 0:128], sc["egc"][0:R, b, h:h + 1], OP.mult, [pq_, sc["egc"]], [to])
                yield
                pv_ = newps()
                mmg(pv_, pv_[0:R, 0:128], [(Ptb[0:R, 0:R], dd[0:R, :])], [Ptb, dd])
                cp(vnew[0:R, :], pv_[0:R, 0:128], [pv_], [vnew], eng="act")
                yield
                po = newps()
                mmg(po, po[0:R, 0:128], [(qkT[0:R, 0:R], vnew[0:R, :])], [qkT, vnew])
                tt(oh[0:R, :], to[0:R, :], po[0:R, 0:128], OP.add, [to, po], [oh])
                if sample:
                    tt(kdm[:, :, :], kd[0:R, :].unsqueeze(1).broadcast_to([R, 16, 128]),
                       smk[0:R, :].unsqueeze(2).broadcast_to([R, 16, 128]), OP.mult, [kd, smk], [kdm])
                    for q in range(16):
                        pss = newps()
                        mmg(pss, pss[:, 0:128], [(kdm[:, q, :], vnew[0:R, :])], [kdm, vnew])
                        stt(S32s[:, q, :], S32s[:, q, :], eglb[:, b, q * NH + h: q * NH + h + 1], pss[:, 0:128],
                            OP.mult, OP.add, [S32s, eglb, pss], [S32s])
                    store(S32s, nd_s[l, :, h].rearrange("q k v -> k q v"), S32s[:])
                else:
                    pss = newps()
                    mmg(pss, pss[:, 0:128], [(kd[0:R, :], vnew[0:R, :])], [kd, vnew])
                    stt(S32[l][:, h, :], S32[l][:, h, :], eglb[:, b, h:h + 1], pss[:, 0:128], OP.mult, OP.add,
                        [S32[l], eglb, pss], [S32[l]])
                yield
                tt(to[0:R, :], oh[0:R, :], oh[0:R, :], OP.mult, [oh], [to])
                k.op("dve", lambda hh: hh.tensor_reduce(out=ss[0:R, :], in_=to[0:R, :], axis=AX.X, op=OP.add),
                     [to], [ss])
                ts(ss[0:R, :], ss[0:R, :], 1.0 / 128, OP.mult, [ss], [ss], s2=RMS_EPS, op1=OP.add)
                yield
                act(ss[0:R, :], ss[0:R, :], AF.Sqrt, [ss], [ss])
                k.op("dve", lambda hh: hh.reciprocal(out=ss[0:R, :], in_=ss[0:R, :]), [ss], [ss])
                tt(ngz[0:R, :], zs[0:R, b, :], ngb[0:R, :], OP.mult, [zs, ngb], [ngz])
                stt(og[0:R, :], oh[0:R, :], ss[0:R, 0:1], ngz[0:R, :], OP.mult, OP.mult, [oh, ss, ngz], [og])
                yield
                pob = newps()
                mmg(pob, pob[:, 0:R], [(og[0:R, :], identb[0:R, 0:R])], [og, identb])
                cp(mB[:, h, cs], pob[:, 0:R], [pob], [mB], eng="act")

            for _ in gen_chain(gen_proj(0, qkvT2[0], zs2[0]), gen_prep(0, qkvT2[0], TS2[0])):
                pass
            for h in range(NH):
                par = h % 2
                gl_ = [gen_scan(h, zs2[par], TS2[par])]
                if h + 1 < NH:
                    gl_.append(gen_chain(gen_proj(h + 1, qkvT2[1 - par], zs2[1 - par]),
                                         gen_prep(h + 1, qkvT2[1 - par], TS2[1 - par])))
                while gl_:
                    for g_ in list(gl_):
                        try:
                            next(g_)
                        except StopIteration:
                            gl_.remove(g_)
            if sample:
                store(hs_q, ncq_s[l], hs_q[:])

            dump("mB", mB, mB[:])
            stage("SB")
            for cg in range(8):
                c0 = cg * 256
                for br, (wout, mbuf) in enumerate(((w_a_out, mA), (w_b_out, mB), (w_c_out, mC))):
                    wo_, _, _ = wnext(wout[l, :, c0:c0 + 256])
                    wg_, _, _ = wnext(w_in[l, :, O_G + br * D + c0: O_G + br * D + c0 + 256], held=1)
                    for b in range(NB):
                        po_, pg2 = newps(), newps()
                        xst(po_, 0, wo_, 8, 256, mbuf, lambda kc, mbuf=mbuf: mbuf[:, kc, :], b)
                        xst(pg2, 0, wg_, KC, 256, xT, xTk, b)
                        act(sg[0:R, :], pg2[0:R, 0:256], AF.Sigmoid, [pg2], [sg])
                        if br == 0:
                            tt(mgt[0:R, b, :], sg[0:R, :], po_[0:R, 0:256], OP.mult, [sg, po_], [mgt])
                        else:
                            tt(sg[0:R, :], sg[0:R, :], po_[0:R, 0:256], OP.mult, [sg, po_], [sg])
                            if br == 1:
                                tt(mgt[0:R, b, :], mgt[0:R, b, :], sg[0:R, :], OP.add, [mgt, sg], [mgt])
                            else:
                                tt(mg[0:R, b, c0:c0 + 256], mgt[0:R, b, :], sg[0:R, :], OP.add, [mgt, sg], [mg])
            dump("mg", mg, mg[:])
            stage("S6")
            for b in range(NB):
                transpose_block(R, b, b * R, srcbuf=mg, src_of=lambda kc, b=b: mg[0:R, b, kc * 128:(kc + 1) * 128])
        k.barrier()

        for cg in range(8):
            c0 = cg * 256
            wo_, _, _ = wnext(w_o[l, :, c0:c0 + 256])
            for b in range(NB):
                py = newps()
                xst(py, 0, wo_, KC, 256, xT, xTk, b)
                stt(x32[0:R, b, c0:c0 + 256], x32[0:R, b, c0:c0 + 256], ALPHA, py[0:R, 0:256], OP.mult, OP.add,
                    [x32, py], [x32])
        phf = ExitStack()
        with phf:
            hT = k.sb([128, FCH, N], BF16, "hT", phf)
            sg2 = k.sb([128, WCOLS], F32, "sg2", phf)
            lnp_alloc(phf)
            load(lnp, lnp[:], ln1[l])

            def norm_and_transpose(last):
                for b in range(NB):
                    layernorm_block(R, b)
                    if last:
                        dst = y_s[:, :] if sample else y_p[tok0 + b * 128: tok0 + (b + 1) * 128, :]
                        store(x32, dst, x32[0:R, b, :])
                    else:
                        cp(xin16[0:R, :], x32[0:R, b, :], [x32], [xin16], eng="act")
                        transpose_block(R, b, b * R, srcbuf=xin16,
                                        src_of=lambda kc: xin16[0:R, kc * 128:(kc + 1) * 128])

            norm_and_transpose(False)
            dump("x1", x32, x32[:])
            stage("S7")

            for mp in range(22):
                nm = 2 if mp < 21 else 1
                wgt, _, _ = wnext(w_up[l, :, mp * 256: mp * 256 + nm * 128])
                wut, _, _ = wnext(w_up[l, :, DFF + mp * 256: DFF + mp * 256 + nm * 128], held=1)
                for mi in range(nm):
                    m = mp * 2 + mi
                    e = m % NCB
                    pgh, puh = newps(), newps()
                    wst(pgh, wgt, KC, mi * 128, xT, xTk, 0, N)
                    wst(puh, wut, KC, mi * 128, xT, xTk, 0, N)
                    a = conv_chunk(pgh, 3, cw_f, m, hist_f[l], hs_f, e)
                    act(tmpa[e % 2][:, 0:N], a[:, 0:N], AF.Silu, [a], [tmpa[e % 2]])
                    tt(hT[:, m, 0:N], tmpa[e % 2][:, 0:N], puh[:, 0:N], OP.mult, [tmpa[e % 2], puh], [hT])
            if sample:
                store(hs_f, ncf_s[l], hs_f[:])

            stage("S8")
            load(pT32, pT32[:, :, 0:N], (psT[l] if sample else ppT[l, :, tok0:tok0 + N]).rearrange(
                "(kc p) n -> p kc n", p=128))
            cp(pT[:, :, 0:N], pT32[:, :, 0:N], [pT32], [pT], eng="dve")
            load(lnp, lnp[:], ln2[l])
            for cg in range(8):
                c0 = cg * 256
                pf = [newps() for _ in range(NB)]
                kgs = ((0, 16), (16, 16), (32, 11))
                for gi, (k0, kn_) in enumerate(kgs):
                    wd_, _, _ = wnext(w_down[l, k0 * 128:(k0 + kn_) * 128, c0:c0 + 256])
                    for b in range(NB):
                        def fn(h, b=b, wd_=wd_, k0=k0, kn_=kn_, gi=gi):
                            ins = None
                            for kc in range(kn_):
                                ins = h.matmul(pf[b][0:R, 0:256], hT[:, k0 + kc, b * R:(b + 1) * R], wd_[:, kc, 0:256],
                                               start=(gi == 0 and kc == 0), stop=(gi == 2 and kc == kn_ - 1))
                            return ins
                        k.op("pe", fn, [hT, wd_], [pf[b]])
                wpg_, _, _ = wnext(w_pg[l, :, c0:c0 + 256])
                wpe_, _, _ = wnext(w_pe[l, :, c0:c0 + 256], held=1)
                for b in range(NB):
                    ppg, ppe = newps(), newps()
                    xst(ppg, 0, wpg_, KC, 256, xT, xTk, b)
                    xst(ppe, 0, wpe_, 2, 256, pT, lambda kc: pT[:, kc, :], b)
                    act(sg2[0:R, :], ppg[0:R, 0:256], AF.Sigmoid, [ppg], [sg2])
                    tt(sg2[0:R, :], sg2[0:R, :], ppe[0:R, 0:256], OP.mult, [sg2, ppe], [sg2])
                    tt(sg2[0:R, :], sg2[0:R, :], pf[b][0:R, 0:256], OP.add, [sg2, pf[b]], [sg2])
                    stt(x32[0:R, b, c0:c0 + 256], x32[0:R, b, c0:c0 + 256], ALPHA, sg2[0:R, :], OP.mult, OP.add,
                        [x32, sg2], [x32])
            dump("hT", hT, hT[:])
            norm_and_transpose(last_layer)
            dump("x2", x32, x32[:])
        k.barrier()

    def first_layer_input(tl):
        kind, R, NB, tok0 = tl["kind"], tl["R"], tl["NB"], tl["tok0"]
        with ExitStack() as st0:
            lnp_alloc(st0)
            load(lnp, lnp[:], lnin[:, :, :])
            for b in range(NB):
                src = xs[:, :] if kind == "s" else xp[tok0 + b * 128: tok0 + (b + 1) * 128, :]
                load(x32, x32[0:R, b, :], src)
                layernorm_block(R, b)
                cp(xin16[0:R, :], x32[0:R, b, :], [x32], [xin16], eng="act")
                transpose_block(R, b, b * R, srcbuf=xin16, src_of=lambda kc: xin16[0:R, kc * 128:(kc + 1) * 128])
            k.barrier()

    tiles = [dict(kind="p", R=128, NB=TB, nseq=1, Ls=128, tok0=t * TB * 128) for t in range(NPT)]
    tiles.append(dict(kind="s", R=64, NB=1, nseq=16, Ls=4, tok0=0))

    def emit_all():
        setup_consts()
        stage("consts")
        tl_list = tiles if dbg is None else [tiles[i] for i in dbg["tiles"]]
        layers = range(2) if dbg is None else dbg["layers"]
        for tl in tl_list:
            first_layer_input(tl)
            dump("xT0", xT, xT[:])
            dump("x0", x32, x32[:])
            stage("in")
            for l in layers:
                run_tile_layer(l, tl, False, l == 1)
        for l in range(2):
            store(hist_a[l], nca_p[l], hist_a[l][:])
            store(hist_q[l], ncq_p[l], hist_q[l][:])
            store(hist_f[l], ncf_p[l], hist_f[l][:])
            store(S32[l], nd_p[l].rearrange("h k v -> k h v"), S32[l][:])

    def emit_guard():
        try:
            emit_all()
        except _Stop:
            pass

    k.plan = True
    emit_guard()
    k.plan = False
    k.stopped = False
    psi[0] = 0
    emit_guard()
    k.stopped = False
    k.finish()


_PROG = {}


def _consts():
    def mk(R, Ls):
        idx = np.arange(R)
        seq = idx // Ls
        same = seq[:, None] == seq[None, :]
        c = np.zeros((R, 6, R), np.float32)
        c[:, 0] = np.eye(R)
        low = same & (idx[:, None] > idx[None, :])
        up = same & (idx[:, None] <= idx[None, :])
        c[:, 1] = np.where(low, 0.0, NEG)
        c[:, 2] = np.where(up, 0.0, NEG)
        c[:, 3] = up.astype(np.float32)
        c[:, 4] = up.astype(np.float32)
        c[:, 5] = same.astype(np.float32)
        return c
    smk = (np.arange(64)[:, None] // 4 == np.arange(16)[None, :]).astype(np.float32)
    smT = np.ascontiguousarray(np.broadcast_to(smk.T[None], (128, 16, 64))).astype(np.float32)
    return mk(128, 128), mk(64, 4), smk, smT


def kernel(x_prompt, x_sample, state_conv_a, state_conv_qkv, state_delta, state_conv_ffn,
           p_prompt, p_sample, ln_in_g, ln_in_b,
           w_in, conv_a_w, w_a_out, conv_b_w, a_log, dt_bias, norm_b_g, w_b_out,
           ln_c_g, ln_c_b, w_s, b_s, w_c_out, w_o, ln1_g, ln1_b,
           w_up, conv_f_w, w_down, w_pe, w_pg, ln2_g, ln2_b):
    f = lambda a: np.ascontiguousarray(np.asarray(a, dtype=np.float32))
    x_prompt, x_sample = f(x_prompt), f(x_sample)
    if "nc" not in _PROG:
        _PROG["nc"] = build_program()
    nc = _PROG["nc"]
    cst_p, cst_s, smk, smT = _consts()

    def rep(a):
        a = f(a)
        return np.ascontiguousarray(np.broadcast_to(a[..., None, :], a.shape[:-1] + (128, a.shape[-1])))

    def fm(a, nch):
        a = f(a)
        return np.ascontiguousarray(a.reshape(2, a.shape[1], nch, 128).transpose(0, 3, 2, 1))

    shared = {
        "w_in": f(w_in), "w_a_out": f(w_a_out), "w_b_out": f(w_b_out), "w_c_out": f(w_c_out), "w_o": f(w_o),
        "w_up": f(w_up), "w_down": f(w_down), "w_pe": f(w_pe), "w_pg": f(w_pg),
        "cwa": fm(conv_a_w, 8), "cwq": fm(conv_b_w, 24), "cwf": fm(conv_f_w, FCH),
        "lnin": np.ascontiguousarray(np.stack([rep(ln_in_g), rep(ln_in_b)], axis=1)),
        "ln1": np.ascontiguousarray(np.stack([rep(ln1_g), rep(ln1_b)], axis=2)),
        "ln2": np.ascontiguousarray(np.stack([rep(ln2_g), rep(ln2_b)], axis=2)),
        "lnc": np.ascontiguousarray(np.stack([rep(ln_c_g), rep(ln_c_b)], axis=2)),
        "nbg": rep(norm_b_g), "alog": rep(a_log), "dtb": rep(dt_bias),
        "cst_p": cst_p, "cst_s": cst_s, "smk_s": smk, "smT_s": smT,
    }
    ws = f(w_s)
    bs = f(b_s)
    shared["wsT_p"] = np.ascontiguousarray(ws.transpose(0, 3, 1, 2))
    w4 = ws[:, :, 0:4, 0:4].transpose(0, 3, 1, 2)
    shared["wsT_s"] = np.ascontiguousarray(np.tile(w4, (1, 16, 1, 16)))
    shared["bsb_p"] = np.ascontiguousarray(np.broadcast_to(bs[:, None], (2, 128, 8, 128)))
    shared["bsb_s"] = np.ascontiguousarray(np.broadcast_to(np.tile(bs[:, :, 0:4], (1, 1, 16))[:, None], (2, 128, 8, 64)))

    sca_f, scq_f, scf_f = f(state_conv_a), f(state_conv_qkv), f(state_conv_ffn)
    sdl_f, pp_f, psm_f = f(state_delta), f(p_prompt), f(p_sample)

    def fms(a, nch):
        return np.ascontiguousarray(a.reshape(2, 16, a.shape[2], nch, 128).transpose(0, 4, 3, 1, 2))

    in_maps = []
    for c in range(8):
        s = c % 4
        q0 = 16 * c
        m = dict(shared)
        m["xp"] = np.ascontiguousarray(x_prompt[s])
        m["xs"] = np.ascontiguousarray(x_sample[q0:q0 + 16].reshape(64, D))
        m["ppT"] = np.ascontiguousarray(pp_f[:, s].transpose(0, 2, 1))
        m["psT"] = np.ascontiguousarray(psm_f[:, q0:q0 + 16].reshape(2, 64, DPLE).transpose(0, 2, 1))
        m["sca"] = fms(sca_f[:, q0:q0 + 16], 8)
        m["scq"] = fms(scq_f[:, q0:q0 + 16], 24)
        m["scf"] = fms(scf_f[:, q0:q0 + 16], FCH)
        m["sdl"] = np.ascontiguousarray(sdl_f[:, q0:q0 + 16])
        in_maps.append(m)
    res = run_bass_kernel_spmd(nc, in_maps, core_ids=list(range(8)))
    R = res.results

    def unfm(a):
        return a.transpose(0, 3, 2, 1).reshape(2, a.shape[3], -1)

    def unfms(a):
        return a.transpose(0, 3, 4, 2, 1).reshape(2, 16, a.shape[4], -1)

    y_prompt = np.stack([R[s]["y_p"] for s in range(4)], 0)
    y_sample = np.concatenate([R[c]["y_s"].reshape(16, 4, D) for c in range(8)], 0)
    nca_p = np.stack([unfm(R[s]["nca_p"]) for s in range(4)], 1)
    ncq_p = np.stack([unfm(R[s]["ncq_p"]) for s in range(4)], 1)
    ncf_p = np.stack([unfm(R[s]["ncf_p"]) for s in range(4)], 1)
    nd_p = np.stack([R[s]["nd_p"] for s in range(4)], 1)
    nca_s = np.concatenate([unfms(R[c]["nca_s"]) for c in range(8)], 1)
    ncq_s = np.concatenate([unfms(R[c]["ncq_s"]) for c in range(8)], 1)
    ncf_s = np.concatenate([unfms(R[c]["ncf_s"]) for c in range(8)], 1)
    nd_s = np.concatenate([R[c]["nd_s"] for c in range(8)], 1)
    nv_s = np.concatenate([R[c]["nv_s"].reshape(2, 16, 4, 1024) for c in range(8)], 1)
    outs = (y_prompt, y_sample, nca_p, ncq_p, nd_p, ncf_p, nca_s, ncq_s, nd_s, ncf_s, nv_s)
    return tuple(np.ascontiguousarray(o, dtype=np.float32) for o in outs)
```
